# Optimizing a Trainium2 kernel written in Bass

```python
import math
import jax, jax.numpy as jnp
from jax import lax
import numpy as np

D_MODEL = 1024
BATCH = 2
SEQ = 8192
DEPTH = 4

CHUNK = 64
GDN_HEADS = 4
GDN_DK = 128
GDN_DV = 128
CONV_W = 4
GLA_HEADS = 4
GLA_DK = 64
GLA_DV = 128
GLA_RANK = 16
GLA_NORMALIZER = 16.0
HGRN_HEADS = 4
HGRN_EXPAND = 128
HGRN_DV = 128
LB_FLOOR = 1e-30
N_EXPERTS = 16
N_GROUPS = 4
EXPERTS_PER_GROUP = N_EXPERTS // N_GROUPS
TOP_K = 2
D_FF = 256
ALPHA = (2.0 * DEPTH) ** 0.25
BETA = (8.0 * DEPTH) ** -0.25
LN_EPS = 1e-5
RMS_EPS = 1e-6

GDN_QK = GDN_HEADS * GDN_DK
GDN_V = GDN_HEADS * GDN_DV
GLA_QK = GLA_HEADS * GLA_DK
GLA_V = GLA_HEADS * GLA_DV
HGRN_QK = HGRN_HEADS * HGRN_EXPAND
HGRN_V = HGRN_HEADS * HGRN_DV
SPLIT_SIZES = (GDN_QK, GDN_QK, GDN_V, GDN_HEADS, GDN_HEADS, GDN_V,
               GLA_QK, GLA_QK, GLA_V, GLA_RANK, GLA_V,
               HGRN_QK, HGRN_QK, HGRN_V, HGRN_V,
               D_MODEL, D_MODEL, D_MODEL)
SPLIT_POINTS = tuple(int(v) for v in np.cumsum(SPLIT_SIZES)[:-1])
IN_WIDTH = int(sum(SPLIT_SIZES))

kernel_name = "hybrid_gdn_gla_hgrn2_grouped_moe_deepnorm"


def layer_norm(x, g, b):
    xf = x.astype(jnp.float32)
    mu = xf.mean(-1, keepdims=True)
    var = jnp.square(xf - mu).mean(-1, keepdims=True)
    return ((xf - mu) * lax.rsqrt(var + LN_EPS) * g.astype(jnp.float32) + b.astype(jnp.float32)).astype(x.dtype)


def l2_normalize(x):
    return x * lax.rsqrt(jnp.sum(jnp.square(x), -1, keepdims=True) + RMS_EPS)


def to_chunks(x, n_heads):
    b, s, w = x.shape
    return x.reshape(b, s // CHUNK, CHUNK, n_heads, w // n_heads).transpose(0, 3, 1, 2, 4)


def scalar_to_chunks(x):
    b, s, h = x.shape
    return x.reshape(b, s // CHUNK, CHUNK, h).transpose(0, 3, 1, 2)


def from_chunks(o):
    b, h, n, c, d = o.shape
    return o.transpose(0, 2, 3, 1, 4).reshape(b, n * c, h * d)


def masked_exp(diff, mask):
    return jnp.where(mask, jnp.exp(jnp.where(mask, diff, 0.0)), 0.0)


def gated_rms_heads(o, gate, g, n_heads):
    b, s, w = o.shape
    oh = o.astype(jnp.float32).reshape(b, s, n_heads, w // n_heads)
    oh = oh * lax.rsqrt(jnp.mean(jnp.square(oh), -1, keepdims=True) + RMS_EPS) * g.astype(jnp.float32)
    gh = jax.nn.silu(gate.astype(jnp.float32)).reshape(b, s, n_heads, w // n_heads)
    return (oh * gh).reshape(b, s, w).astype(gate.dtype)


def causal_short_conv(x, w):
    s = x.shape[1]
    xp = jnp.pad(x, ((0, 0), (CONV_W - 1, 0), (0, 0)))
    y = xp[:, 0:s] * w[0]
    for k in range(1, CONV_W):
        y = y + xp[:, k:k + s] * w[k]
    return y


def gated_delta_rule(q, k, v, beta, log_decay):
    c = q.shape[-2]
    incl = jnp.tril(jnp.ones((c, c), bool))
    strict = jnp.tril(jnp.ones((c, c), bool), -1)
    cum = jnp.cumsum(log_decay, axis=-1)
    diff = cum[..., :, None] - cum[..., None, :]
    decay_mat = masked_exp(diff, incl)
    kk = jnp.einsum('bhnid,bhnjd->bhnij', k, k)
    a_strict = jnp.where(strict, beta[..., :, None] * kk * decay_mat, 0.0)
    t_mat = a_strict + jnp.eye(c, dtype=q.dtype)
    rhs = jnp.concatenate([v * beta[..., None], k * (beta * jnp.exp(cum))[..., None]], -1)
    sol = lax.linalg.triangular_solve(t_mat, rhs, left_side=True, lower=True, unit_diagonal=True)
    u, w = sol[..., :v.shape[-1]], sol[..., v.shape[-1]:]
    attn = jnp.einsum('bhnid,bhnjd->bhnij', q, k) * decay_mat
    q_inter = q * jnp.exp(cum)[..., None]
    k_state = k * jnp.exp(cum[..., -1:] - cum)[..., None]
    last = jnp.exp(cum[..., -1])

    def step(s, xs):
        qi_c, u_c, w_c, attn_c, ks_c, last_c = xs
        v_new = u_c - jnp.einsum('bhcd,bhde->bhce', w_c, s)
        o = jnp.einsum('bhcd,bhde->bhce', qi_c, s) + jnp.einsum('bhij,bhje->bhie', attn_c, v_new)
        s = s * last_c[..., None, None] + jnp.einsum('bhcd,bhce->bhde', ks_c, v_new)
        return s, o

    b, h = q.shape[0], q.shape[1]
    s0 = jnp.zeros((b, h, q.shape[-1], v.shape[-1]), q.dtype)
    xs = tuple(jnp.moveaxis(a, 2, 0) for a in (q_inter, u, w, attn, k_state, last))
    _, o = lax.scan(step, s0, xs)
    return jnp.moveaxis(o, 0, 2)


def diag_decay_linear_attention(q, k, v, log_a):
    c = q.shape[-2]
    incl = jnp.tril(jnp.ones((c, c), bool))[:, :, None]
    cum = jnp.cumsum(log_a, axis=-2)
    last = cum[..., -1:, :]
    q_inter = q * jnp.exp(cum)
    k_state = k * jnp.exp(last - cum)
    decay_last = jnp.exp(last[..., 0, :])

    def step(s, xs):
        q_c, k_c, v_c, cum_c, qi_c, ks_c, dl_c = xs
        diff = cum_c[..., :, None, :] - cum_c[..., None, :, :]
        rel = masked_exp(diff, incl)
        attn = jnp.einsum('bhid,bhjd,bhijd->bhij', q_c, k_c, rel)
        o = jnp.einsum('bhij,bhje->bhie', attn, v_c) + jnp.einsum('bhid,bhde->bhie', qi_c, s)
        s = s * dl_c[..., :, None] + jnp.einsum('bhjd,bhje->bhde', ks_c, v_c)
        return s, o

    b, h = q.shape[0], q.shape[1]
    s0 = jnp.zeros((b, h, q.shape[-1], v.shape[-1]), q.dtype)
    xs = tuple(jnp.moveaxis(a, 2, 0) for a in (q, k, v, cum, q_inter, k_state, decay_last))
    _, o = lax.scan(step, s0, xs)
    return jnp.moveaxis(o, 0, 2)


def hybrid_mixer(x, lower_bound, w_in, gdn_conv, gdn_a_log, gdn_dt_bias, gdn_norm, gla_w2, gla_b2,
                 gla_norm, hgrn_norm, w_br_a, w_br_b, w_br_c, w_out):
    f32 = jnp.float32
    proj = x @ w_in
    (a_q, a_k, a_v, a_beta, a_dt, a_g, b_q, b_k, b_v, b_lr, b_g,
     c_q, c_f, c_i, c_g, m_a, m_b, m_c) = jnp.split(proj, SPLIT_POINTS, axis=-1)

    qkv = jax.nn.silu(causal_short_conv(jnp.concatenate([a_q, a_k, a_v], -1), gdn_conv)).astype(f32)
    aq, ak, av = jnp.split(qkv, (GDN_QK, 2 * GDN_QK), axis=-1)
    aq = l2_normalize(to_chunks(aq, GDN_HEADS)) * (GDN_DK ** -0.5)
    ak = l2_normalize(to_chunks(ak, GDN_HEADS))
    av = to_chunks(av, GDN_HEADS)
    beta = scalar_to_chunks(jax.nn.sigmoid(a_beta.astype(f32)))
    g_a = -jnp.exp(gdn_a_log.astype(f32)) * jax.nn.softplus(a_dt.astype(f32) + gdn_dt_bias.astype(f32))
    o_a = from_chunks(gated_delta_rule(aq, ak, av, beta, scalar_to_chunks(g_a)))
    o_a = gated_rms_heads(o_a, a_g, gdn_norm, GDN_HEADS)

    bq = to_chunks(b_q.astype(f32), GLA_HEADS) * (GLA_DK ** -0.5)
    bk = to_chunks(b_k.astype(f32), GLA_HEADS)
    bv = to_chunks(b_v.astype(f32), GLA_HEADS)
    log_alpha = jax.nn.log_sigmoid((b_lr @ gla_w2 + gla_b2).astype(f32)) / GLA_NORMALIZER
    o_b = from_chunks(diag_decay_linear_attention(bq, bk, bv, to_chunks(log_alpha, GLA_HEADS)))
    o_b = gated_rms_heads(o_b, b_g, gla_norm, GLA_HEADS)

    cf = c_f.astype(f32)
    lb = lower_bound.astype(f32)
    log_lb = jnp.log(jnp.maximum(lb, LB_FLOOR))
    log_f = jnp.logaddexp(log_lb, jnp.log1p(-lb) + jax.nn.log_sigmoid(cf))
    k_c = (1.0 - lb) * jax.nn.sigmoid(-cf)
    cq = to_chunks(jax.nn.silu(c_q.astype(f32)), HGRN_HEADS) * (HGRN_EXPAND ** -0.5)
    o_c = from_chunks(diag_decay_linear_attention(cq, to_chunks(k_c, HGRN_HEADS),
                                                  to_chunks(c_i.astype(f32), HGRN_HEADS),
                                                  to_chunks(log_f, HGRN_HEADS)))
    o_c = gated_rms_heads(o_c, c_g, hgrn_norm, HGRN_HEADS)

    y = (jax.nn.sigmoid(m_a) * (o_a @ w_br_a) + jax.nn.sigmoid(m_b) * (o_b @ w_br_b)
         + jax.nn.sigmoid(m_c) * (o_c @ w_br_c))
    return y @ w_out


def grouped_moe(x, w_router, router_bias, w_gate, w_up, w_down):
    b, s, d = x.shape
    t = x.reshape(b * s, d)
    scores = jax.nn.sigmoid((t @ w_router).astype(jnp.float32))
    sel = (scores + router_bias.astype(jnp.float32)).reshape(-1, N_GROUPS, EXPERTS_PER_GROUP)
    group_score = lax.top_k(sel, TOP_K)[0].sum(-1)
    g_idx = jnp.argmax(group_score, axis=-1)
    in_group = jnp.take_along_axis(sel, g_idx[:, None, None], axis=1)[:, 0]
    _, e_local = lax.top_k(in_group, TOP_K)
    e_idx = g_idx[:, None] * EXPERTS_PER_GROUP + e_local
    w_sel = jnp.take_along_axis(scores, e_idx, axis=1)
    w_sel = w_sel / jnp.sum(w_sel, -1, keepdims=True)
    combine = jnp.sum(jax.nn.one_hot(e_idx, N_EXPERTS, dtype=jnp.float32) * w_sel[..., None], 1)
    h = jax.nn.silu(jnp.einsum('td,edf->tef', t, w_gate)) * jnp.einsum('td,edf->tef', t, w_up)
    y = jnp.einsum('tef,efd->td', h * combine[..., None].astype(h.dtype), w_down)
    return y.reshape(b, s, d)


def setup_inputs(seed: int = 0) -> dict:
    key = jax.random.key(seed)
    ks = jax.random.split(key, 26)
    f32 = jnp.float32

    def nrm(k, shape, scale):
        return jax.random.normal(k, shape, f32) * scale

    dt = jnp.exp(jax.random.uniform(ks[6], (DEPTH, GDN_HEADS), f32, math.log(1e-3), math.log(1e-1)))
    return {
        "x": nrm(ks[0], (BATCH, SEQ, D_MODEL), 1.0),
        "ln0_g": 1.0 + nrm(ks[1], (D_MODEL,), 0.02),
        "ln0_b": nrm(ks[2], (D_MODEL,), 0.02),
        "w_in": nrm(ks[3], (DEPTH, D_MODEL, IN_WIDTH), D_MODEL ** -0.5),
        "gdn_conv": nrm(ks[4], (DEPTH, CONV_W, 2 * GDN_QK + GDN_V), CONV_W ** -0.5),
        "gdn_a_log": jnp.log(jax.random.uniform(ks[5], (DEPTH, GDN_HEADS), f32, 1.0, 16.0)),
        "gdn_dt_bias": dt + jnp.log(-jnp.expm1(-dt)),
        "gdn_norm": 1.0 + nrm(ks[7], (DEPTH, GDN_DV), 0.02),
        "gla_w2": nrm(ks[8], (DEPTH, GLA_RANK, GLA_QK), GLA_RANK ** -0.5),
        "gla_b2": nrm(ks[9], (DEPTH, GLA_QK), 0.1),
        "gla_norm": 1.0 + nrm(ks[10], (DEPTH, GLA_DV), 0.02),
        "hgrn_lb_logits": nrm(ks[11], (DEPTH, HGRN_QK), 0.5),
        "hgrn_norm": 1.0 + nrm(ks[12], (DEPTH, HGRN_DV), 0.02),
        "w_br_a": nrm(ks[13], (DEPTH, GDN_V, D_MODEL), BETA * GDN_V ** -0.5),
        "w_br_b": nrm(ks[14], (DEPTH, GLA_V, D_MODEL), BETA * GLA_V ** -0.5),
        "w_br_c": nrm(ks[15], (DEPTH, HGRN_V, D_MODEL), BETA * HGRN_V ** -0.5),
        "w_out": nrm(ks[16], (DEPTH, D_MODEL, D_MODEL), BETA * D_MODEL ** -0.5),
        "ln1_g": 1.0 + nrm(ks[17], (DEPTH, D_MODEL), 0.02),
        "ln1_b": nrm(ks[18], (DEPTH, D_MODEL), 0.02),
        "w_router": nrm(ks[19], (D_MODEL, N_EXPERTS), D_MODEL ** -0.5),
        "router_bias": nrm(ks[20], (N_EXPERTS,), 0.01),
        "w_gate": nrm(ks[21], (DEPTH, N_EXPERTS, D_MODEL, D_FF), BETA * D_MODEL ** -0.5),
        "w_up": nrm(ks[22], (DEPTH, N_EXPERTS, D_MODEL, D_FF), BETA * D_MODEL ** -0.5),
        "w_down": nrm(ks[23], (DEPTH, N_EXPERTS, D_FF, D_MODEL), BETA * D_FF ** -0.5),
        "ln2_g": 1.0 + nrm(ks[24], (DEPTH, D_MODEL), 0.02),
        "ln2_b": nrm(ks[25], (DEPTH, D_MODEL), 0.02),
    }


def reference(x, ln0_g, ln0_b, w_in, gdn_conv, gdn_a_log, gdn_dt_bias, gdn_norm, gla_w2, gla_b2,
              gla_norm, hgrn_lb_logits, hgrn_norm, w_br_a, w_br_b, w_br_c, w_out, ln1_g, ln1_b,
              w_router, router_bias, w_gate, w_up, w_down, ln2_g, ln2_b):
    p = jax.nn.softmax(hgrn_lb_logits.astype(jnp.float32), axis=0)
    lower_bounds = jnp.clip(jnp.cumsum(p, axis=0) - p[0], 0.0, 1.0)
    h = layer_norm(x, ln0_g, ln0_b)
    for l in range(DEPTH):
        mix = hybrid_mixer(h, lower_bounds[l], w_in[l], gdn_conv[l], gdn_a_log[l], gdn_dt_bias[l],
                           gdn_norm[l], gla_w2[l], gla_b2[l], gla_norm[l], hgrn_norm[l],
                           w_br_a[l], w_br_b[l], w_br_c[l], w_out[l])
        h = layer_norm(ALPHA * h + mix, ln1_g[l], ln1_b[l])
        ffn = grouped_moe(h, w_router, router_bias, w_gate[l], w_up[l], w_down[l])
        h = layer_norm(ALPHA * h + ffn, ln2_g[l], ln2_b[l])
    return h
```

```python
import math
import os
import numpy as np
from contextlib import ExitStack
import concourse.bass as bass
import concourse.mybir as mybir
from concourse.bass_utils import run_bass_kernel_spmd

F32 = mybir.dt.float32
BF16 = mybir.dt.bfloat16
AF = mybir.ActivationFunctionType
ALU = mybir.AluOpType
AX = mybir.AxisListType

D = 1024
KC = 8
SEQ = 8192
NBATCH = 2
DEPTH = 4
ALPHA = (2.0 * DEPTH) ** 0.25
LN_EPS = 1e-5
RMS_EPS = 1e-6
TB = 512
CA_Q, CA_K, CA_V, CA_BR, CA_DR, CA_G, CA_BD = 0, 128, 256, 384, 512, 640, 768
CB0 = 770
CB_Q, CB_K, CB_LR, CB_V, CB_G = CB0, CB0 + 64, CB0 + 128, CB0 + 144, CB0 + 272
CC0 = CB0 + 400
CC_Q, CC_F, CC_I, CC_G = CC0, CC0 + 128, CC0 + 256, CC0 + 384
NCOL = CC0 + 512


class Buf:
    __slots__ = ("name", "w", "r")
    registry = []

    def __init__(self, name):
        self.name = name
        self.w = None
        self.r = []
        Buf.registry.append(self)


class V:
    __slots__ = ("ap", "bufs")

    def __init__(self, ap, bufs):
        self.ap = ap
        self.bufs = bufs

    def __getitem__(self, idx):
        return V(self.ap[idx], self.bufs)

    def re(self, s, **kw):
        return V(self.ap.rearrange(s, **kw), self.bufs)

    def bc(self, shape):
        return V(self.ap.to_broadcast(list(shape)), self.bufs)


class Prog:
    CENG = ("pe", "dve", "act", "pool")

    def __init__(self, nc, es, arena_words=49152):
        self.nc = nc
        self.es = es
        self.q = {e: [] for e in ("pe", "dve", "act", "pool", "sp")}
        self.cnt = {e: 0 for e in self.CENG}
        self.known = {e: {} for e in self.q}
        self.sems = {}
        self.dcnt = {}
        self.epoch = 0
        self.key = {}
        self.waited = {}
        Buf.registry = []
        for e in self.CENG:
            self.key[e] = (e, 0)
            self.sems[self.key[e]] = es.enter_context(nc.semaphore("s_%s0" % e))
        self.n_inst = 0
        self.n_wait = 0
        arena_words = int(nc.sbuf_bytes_remaining) // 4 - 512
        self.arena = es.enter_context(nc.sbuf_tensor("arena", [128, arena_words], F32))
        self.aoff = 0
        self.psn = 0

    def mark(self):
        return self.aoff

    def release(self, m):
        self.aoff = m

    def sb(self, name, shape, dt=F32, nb=1):
        n = 1
        for v in shape[1:]:
            n *= v
        words = n if dt == F32 else (n + 1) // 2
        words = (words + 7) // 8 * 8
        assert self.aoff + words <= self.arena.shape[1], ("SBUF arena overflow", name, self.aoff, words)
        ap = self.arena[0:shape[0], self.aoff:self.aoff + words]
        self.aoff += words
        if dt != F32:
            ap = ap.bitcast(dt)
        ap = ap[:, 0:n]
        if len(shape) == 3:
            ap = ap.rearrange("p (a b) -> p a b", b=shape[2])
        elif len(shape) == 4:
            ap = ap.rearrange("p (a b c) -> p a b c", b=shape[2], c=shape[3])
        return V(ap, [Buf(name)])

    def ps(self, name, shape, dt=F32):
        t = self.es.enter_context(self.nc.psum_tensor("ps_" + name, list(shape), dt))
        return V(t[:], [Buf(name)])

    def barrier(self):
        latest = []
        for e in self.CENG:
            if self.cnt[e] > 0:
                latest.append((self.key[e], self.cnt[e]))
        for kk, v in self.dcnt.items():
            if v > 0:
                latest.append((kk, v))
        for eng in self.q:
            kn = self.known[eng]
            for (kk, v) in latest:
                if kn.get(kk, 0) < v:
                    kn[kk] = v
                    self.q[eng].append(("w", self.sems[kk], v))
                    self.n_wait += 1
        self.epoch += 1
        for e in self.CENG:
            self.key[e] = (e, self.epoch)
            self.sems[self.key[e]] = self.es.enter_context(self.nc.semaphore("s_%s%d" % (e, self.epoch)))
            self.cnt[e] = 0
        for b in Buf.registry:
            b.w = None
            b.r = []

    def _deps(self, eng, reads, writes):
        deps = []
        for b in reads:
            if b.w is not None:
                deps.append(b.w)
        for b in writes:
            if b.w is not None:
                deps.append(b.w)
            deps.extend(b.r)
        kn = self.known[eng]
        need = {}
        for (k, v) in deps:
            if k[0] == "pe" and eng == "pe":
                continue
            if kn.get(k, 0) >= v:
                continue
            if need.get(k, 0) < v:
                need[k] = v
        for k, v in need.items():
            kn[k] = v
            self.q[eng].append(("w", self.sems[k], v))
            self.n_wait += 1
            if k[0] == "d" and self.waited.get(k, 0) < v:
                self.waited[k] = v

    def _observe(self, eng, k):
        v = self.waited.get(k, 0)
        if v > 0 and self.known[eng].get(k, 0) < v:
            self.known[eng][k] = v
            self.q[eng].append(("w", self.sems[k], v))
            self.n_wait += 1

    def _record(self, ev, reads, writes):
        for b in reads:
            if len(b.r) > 24:
                b.r = b.r[-24:]
            b.r.append(ev)
        for b in writes:
            b.w = ev
            b.r = []

    def op(self, eng, fn, reads=(), writes=()):
        self._deps(eng, reads, writes)
        self.cnt[eng] += 1
        ev = (self.key[eng], self.cnt[eng])
        self.q[eng].append(("i", fn, self.sems[self.key[eng]], 1))
        self._record(ev, reads, writes)
        self.n_inst += 1
        return ev

    def dma(self, eng, key, out, in_, reads=(), writes=()):
        k = ("d", key)
        if k not in self.sems:
            self.sems[k] = self.es.enter_context(self.nc.semaphore("d_" + str(key)))
            self.dcnt[k] = 0
        self._deps(eng, reads, writes)
        self._observe(eng, k)
        self.dcnt[k] += 16
        ev = (k, self.dcnt[k])
        self.q[eng].append(("i", lambda e: e.dma_start(out=out, in_=in_), self.sems[k], 16))
        self._record(ev, reads, writes)
        self.n_inst += 1
        return ev

    def coll(self, key, fn, reads=(), writes=()):
        k = ("d", key)
        if k not in self.sems:
            self.sems[k] = self.es.enter_context(self.nc.semaphore("c_" + str(key)))
            self.dcnt[k] = 0
        self._deps("pool", reads, writes)
        self._observe("pool", k)
        self.dcnt[k] += 1
        ev = (k, self.dcnt[k])
        self.q["pool"].append(("i", fn, self.sems[k], 1))
        self._record(ev, reads, writes)
        self.n_inst += 1
        return ev

    def wait_all(self, eng, bufs):
        self._deps(eng, bufs, ())

    def run(self):
        q = self.q

        def replay(e, lst):
            for it in lst:
                if it[0] == "w":
                    e.wait_ge(it[1], it[2])
                else:
                    it[1](e).then_inc(it[2], it[3])

        with self.nc.Block() as block:
            @block.sync
            def _(e):
                replay(e, q["sp"])

            @block.tensor
            def _(e):
                replay(e, q["pe"])

            @block.vector
            def _(e):
                replay(e, q["dve"])

            @block.scalar
            def _(e):
                replay(e, q["act"])

            @block.gpsimd
            def _(e):
                replay(e, q["pool"])


class K:
    def __init__(self, P):
        self.P = P

    @staticmethod
    def _b(*xs):
        out = []
        for x in xs:
            if isinstance(x, V):
                out.extend(x.bufs)
        return out

    @staticmethod
    def _a(x):
        return x.ap if isinstance(x, V) else x

    def mm(self, out, lhsT, rhs, start=True, stop=True):
        self.P.op("pe", lambda e: e.matmul(out.ap, lhsT=lhsT.ap, rhs=rhs.ap, start=start, stop=stop),
                  reads=lhsT.bufs + rhs.bufs, writes=out.bufs)

    def tr(self, out, in_, ident):
        self.P.op("pe", lambda e: e.transpose(out.ap, in_.ap, ident.ap),
                  reads=in_.bufs + ident.bufs, writes=out.bufs)

    def act(self, out, in_, func, bias=0.0, scale=1.0, accum=None):
        a = self._a
        if accum is None:
            fn = lambda e: e.activation(out=out.ap, in_=in_.ap, func=func, bias=a(bias), scale=a(scale))
            wr = out.bufs
        else:
            fn = lambda e: e.activation(out=out.ap, in_=in_.ap, func=func, bias=a(bias), scale=a(scale),
                                        accum_out=accum.ap)
            wr = out.bufs + accum.bufs
        self.P.op("act", fn, reads=self._b(in_, bias, scale), writes=wr)

    def cp(self, eng, out, in_):
        if eng == "act":
            self.P.op("act", lambda e: e.copy(out=out.ap, in_=in_.ap), reads=in_.bufs, writes=out.bufs)
        else:
            self.P.op(eng, lambda e: e.tensor_copy(out=out.ap, in_=in_.ap), reads=in_.bufs, writes=out.bufs)

    def tt(self, eng, out, in0, in1, op):
        self.P.op(eng, lambda e: e.tensor_tensor(out=out.ap, in0=in0.ap, in1=in1.ap, op=op),
                  reads=in0.bufs + in1.bufs, writes=out.bufs)

    def ts(self, eng, out, in0, s1, op0, s2=None, op1=None):
        a = self._a
        if op1 is None:
            fn = lambda e: e.tensor_scalar(out=out.ap, in0=in0.ap, scalar1=a(s1), scalar2=None, op0=op0)
        else:
            fn = lambda e: e.tensor_scalar(out=out.ap, in0=in0.ap, scalar1=a(s1), scalar2=a(s2), op0=op0, op1=op1)
        self.P.op(eng, fn, reads=self._b(in0, s1, s2), writes=out.bufs)

    def stt(self, out, in0, scalar, in1, op0, op1):
        a = self._a
        self.P.op("dve", lambda e: e.scalar_tensor_tensor(out=out.ap, in0=in0.ap, scalar=a(scalar), in1=in1.ap,
                                                          op0=op0, op1=op1),
                  reads=self._b(in0, scalar, in1), writes=out.bufs)

    def scan(self, out, d0, d1):
        self.P.op("dve", lambda e: e.tensor_tensor_scan(out=out.ap, data0=d0.ap, data1=d1.ap, initial=0.0,
                                                        op0=ALU.mult, op1=ALU.add),
                  reads=d0.bufs + d1.bufs, writes=out.bufs)

    def recip(self, out, in_):
        self.P.op("dve", lambda e: e.reciprocal(out=out.ap, in_=in_.ap), reads=in_.bufs, writes=out.bufs)

    def memset(self, eng, out, val):
        self.P.op(eng, lambda e: e.memset(out.ap, val), writes=out.bufs)

    def asel(self, out, pattern, cmp, cm, base=0):
        self.P.op("pool", lambda e: e.affine_select(out=out.ap, in_=out.ap, pattern=pattern, compare_op=cmp,
                                                    fill=0.0, base=base, channel_multiplier=cm),
                  reads=out.bufs, writes=out.bufs)

    def dma(self, eng, key, out, in_):
        rd = in_.bufs if isinstance(in_, V) else []
        wr = out.bufs if isinstance(out, V) else []
        self.P.dma(eng, key, self._a(out), self._a(in_), reads=rd, writes=wr)


def make_consts(P, k):
    c = {}
    c["ident"] = P.sb("ident", [128, 128])
    k.memset("pool", c["ident"], 1.0)
    k.asel(c["ident"], [[-1, 128]], ALU.is_equal, 1)
    c["identb"] = P.sb("identb", [128, 128], BF16)
    k.cp("pool", c["identb"], c["ident"])
    c["onesb"] = P.sb("onesb", [128, 128], BF16)
    k.memset("pool", c["onesb"], 1.0)
    c["onesf"] = P.sb("onesf", [128, 128])
    k.memset("pool", c["onesf"], 1.0)
    for name, cmp in (("mTi", ALU.is_ge), ("mTs", ALU.is_gt)):
        m = P.sb(name, [128, 4, 128])
        k.memset("pool", m, 0.0)
        k.memset("pool", m[0:64, :, 0:64], 1.0)
        k.memset("pool", m[64:128, :, 64:128], 1.0)
        k.asel(m, [[0, 4], [1, 128]], cmp, -1)
        c[name] = m
    m = P.sb("LTs", [128, 128])
    k.memset("pool", m, 0.0)
    k.memset("pool", m[0:64, 0:64], 1.0)
    k.memset("pool", m[64:128, 64:128], 1.0)
    k.asel(m, [[-1, 128]], ALU.is_gt, 1)
    c["LTs"] = m
    cm = P.sb("cmask", [128, TB])
    k.memset("pool", cm, 1.0)
    k.memset("pool", cm.re("p (c t) -> p c t", t=64)[:, :, 0:1], 0.0)
    c["cmask"] = cm
    ec = P.sb("epsc", [128, 4])
    k.memset("pool", ec[:, 0:1], RMS_EPS)
    k.memset("pool", ec[:, 1:2], 1.0)
    k.memset("pool", ec[:, 2:3], LN_EPS)
    k.memset("pool", ec[:, 3:4], 0.0)
    c["eps"] = ec
    bm = P.sb("bm", [128, 2])
    k.memset("pool", bm, 0.0)
    k.memset("pool", bm[0:64, 0:1], 1.0)
    k.memset("pool", bm[64:128, 1:2], 1.0)
    c["bm"] = bm
    return c


def load_weights_bf16(P, k, wdram, wbf, ncol, tag, step=256):
    wv = wdram.rearrange("(kc p) n -> p kc n", p=128)
    for c0 in range(0, ncol, step):
        c1 = min(ncol, c0 + step)
        k.dma("pool", tag, wbf[:, :, c0:c1], wv[:, :, c0:c1])


def build_M(nb=16):
    NT = nb * TB
    nc = bass.Bass("TRN2", target_bir_lowering=False)
    hT_d = nc.dram_tensor("hT", [D, NT], F32, kind="ExternalInput").ap()
    w_d = nc.dram_tensor("w", [D, NCOL], F32, kind="ExternalInput").ap()
    sm_d = nc.dram_tensor("sm", [128, 32], F32, kind="ExternalInput").ap()
    nrm_d = nc.dram_tensor("nrm", [128, 3, 128], F32, kind="ExternalInput").ap()
    w2_d = nc.dram_tensor("w2", [16, 64], F32, kind="ExternalInput").ap()
    o_d = nc.dram_tensor("oT", [3, 128, NT], F32, kind="ExternalOutput").ap()
    with ExitStack() as es:
        P = Prog(nc, es)
        k = K(P)
        banks = [P.ps("bk%d" % i, [128, 512]) for i in range(8)]
        c = make_consts(P, k)
        hv = hT_d.rearrange("(kc p) t -> p kc t", p=128)
        fin = emit_M(P, k, c, banks, nb, lambda bi, hd: k.dma("pool", "h%d" % (bi % 2), hd, hv[:, :, bi * TB:(bi + 1) * TB]),
                     lambda bi: o_d[:, :, bi * TB:(bi + 1) * TB].rearrange("m p t -> p m t"),
                     w_d, sm_d, nrm_d, w2_d)
        P.wait_all("sp", fin)
        P.run()
        print("M program: inst", P.n_inst, "waits", P.n_wait, {e: len(v) for e, v in P.q.items()})
    return nc


class _Stop(Exception):
    pass


def emit_M(P, k, c, banks, nb, h_load, o_dst, w_d, sm_d, nrm_d, w2_d, obufs=None, after_batch=None):
    MSTOP = float(os.environ.get("MSTOP", "99"))

    def stage(n):
        if n > MSTOP:
            raise _Stop()
    ident, identb, onesb, onesf = c["ident"], c["identb"], c["onesb"], c["onesf"]
    mTi, mTs, LTs, cmask, bm = c["mTi"], c["mTs"], c["LTs"], c["cmask"], c["bm"]
    UT = mTi[:, 0, :]
    EPS_R, ONE_C = c["eps"][:, 0:1], c["eps"][:, 1:2]

    pj = banks[0:2]
    pm = banks[2:5]
    pbf = V(banks[5].ap.bitcast(BF16), banks[5].bufs)
    bk6, bk7 = banks[6], banks[7]
    pcA = [bk6[:, 0:128], bk6[:, 128:256]]
    pcB = [bk7[:, 0:128]]
    pcC = [bk7[:, 128:256]]
    pjn = [0]

    def nextpj():
        pjn[0] ^= 1
        return pj[pjn[0]]

    pmn = [0]

    def nextpm():
        pmn[0] = (pmn[0] + 1) % 3
        return pm[pmn[0]]

    wbf = P.sb("wbf", [128, KC, NCOL], BF16)
    load_weights_bf16(P, k, w_d, wbf, NCOL, "wM")
    sm = P.sb("sm", [128, 32])
    k.dma("sp", "sm", sm, sm_d)
    nrm = P.sb("nrm", [128, 3, 128])
    k.dma("sp", "nrm", nrm, nrm_d)
    w2f = P.sb("w2f", [16, 64])
    k.dma("sp", "w2", w2f, w2_d)
    w2b = P.sb("w2b", [16, 64], BF16)
    k.cp("dve", w2b, w2f)
    prm = P.sb("prm", [128, 16])
    k.act(prm[:, 0:1], sm[:, 12:13], AF.Exp)
    k.ts("dve", prm[:, 0:1], prm[:, 0:1], -1.0, ALU.mult)
    negA = prm[:, 0:1]
    dtb = sm[:, 13:14]
    k.ts("dve", prm[:, 1:2], sm[:, 14:15], -1.0, ALU.mult)
    negb2 = prm[0:64, 1:2]
    lbt = P.sb("lbt", [128, 8])
    k.act(lbt[:, 0:4], sm[:, 16:20], AF.Exp)
    k.P.op("dve", lambda e: e.tensor_reduce(out=lbt.ap[:, 4:5], in_=lbt.ap[:, 0:4], axis=AX.X, op=ALU.add),
           reads=lbt.bufs, writes=lbt.bufs)
    k.recip(lbt[:, 5:6], lbt[:, 4:5])
    k.tt("dve", lbt[:, 0:4], lbt[:, 0:4], sm[:, 20:24], ALU.mult)
    k.P.op("dve", lambda e: e.tensor_reduce(out=lbt.ap[:, 6:7], in_=lbt.ap[:, 0:4], axis=AX.X, op=ALU.add),
           reads=lbt.bufs, writes=lbt.bufs)
    k.tt("dve", prm[:, 2:3], lbt[:, 6:7], lbt[:, 5:6], ALU.mult)
    k.ts("dve", prm[:, 2:3], prm[:, 2:3], 0.0, ALU.max, 1.0, ALU.min)
    lb = prm[:, 2:3]
    k.ts("dve", prm[:, 3:4], lb, -1.0, ALU.mult, 1.0, ALU.add)
    oml = prm[:, 3:4]
    k.ts("dve", prm[:, 4:5], oml, -1.0, ALU.mult)
    noml = prm[:, 4:5]

    hb = [P.sb("hb%d" % i, [128, KC, TB], BF16) for i in range(2)]
    ost = [P.sb("ost%d" % i, [128, 3, TB]) for i in range(2)]

    def W(c0, n):
        return lambda kc: wbf[:, kc, c0:c0 + n]

    def proj_fm(ps, h, wsel, m):
        for kc in range(KC):
            k.mm(ps[0:m, :], wsel(kc), h[:, kc, :], start=(kc == 0), stop=(kc == KC - 1))

    def proj_tm(ps, h, p, wsel):
        for kc in range(KC):
            k.mm(ps, h[:, kc, p * 128:(p + 1) * 128], wsel(kc), start=(kc == 0), stop=(kc == KC - 1))

    pre = [P.sb("pre%d" % i, [128, TB + 3]) for i in range(3)]
    for t in pre:
        k.memset("pool", t[:, 0:3], 0.0)
    cv = [P.sb("cv%d" % i, [128, TB]) for i in range(3)]
    sqb = P.sb("sqb", [128, TB], BF16)
    rn = P.sb("rn", [128, TB])
    qTb = P.sb("qTb", [128, TB], BF16)
    qTf = P.sb("qTf", [128, TB])
    kTf = P.sb("kTf", [128, TB])
    kTb = P.sb("kTb", [128, TB], BF16)
    kbTb = P.sb("kbTb", [128, TB], BF16)
    betaB = P.sb("betaB", [128, TB])
    gB = P.sb("gB", [128, TB])
    cumB = P.sb("cumB", [128, TB])
    ecumB = P.sb("ecumB", [128, TB])
    tok = P.sb("tok", [128, 64])
    gsel = P.sb("gsel", [128, 4, 2])
    lastA = P.sb("lastA", [128, 8])
    kb = P.sb("kb", [128, 4, 128], BF16)
    ksA = P.sb("ksA", [128, 4, 128], BF16)
    vb = P.sb("vb", [128, 4, 128], BF16)
    decT = P.sb("decT", [128, 4, 128])
    dmi = P.sb("dmi", [128, 4, 128])
    dms = P.sb("dms", [128, 4, 128])
    attA = P.sb("attA", [128, 4, 128], BF16)
    Xb = [P.sb("Xb%d" % i, [128, 4, 128], BF16) for i in range(2)]
    Yb = [P.sb("Yb%d" % i, [128, 4, 128], BF16) for i in range(2)]
    Rb = P.sb("Rb", [128, 4, 128], BF16)
    uf = P.sb("uf", [128, 4, 128])
    wTE = P.sb("wTE", [128, 4, 128])
    wTO = P.sb("wTO", [128, 4, 128])
    qEA = P.sb("qEA", [128, 4, 128])
    qOA = P.sb("qOA", [128, 4, 128])
    for t in (wTE, wTO, qEA, qOA):
        k.memset("pool", t, 0.0)
    vnew = P.sb("vnew", [128, 128], BF16)
    SA = [P.sb("SA%d" % i, [128, 128]) for i in range(3)]
    k.memset("pool", SA[0], 0.0)
    sA = [0]

    def dd_tiles(tag, dk):
        d = {}
        for nme in ("q", "kk", "c", "e", "dm", "ksf"):
            d[nme] = scr[nme][0:dk, :]
        d["qm"] = P.sb(tag + "qm", [dk, TB], BF16)
        d["kd"] = P.sb(tag + "kd", [dk, TB], BF16)
        d["qE"] = P.sb(tag + "qE", [dk, 4, 128])
        d["qO"] = P.sb(tag + "qO", [dk, 4, 128])
        k.memset("pool", d["qE"], 0.0)
        k.memset("pool", d["qO"], 0.0)
        d["ks"] = P.sb(tag + "ks", [128, 4, dk], BF16)
        d["v"] = P.sb(tag + "v", [128, 4, 128], BF16)
        d["att"] = P.sb(tag + "att", [128, 4, 128], BF16)
        d["dl"] = P.sb(tag + "dl", [dk, 8])
        d["S"] = [P.sb(tag + "S%d" % i, [dk, 128]) for i in range(3)]
        k.memset("pool", d["S"][0], 0.0)
        d["si"] = 0
        d["dk"] = dk
        return d

    scr = {nme: P.sb("scr" + nme, [128, TB]) for nme in ("q", "kk", "c", "e", "dm", "ksf")}
    tB = dd_tiles("B", 64)
    tC = dd_tiles("C", 128)
    lrb = P.sb("lrb", [16, TB], BF16)
    sgC = P.sb("sgC", [128, TB])
    esc = [{n_: P.sb("%s%d" % (n_, mi_), [128, 128] if n_ != "ssq" else [128, 4]) for n_ in ("junk", "ssq", "sgt", "gw", "onf")}
           for mi_ in range(3)]
    ebank = [pj[0], pj[1], banks[5]]
    obank = [pm[0], pm[1], pm[2]]

    gwS = P.sb("gwS", [128, 12, 128])
    sgG = P.sb("sgG", [128, 128])

    def gen_G(bi, h):
        for mi, gcol in ((0, CA_G), (1, CB_G), (2, CC_G)):
            for p in range(4):
                gp = nextpj()
                proj_tm(gp[:, 0:128], h, p, W(gcol, 128))
                k.act(sgG, gp[:, 0:128], AF.Silu)
                k.tt("pool", gwS[:, mi * 4 + p, :], sgG, nrm[:, mi, :], ALU.mult)
                yield

    def epilogue(mi, o_ps, h, p, gcol, ostg):
        e_ = esc[mi]
        junk, ssq, onf = e_["junk"], e_["ssq"], e_["onf"]
        k.memset("pool", ssq[:, 0:1], 0.0)
        k.act(junk, o_ps, AF.Square, accum=ssq[:, 0:1])
        yield
        k.act(ssq[:, 1:2], ssq[:, 0:1], AF.Sqrt, bias=EPS_R, scale=1.0 / 128.0)
        k.recip(ssq[:, 2:3], ssq[:, 1:2])
        yield
        k.stt(onf, o_ps, ssq[:, 2:3], gwS[:, mi * 4 + p, :], ALU.mult, ALU.mult)
        yield
        tp = ebank[mi]
        k.tr(tp[:, 0:128], onf, ident)
        k.cp("act", ostg[:, mi, p * 128:(p + 1) * 128], tp[:, 0:128])
        yield

    def dd_prep(t, sc):
        dk = t["dk"]
        c3 = t["c"].re("p (c t) -> p c t", t=64)
        k.act(t["e"], t["c"], AF.Exp, scale=sc)
        e4 = t["e"].re("p (a b t) -> p a b t", b=2, t=64)
        q4 = t["q"].re("p (a b t) -> p a b t", b=2, t=64)
        k.tt("dve", t["qE"][:, :, 0:64], q4[:, :, 0, :], e4[:, :, 0, :], ALU.mult)
        k.tt("dve", t["qO"][:, :, 64:128], q4[:, :, 1, :], e4[:, :, 1, :], ALU.mult)
        yield
        k.act(t["dl"], c3[:, :, 63], AF.Exp, scale=sc)
        dm3 = t["dm"].re("p (c t) -> p c t", t=64)
        k.tt("dve", dm3, c3, c3[:, :, 31:32].bc([dk, 8, 64]), ALU.subtract)
        k.act(t["e"], t["dm"], AF.Exp, scale=sc)
        k.tt("dve", t["qm"], t["q"], t["e"], ALU.mult)
        yield
        k.act(t["e"], t["dm"], AF.Exp, scale=-sc)
        k.tt("dve", t["kd"], t["kk"], t["e"], ALU.mult)
        yield
        k.tt("dve", dm3, c3[:, :, 63:64].bc([dk, 8, 64]), c3, ALU.subtract)
        k.act(t["e"], t["dm"], AF.Exp, scale=sc)
        k.tt("dve", t["ksf"], t["kk"], t["e"], ALU.mult)
        yield
        tp = nextpm()
        for p in range(4):
            k.tr(tp[:, p * 128:p * 128 + dk], t["ksf"][:, p * 128:(p + 1) * 128], ident[0:dk, 0:dk])
        k.cp("dve", t["ks"], tp.re("p (a b) -> p a b", b=128)[:, :, 0:dk])
        yield
        ap_ = nextpm()
        for p in range(4):
            k.mm(ap_[:, p * 128:(p + 1) * 128], t["kd"][:, p * 128:(p + 1) * 128], t["qm"][:, p * 128:(p + 1) * 128])
        k.tt("dve", t["att"], ap_.re("p (a b) -> p a b", b=128), mTi, ALU.mult)
        yield

    def dd_chain(t, pc, mi, h, gcol, ostg):
        dk = t["dk"]
        S = t["S"]
        for p in range(4):
            s0, s1, s2 = t["si"], (t["si"] + 1) % 3, (t["si"] + 2) % 3
            ds = pc[0]
            k.mm(ds[0:dk, :], t["ks"][0:64, p, :], t["v"][0:64, p, :])
            k.stt(S[s1], S[s0], t["dl"][:, 2 * p:2 * p + 1], ds[0:dk, :], ALU.mult, ALU.add)
            yield
            o = obank[mi][:, 0:128]
            k.mm(o, t["att"][:, p, :], t["v"][:, p, :], start=True, stop=False)
            k.mm(o, t["qE"][:, p, :], S[s0], start=False, stop=False)
            k.mm(o, t["qO"][:, p, :], S[s1], start=False, stop=True)
            k.mm(ds[0:dk, :], t["ks"][64:128, p, :], t["v"][64:128, p, :])
            k.stt(S[s2], S[s1], t["dl"][:, 2 * p + 1:2 * p + 2], ds[0:dk, :], ALU.mult, ALU.add)
            t["si"] = s2
            yield
            yield from epilogue(mi, o, h, p, gcol, ostg)

    def gen_A(bi, h):
        for i, c0 in enumerate((CA_Q, CA_K, CA_V)):
            ps = nextpj()
            proj_fm(ps, h, W(c0, 128), 128)
            if bi > 0:
                k.cp("pool", pre[i][:, 0:3], pre[i][:, TB:TB + 3])
            k.cp("act", pre[i][:, 3:TB + 3], ps)
            k.ts("dve", cv[i], pre[i][:, 0:TB], sm[:, 4 * i:4 * i + 1], ALU.mult)
            for j in range(1, 4):
                k.stt(cv[i], pre[i][:, j:TB + j], sm[:, 4 * i + j:4 * i + j + 1], cv[i], ALU.mult, ALU.add)
            k.act(cv[i], cv[i], AF.Silu)
            yield
        for i in range(2):
            k.act(sqb, cv[i], AF.Square)
            ps = nextpj()
            k.mm(ps, onesb, sqb)
            k.act(rn, ps, AF.Sqrt, bias=EPS_R)
            k.recip(rn, rn)
            if i == 0:
                k.stt(qTf, cv[0], 128.0 ** -0.5, rn, ALU.mult, ALU.mult)
                k.cp("dve", qTb, qTf)
            else:
                k.tt("dve", kTf, cv[1], rn, ALU.mult)
                k.cp("dve", kTb, kTf)
            yield
        yield
        ps = nextpj()
        proj_fm(ps, h, W(CA_BR, 128), 128)
        k.act(betaB, ps, AF.Sigmoid)
        k.tt("dve", kbTb, kTf, betaB, ALU.mult)
        yield
        ps = nextpj()
        proj_fm(ps, h, W(CA_DR, 128), 128)
        k.act(gB, ps, AF.Exp, bias=dtb)
        k.act(gB, gB, AF.Ln, bias=ONE_C)
        k.ts("dve", gB, gB, negA, ALU.mult)
        k.scan(cumB, cmask, gB)
        k.act(ecumB, cumB, AF.Exp)
        yield
        e4 = ecumB.re("p (a b t) -> p a b t", b=2, t=64)
        q4 = qTf.re("p (a b t) -> p a b t", b=2, t=64)
        k.tt("dve", qEA[:, :, 0:64], q4[:, :, 0, :], e4[:, :, 0, :], ALU.mult)
        k.tt("dve", qOA[:, :, 64:128], q4[:, :, 1, :], e4[:, :, 1, :], ALU.mult)
        yield
        ptok = nextpj()[:, 0:128]
        for p in range(4):
            proj_tm(ptok[:, 16 * p:16 * p + 16], h, p, W(CA_BD - 14, 16))
        pt2 = ptok[:, 0:64].re("p (a b) -> p a b", b=16)
        k.act(tok[:, 0:4], pt2[:, :, 14], AF.Sigmoid)
        k.act(tok[:, 4:8], pt2[:, :, 15], AF.Exp, bias=dtb)
        k.act(tok[:, 4:8], tok[:, 4:8], AF.Ln, bias=ONE_C)
        k.ts("dve", tok[:, 4:8], tok[:, 4:8], negA, ALU.mult)
        k.mm(ptok[:, 64:68], UT, tok[:, 4:8])
        k.mm(ptok[:, 68:72], LTs, tok[:, 4:8])
        k.ts("dve", gsel[:, :, 0], tok[:, 4:8], bm[:, 0:1], ALU.mult)
        k.ts("dve", gsel[:, :, 1], tok[:, 4:8], bm[:, 1:2], ALU.mult)
        k.mm(ptok[:, 72:80], onesf, gsel.re("p a b -> p (a b)"))
        k.act(lastA, ptok[:, 72:80], AF.Exp)
        k.cp("dve", tok[:, 8:12], ptok[:, 64:68])
        k.act(tok[:, 16:20], ptok[:, 64:68], AF.Exp)
        k.tt("dve", tok[:, 16:20], tok[:, 16:20], tok[:, 0:4], ALU.mult)
        k.act(tok[:, 20:24], ptok[:, 68:72], AF.Exp)
        yield
        tpk = nextpm()
        for p in range(4):
            k.tr(tpk[:, p * 128:(p + 1) * 128], kTf[:, p * 128:(p + 1) * 128], ident)
        tpv = nextpm()
        for p in range(4):
            k.tr(tpv[:, p * 128:(p + 1) * 128], cv[2][:, p * 128:(p + 1) * 128], ident)
        for p in range(4):
            k.ts("dve", kb[:, p, :], tpk[:, p * 128:(p + 1) * 128], tok[:, 16 + p:17 + p], ALU.mult)
            k.ts("dve", ksA[:, p, :], tpk[:, p * 128:(p + 1) * 128], tok[:, 20 + p:21 + p], ALU.mult)
            k.ts("dve", vb[:, p, :], tpv[:, p * 128:(p + 1) * 128], tok[:, p:p + 1], ALU.mult)
        yield
        for p in range(4):
            k.ts("dve", decT[:, p, :], cumB[:, p * 128:(p + 1) * 128], tok[:, 8 + p:9 + p], ALU.subtract,
                 0.0, ALU.min)
        k.act(decT, decT, AF.Exp)
        yield
        k.tt("pool", dmi, decT, mTi, ALU.mult)
        k.tt("pool", dms, decT, mTs, ALU.mult)
        yield
        pq = nextpm()
        for p in range(4):
            sl = slice(p * 128, (p + 1) * 128)
            k.mm(pq[:, sl], kTb[:, sl], qTb[:, sl])
        k.tt("dve", attA, pq.re("p (a b) -> p a b", b=128), dmi, ALU.mult)
        yield
        pk = nextpm()
        for p in range(4):
            sl = slice(p * 128, (p + 1) * 128)
            k.mm(pk[:, sl], kTb[:, sl], kbTb[:, sl])
        X, Y = Xb[0], Yb[0]
        k.tt("dve", X, pk.re("p (a b) -> p a b", b=128), dms, ALU.mult)
        yield
        pb3 = pbf[:, 0:512].re("p (a b) -> p a b", b=128)
        for p in range(4):
            k.tr(pb3[:, p, :], X[:, p, :], identb)
        k.cp("act", Y, pb3)
        yield
        k.tt("pool", Rb, identb.re("p (a b) -> p a b", a=1).bc([128, 4, 128]), X, ALU.subtract)
        yield
        xi = 0
        for lvl in range(int(os.environ.get('NLVL', '5'))):
            Xn, Yn = Xb[xi ^ 1], Yb[xi ^ 1]
            yield
            py = nextpm()
            for p in range(4):
                k.mm(py[:, p * 128:(p + 1) * 128], X[:, p, :], Y[:, p, :])
            k.cp("act", Yn, py.re("p (a b) -> p a b", b=128))
            yield
            if lvl < 4:
                px = nextpm()
                for p in range(4):
                    k.mm(px[:, p * 128:(p + 1) * 128], Y[:, p, :], X[:, p, :])
                k.cp("dve", Xn, px.re("p (a b) -> p a b", b=128))
            yield
            pr = nextpm()
            for p in range(4):
                k.mm(pr[:, p * 128:(p + 1) * 128], Yn[:, p, :], Rb[:, p, :])
            k.tt("dve", Rb, pr.re("p (a b) -> p a b", b=128), Rb, ALU.add)
            X, Y = Xn, Yn
            xi ^= 1
        yield
        pu = nextpm()
        for p in range(4):
            k.mm(pu[:, p * 128:(p + 1) * 128], Rb[:, p, :], vb[:, p, :])
        k.cp("act", uf, pu.re("p (a b) -> p a b", b=128))
        yield
        pw = nextpm()
        for p in range(4):
            k.mm(pw[:, p * 128:(p + 1) * 128], kb[:, p, :], Rb[:, p, :])
        pw3 = pw.re("p (a b) -> p a b", b=128)
        k.cp("dve", wTE[:, :, 0:64], pw3[:, :, 0:64])
        k.cp("dve", wTO[:, :, 64:128], pw3[:, :, 64:128])

        yield

    def gen_BC(bi, h):
        ps = nextpj()
        proj_fm(ps, h, W(CB_Q, 64), 64)
        k.act(tB["q"], ps[0:64, :], AF.Identity, scale=64.0 ** -0.5)
        yield
        ps = nextpj()
        proj_fm(ps, h, W(CB_K, 64), 64)
        k.cp("act", tB["kk"], ps[0:64, :])
        yield
        ps = nextpj()
        proj_fm(ps, h, W(CB_LR, 16), 16)
        k.cp("act", lrb, ps[0:16, :])
        yield
        ps = nextpj()
        k.mm(ps[0:64, :], w2b, lrb)
        k.act(tB["e"], ps[0:64, :], AF.Exp, bias=negb2, scale=-1.0)
        k.act(tB["e"], tB["e"], AF.Ln, bias=ONE_C[0:64, :])
        k.scan(tB["c"], cmask[0:64, :], tB["e"])
        yield
        for p in range(4):
            ps = nextpj()
            proj_tm(ps[:, 0:128], h, p, W(CB_V, 128))
            k.cp("act", tB["v"][:, p, :], ps[:, 0:128])
            yield
        yield from dd_prep(tB, -1.0 / 16.0)

        yield
        ps = nextpj()
        proj_fm(ps, h, W(CC_Q, 128), 128)
        k.act(tC["q"], ps, AF.Silu)
        k.ts("dve", tC["q"], tC["q"], 128.0 ** -0.5, ALU.mult)
        yield
        ps = nextpj()
        proj_fm(ps, h, W(CC_F, 128), 128)
        k.act(sgC, ps, AF.Sigmoid)
        k.ts("dve", tC["kk"], sgC, noml, ALU.mult, oml, ALU.add)
        k.ts("dve", tC["e"], sgC, oml, ALU.mult, lb, ALU.add)
        k.act(tC["e"], tC["e"], AF.Ln)
        k.scan(tC["c"], cmask, tC["e"])
        yield
        for p in range(4):
            ps = nextpj()
            proj_tm(ps[:, 0:128], h, p, W(CC_I, 128))
            k.cp("act", tC["v"][:, p, :], ps[:, 0:128])
            yield
        yield from dd_prep(tC, 1.0)

        yield

    def chain_A(h, ostg):
        for p in range(4):
            s0, s1, s2 = sA[0], (sA[0] + 1) % 3, (sA[0] + 2) % 3
            wS, dS = pcA
            o = obank[0][:, 0:128]
            k.mm(wS, wTE[:, p, :], SA[s0])
            k.tt("dve", vnew[0:64, :], uf[0:64, p, :], wS[0:64, :], ALU.subtract)
            yield
            k.mm(dS, ksA[0:64, p, :], vnew[0:64, :])
            k.stt(SA[s1], SA[s0], lastA[:, 2 * p:2 * p + 1], dS, ALU.mult, ALU.add)
            yield
            k.mm(wS, wTO[:, p, :], SA[s1])
            k.tt("dve", vnew[64:128, :], uf[64:128, p, :], wS[64:128, :], ALU.subtract)
            yield
            k.mm(dS, ksA[64:128, p, :], vnew[64:128, :])
            k.mm(o, attA[:, p, :], vnew, start=True, stop=False)
            k.mm(o, qEA[:, p, :], SA[s0], start=False, stop=False)
            k.mm(o, qOA[:, p, :], SA[s1], start=False, stop=True)
            k.stt(SA[s2], SA[s1], lastA[:, 2 * p + 1:2 * p + 2], dS, ALU.mult, ALU.add)
            sA[0] = s2
            yield
            yield from epilogue(0, o, h, p, CA_G, ostg)

    def interleave(gens):
        gens = list(gens)
        while gens:
            for g in list(gens):
                try:
                    next(g)
                except StopIteration:
                    gens.remove(g)

    def batch_body(bi, t0, hs, h, ostg):
        if bi + 1 < nb:
            h_load(bi + 1, hb[(bi + 1) % 2])
        if os.environ.get("NOILV", "0") == "1":
            for g in (gen_A(bi, h), gen_BC(bi, h), gen_G(bi, h), chain_A(h, ostg), dd_chain(tB, pcB, 1, h, CB_G, ostg),
                      dd_chain(tC, pcC, 2, h, CC_G, ostg)):
                for _ in g:
                    pass
            return
        interleave([gen_A(bi, h), gen_BC(bi, h), gen_G(bi, h)])
        interleave([chain_A(h, ostg), dd_chain(tB, pcB, 1, h, CB_G, ostg), dd_chain(tC, pcC, 2, h, CC_G, ostg)])


    fin = []
    h_load(0, hb[0])
    for bi in range(nb):
        t0 = bi * TB
        hs, h, ostg = None, hb[bi % 2], ost[bi % 2]
        if MSTOP < 99:
            k.memset("pool", ostg, 0.0)
        try:
            batch_body(bi, t0, hs, h, ostg)
        except _Stop:
            pass
        ob = obufs[bi] if obufs is not None else Buf("oT_out%d" % bi)
        P.dma("sp", "o%d" % (bi % 2), o_dst(bi), ostg.ap, reads=ostg.bufs, writes=[ob])
        fin.append(ob)
        if after_batch is not None:
            after_batch(bi)
    return fin


def m_inputs(hT_b, l, hd, w_in, gdn_conv, gdn_a_log, gdn_dt_bias, gdn_norm, gla_w2, gla_b2, gla_norm,
             hgrn_lb_logits, hgrn_norm):
    wl = w_in[l]
    o = 0
    offs = {}
    for name, sz in (("aq", 512), ("ak", 512), ("av", 512), ("ab", 4), ("adt", 4), ("ag", 512),
                     ("bq", 256), ("bk", 256), ("bv", 512), ("blr", 16), ("bg", 512),
                     ("cq", 512), ("cf", 512), ("ci", 512), ("cg", 512)):
        offs[name] = o
        o += sz

    def col(name, width):
        s = offs[name] + hd * width
        return wl[:, s:s + width]

    w = np.empty((D, NCOL), np.float32)
    w[:, CA_Q:CA_Q + 128] = col("aq", 128)
    w[:, CA_K:CA_K + 128] = col("ak", 128)
    w[:, CA_V:CA_V + 128] = col("av", 128)
    w[:, CA_BR:CA_BR + 128] = np.repeat(col("ab", 1), 128, axis=1)
    w[:, CA_DR:CA_DR + 128] = np.repeat(col("adt", 1), 128, axis=1)
    w[:, CA_G:CA_G + 128] = col("ag", 128)
    w[:, CA_BD:CA_BD + 1] = col("ab", 1)
    w[:, CA_BD + 1:CA_BD + 2] = col("adt", 1)
    w[:, CB_Q:CB_Q + 64] = col("bq", 64)
    w[:, CB_K:CB_K + 64] = col("bk", 64)
    w[:, CB_LR:CB_LR + 16] = wl[:, offs["blr"]:offs["blr"] + 16]
    w[:, CB_V:CB_V + 128] = col("bv", 128)
    w[:, CB_G:CB_G + 128] = col("bg", 128)
    w[:, CC_Q:CC_Q + 128] = col("cq", 128)
    w[:, CC_F:CC_F + 128] = col("cf", 128)
    w[:, CC_I:CC_I + 128] = col("ci", 128)
    w[:, CC_G:CC_G + 128] = col("cg", 128)
    sm = np.zeros((128, 32), np.float32)
    cw = gdn_conv[l]
    for i in range(3):
        sm[:, 4 * i:4 * i + 4] = cw[:, i * 512 + hd * 128:i * 512 + (hd + 1) * 128].T
    sm[:, 12] = gdn_a_log[l, hd]
    sm[:, 13] = gdn_dt_bias[l, hd]
    sm[0:64, 14] = gla_b2[l, hd * 64:(hd + 1) * 64]
    sm[:, 16:20] = hgrn_lb_logits[:, hd * 128:(hd + 1) * 128].T
    for j in range(4):
        sm[:, 20 + j] = 1.0 if (1 <= j <= l) else 0.0
    nrm = np.empty((128, 3, 128), np.float32)
    nrm[:, 0, :] = gdn_norm[l][None, :]
    nrm[:, 1, :] = gla_norm[l][None, :]
    nrm[:, 2, :] = hgrn_norm[l][None, :]
    w2 = np.ascontiguousarray(gla_w2[l][:, hd * 64:(hd + 1) * 64])
    return {"hT": np.ascontiguousarray(hT_b), "w": w, "sm": sm, "nrm": nrm, "w2": w2}


TF = 512
NTF = 2048


class Rot:
    def __init__(self, tiles):
        self.t = tiles
        self.i = -1

    def nxt(self):
        self.i = (self.i + 1) % len(self.t)
        return self.t[self.i]


def f_common(P, k, banks, ident, onesf):
    c = {"onesf": onesf, "ident": ident}
    ec = P.sb("epscF", [128, 4])
    k.memset("pool", ec[:, 0:1], LN_EPS)
    c["eps"] = ec[:, 0:1]
    c["banks"] = banks
    sel = P.sb("sel", [16, 16, 128])
    k.memset("pool", sel, 1.0)
    k.asel(sel, [[-1, 16], [0, 128]], ALU.is_equal, 1)
    c["sel"] = sel
    return c


def f_work(P, c):
    c["ps"] = Rot(c["banks"])
    c["ln"] = [P.sb("lnw%d" % i, [128, TF]) for i in range(4)]
    c["zsq"] = P.sb("zsq", [128, TF])


def emit_ln(P, k, c, z, g, b, out32, outb=None):
    onesf = c["onesf"]
    mean, msq, var, rstd = c["ln"]
    zsq = c["zsq"]
    p1 = c["ps"].nxt()
    for kc in range(KC):
        k.mm(p1, onesf, z[:, kc, :], start=(kc == 0), stop=(kc == KC - 1))
    k.act(mean, p1, AF.Identity, scale=1.0 / D)
    p2 = c["ps"].nxt()
    for kc in range(KC):
        k.act(zsq, z[:, kc, :], AF.Square)
        k.mm(p2, onesf, zsq, start=(kc == 0), stop=(kc == KC - 1))
    k.tt("dve", msq, mean, mean, ALU.mult)
    k.stt(var, p2, 1.0 / D, msq, ALU.mult, ALU.subtract)
    k.act(var, var, AF.Sqrt, bias=c["eps"])
    k.recip(rstd, var)
    for kc in range(KC):
        k.tt("dve", zsq, z[:, kc, :], mean, ALU.subtract)
        k.tt("dve", zsq, zsq, rstd, ALU.mult)
        k.ts("dve", out32[:, kc, :], zsq, g[:, kc:kc + 1], ALU.mult, b[:, kc:kc + 1], ALU.add)
        if outb is not None:
            k.cp("act", outb[:, kc, :], out32[:, kc, :])


def emit_L0(P, k, c, ntf, x_src, gb_d, out_dst, obuf, after_tile=None):
    f_work(P, c)
    gb = P.sb("gb0", [128, 16])
    k.dma("sp", "gb", gb, gb_d)
    zt = [P.sb("z%d" % i, [128, KC, TF]) for i in range(2)]
    for ti in range(ntf // TF):
        z = zt[ti % 2]
        k.dma("sp", "z%d" % (ti % 2), z, x_src(ti))
        emit_ln(P, k, c, z, gb[:, 0:8], gb[:, 8:16], z)
        P.dma("sp", "hTo", out_dst(ti), z.ap, reads=z.bufs, writes=[obuf(ti)])
        if after_tile is not None:
            after_tile(ti)


def emit_F(P, k, c, ntf, h_src, o_load, wmg_d, wbr_d, wout_d, gb_d, wr_d, rb_d, wg_d, wu_d, wd_d, out_dst, obuf,
           after_tile=None):
    f_work(P, c)
    PS = c["ps"]
    sel = c["sel"]
    gb = P.sb("gb", [128, 32])
    k.dma("sp", "gb", gb, gb_d)
    wr = P.sb("wr", [128, KC, 16])
    k.dma("sp", "wr", wr, wr_d.rearrange("(kc p) n -> p kc n", p=128))
    rb = P.sb("rb", [128, 16])
    k.dma("sp", "rb", rb, rb_d)
    stg = Rot([P.sb("stg%d" % i, [128, 2048]) for i in range(2)])
    NWB, DEPTH_PF = 12, int(os.environ.get("DPF", "8"))
    wbt = [P.sb("wbt%d" % i, [128, 2048], BF16) for i in range(NWB)]
    ceng = None
    reqs = []
    for ti in range(ntf // TF):
        for n in range(KC):
            for x in range(3):
                reqs.append((wmg_d.rearrange("(kc p) n -> p kc n", p=128)[:, :, x * D + n * 128:x * D + (n + 1) * 128], KC, 128))
                reqs.append((wbr_d[x].rearrange("(hc p) n -> p hc n", p=128)[:, :, n * 128:(n + 1) * 128], 4, 128))
        for n in range(KC):
            reqs.append((wout_d.rearrange("(kc p) n -> p kc n", p=128)[:, :, n * 128:(n + 1) * 128], KC, 128))
        for e in range(16):
            reqs.append((wg_d[e].rearrange("(kc p) f -> p kc f", p=128), KC, 256))
            reqs.append((wu_d[e].rearrange("(kc p) f -> p kc f", p=128), KC, 256))
            reqs.append((wd_d[e].rearrange("(fc p) d -> p fc d", p=128), 2, D))
    st = {"issued": 0, "taken": 0}

    def issue_one():
        i = st["issued"]
        dram3, a, b = reqs[i]
        w = wbt[i % NWB]
        k.dma("pool", "wb%d" % (i % NWB), w[:, 0:a * b].re("p (a b) -> p a b", b=b), dram3)
        st["issued"] += 1

    def stream(dram3, a, b):
        i = st["taken"]
        assert reqs[i][1] == a and reqs[i][2] == b
        while st["issued"] < len(reqs) and st["issued"] < i + 1 + DEPTH_PF:
            issue_one()
        st["taken"] += 1
        return wbt[i % NWB][:, 0:a * b].re("p (a b) -> p a b", b=b)

    hs = P.sb("hs", [128, KC, TF])
    hb = P.sb("hbF", [128, KC, TF], BF16)
    ob = P.sb("ob", [128, 3, 4, TF], BF16)
    yc = P.sb("yc", [128, TF])
    t1 = P.sb("t1", [128, TF])
    sg = P.sb("sg", [128, TF])
    yTb = P.sb("yTb", [128, KC, TF], BF16)
    z = P.sb("z", [128, KC, TF])
    h1b = P.sb("h1b", [128, KC, TF], BF16)
    yacc = P.sb("yacc", [128, KC, TF])
    aT = [P.sb("aT%d" % i, [128, TF], BF16) for i in range(2)]
    rt = P.sb("rt", [128, 4, 96])
    combT = P.sb("combT", [16, TF])
    for ti in range(ntf // TF):
        k.dma("sp", "hs", hs, h_src(ti))
        k.dma("pool", "hbF", hb, h_src(ti))
        o_load(ti, ob, stg, ceng)
        for n in range(KC):
            for x in range(3):
                wm = stream(wmg_d.rearrange("(kc p) n -> p kc n", p=128)[:, :, x * D + n * 128:x * D + (n + 1) * 128], KC, 128)
                pm_ = PS.nxt()
                for kc in range(KC):
                    k.mm(pm_, wm[:, kc, :], hb[:, kc, :], start=(kc == 0), stop=(kc == KC - 1))
                k.act(sg, pm_, AF.Sigmoid)
                wb = stream(wbr_d[x].rearrange("(hc p) n -> p hc n", p=128)[:, :, n * 128:(n + 1) * 128], 4, 128)
                pb_ = PS.nxt()
                for hc in range(4):
                    k.mm(pb_, wb[:, hc, :], ob[:, x, hc, :], start=(hc == 0), stop=(hc == 3))
                if x == 0:
                    k.tt("dve", yc, pb_, sg, ALU.mult)
                else:
                    k.tt("dve", t1, pb_, sg, ALU.mult)
                    k.tt("dve", yc, yc, t1, ALU.add)
            k.cp("act", yTb[:, n, :], yc)
        for n in range(KC):
            wo = stream(wout_d.rearrange("(kc p) n -> p kc n", p=128)[:, :, n * 128:(n + 1) * 128], KC, 128)
            pm_ = PS.nxt()
            for kc in range(KC):
                k.mm(pm_, wo[:, kc, :], yTb[:, kc, :], start=(kc == 0), stop=(kc == KC - 1))
            k.stt(z[:, n, :], hs[:, n, :], ALPHA, pm_, ALU.mult, ALU.add)
        emit_ln(P, k, c, z, gb[:, 0:8], gb[:, 8:16], z, h1b)
        pr_ = PS.nxt()
        for blk in range(4):
            for kc in range(KC):
                k.mm(pr_[:, blk * 16:(blk + 1) * 16], z[:, kc, blk * 128:(blk + 1) * 128], wr[:, kc, :],
                     start=(kc == 0), stop=(kc == KC - 1))
        pt_ = PS.nxt()
        for blk in range(4):
            r = rt[:, blk, :]
            sc, se, tm, mk = r[:, 0:16], r[:, 16:32], r[:, 32:48], r[:, 48:64]
            sc3, se3, tm3, mk3 = [v.re("p (g e) -> p g e", e=4) for v in (sc, se, tm, mk)]
            m1, m2, gs, gmax, gmask, den = r[:, 64:68], r[:, 68:72], r[:, 72:76], r[:, 76:77], r[:, 80:84], r[:, 84:85]
            k.act(sc, pr_[:, blk * 16:(blk + 1) * 16], AF.Sigmoid)
            k.tt("dve", se, sc, rb, ALU.add)
            k.P.op("dve", lambda e, o=m1.ap, i=se3.ap: e.tensor_reduce(out=o, in_=i, axis=AX.X, op=ALU.max),
                   reads=r.bufs, writes=r.bufs)
            k.tt("dve", tm3, se3, m1.re("p (g o) -> p g o", o=1).bc([128, 4, 4]), ALU.is_equal)
            k.stt(tm, tm, -1.0e9, se, ALU.mult, ALU.add)
            k.P.op("dve", lambda e, o=m2.ap, i=tm3.ap: e.tensor_reduce(out=o, in_=i, axis=AX.X, op=ALU.max),
                   reads=r.bufs, writes=r.bufs)
            k.tt("dve", gs, m1, m2, ALU.add)
            k.P.op("dve", lambda e, o=gmax.ap, i=gs.ap: e.tensor_reduce(out=o, in_=i, axis=AX.X, op=ALU.max),
                   reads=r.bufs, writes=r.bufs)
            k.ts("dve", gmask, gs, gmax, ALU.is_equal)
            k.tt("dve", mk3, se3, m2.re("p (g o) -> p g o", o=1).bc([128, 4, 4]), ALU.is_ge)
            k.tt("dve", mk3, mk3, gmask.re("p (g o) -> p g o", o=1).bc([128, 4, 4]), ALU.mult)
            k.tt("dve", mk, mk, sc, ALU.mult)
            k.P.op("dve", lambda e, o=den.ap, i=mk.ap: e.tensor_reduce(out=o, in_=i, axis=AX.X, op=ALU.add),
                   reads=r.bufs, writes=r.bufs)
            k.recip(den, den)
            k.ts("dve", mk, mk, den, ALU.mult)
            k.tr(pt_[0:16, blk * 128:(blk + 1) * 128], mk, c["ident"])
        k.cp("dve", combT, pt_[0:16, :])
        for e in range(16):
            wg = stream(wg_d[e].rearrange("(kc p) f -> p kc f", p=128), KC, 256)
            wu = stream(wu_d[e].rearrange("(kc p) f -> p kc f", p=128), KC, 256)
            wd = stream(wd_d[e].rearrange("(fc p) d -> p fc d", p=128), 2, D)
            pc_ = PS.nxt()
            k.mm(pc_, sel[:, e, :], combT)
            for f in range(2):
                pg_ = PS.nxt()
                for kc in range(KC):
                    k.mm(pg_, wg[:, kc, f * 128:(f + 1) * 128], h1b[:, kc, :], start=(kc == 0), stop=(kc == KC - 1))
                pu_ = PS.nxt()
                for kc in range(KC):
                    k.mm(pu_, wu[:, kc, f * 128:(f + 1) * 128], h1b[:, kc, :], start=(kc == 0), stop=(kc == KC - 1))
                k.act(sg, pg_, AF.Silu)
                k.tt("dve", t1, pu_, sg, ALU.mult)
                k.tt("dve", aT[f], pc_, t1, ALU.mult)
            for dc in range(KC):
                py_ = PS.nxt()
                for f in range(2):
                    k.mm(py_, wd[:, f, dc * 128:(dc + 1) * 128], aT[f], start=(f == 0), stop=(f == 1))
                if e == 0:
                    k.cp("dve", yacc[:, dc, :], py_)
                else:
                    k.tt("dve", yacc[:, dc, :], py_, yacc[:, dc, :], ALU.add)
        for n in range(KC):
            k.stt(yacc[:, n, :], z[:, n, :], ALPHA, yacc[:, n, :], ALU.mult, ALU.add)
        emit_ln(P, k, c, yacc, gb[:, 16:24], gb[:, 24:32], yacc)
        P.dma("sp", "hTo", out_dst(ti), yacc.ap, reads=yacc.bufs, writes=[obuf(ti)])
        if after_tile is not None:
            after_tile(ti)


def build_F(NTF=2048):
    nc = bass.Bass("TRN2", target_bir_lowering=False)
    hT_d = nc.dram_tensor("hT", [D, NTF], F32, kind="ExternalInput").ap()
    oT_d = nc.dram_tensor("oT", [3, 512, NTF], F32, kind="ExternalInput").ap()
    wmg_d = nc.dram_tensor("wmg", [D, 3 * D], F32, kind="ExternalInput").ap()
    wbr_d = nc.dram_tensor("wbr", [3, 512, D], F32, kind="ExternalInput").ap()
    wout_d = nc.dram_tensor("wout", [D, D], F32, kind="ExternalInput").ap()
    gb_d = nc.dram_tensor("gb", [128, 32], F32, kind="ExternalInput").ap()
    wr_d = nc.dram_tensor("wr", [D, 16], F32, kind="ExternalInput").ap()
    rb_d = nc.dram_tensor("rb", [128, 16], F32, kind="ExternalInput").ap()
    wg_d = nc.dram_tensor("wg", [16, D, 256], F32, kind="ExternalInput").ap()
    wu_d = nc.dram_tensor("wu", [16, D, 256], F32, kind="ExternalInput").ap()
    wd_d = nc.dram_tensor("wd", [16, 256, D], F32, kind="ExternalInput").ap()
    o_d = nc.dram_tensor("hTo", [D, NTF], F32, kind="ExternalOutput").ap()
    with ExitStack() as es:
        P = Prog(nc, es)
        k = K(P)
        banks = [P.ps("bk%d" % i, [128, 512]) for i in range(8)]
        cm = make_consts(P, k)
        c = f_common(P, k, banks, cm["ident"], cm["onesf"])
        hv = hT_d.rearrange("(kc p) t -> p kc t", p=128)
        ov = o_d.rearrange("(kc p) t -> p kc t", p=128)

        def o_load(ti, ob, stg, ceng):
            for x in range(3):
                s_ = stg.nxt()
                k.dma("sp", "stg%d" % stg.i, s_.re("p (a b) -> p a b", b=TF),
                      oT_d[x].rearrange("(hc p) t -> p hc t", p=128)[:, :, ti * TF:(ti + 1) * TF])
                k.cp("dve", ob[:, x, :, :].re("p a b -> p (a b)"), s_)

        obuf = Buf("hTo")
        emit_F(P, k, c, NTF, lambda ti: hv[:, :, ti * TF:(ti + 1) * TF], o_load, wmg_d, wbr_d, wout_d, gb_d, wr_d, rb_d,
               wg_d, wu_d, wd_d, lambda ti: ov[:, :, ti * TF:(ti + 1) * TF], lambda ti: obuf)
        P.wait_all("sp", [obuf])
        P.run()
        print("F program: inst", P.n_inst, "waits", P.n_wait, {e: len(v) for e, v in P.q.items()})
    return nc


def f_inputs(hT_c, oT_c, l, w_in, w_br_a, w_br_b, w_br_c, w_out, ln1_g, ln1_b, w_router, router_bias,
             w_gate, w_up, w_down, ln2_g, ln2_b):
    gb = np.empty((128, 32), np.float32)
    gb[:, 0:8] = ln1_g[l].reshape(8, 128).T
    gb[:, 8:16] = ln1_b[l].reshape(8, 128).T
    gb[:, 16:24] = ln2_g[l].reshape(8, 128).T
    gb[:, 24:32] = ln2_b[l].reshape(8, 128).T
    return {"hT": np.ascontiguousarray(hT_c), "oT": np.ascontiguousarray(oT_c),
            "wmg": np.ascontiguousarray(w_in[l][:, -3 * D:]),
            "wbr": np.stack([w_br_a[l], w_br_b[l], w_br_c[l]]), "wout": w_out[l], "gb": gb,
            "wr": w_router, "rb": np.ascontiguousarray(np.broadcast_to(router_bias[None, :], (128, 16))),
            "wg": w_gate[l], "wu": w_up[l], "wd": w_down[l]}


RG = [[0, 1, 2, 3], [4, 5, 6, 7]]


def build_fused(nlay=DEPTH):
    nc = bass.Bass("TRN2", target_bir_lowering=False)
    dt = lambda name, shape: nc.dram_tensor(name, shape, F32, kind="ExternalInput").ap()
    xT_d = dt("xT", [D, NTF])
    gb0_d = dt("gb0", [128, 16])
    wM_d = [dt("wM%d" % l, [D, NCOL]) for l in range(nlay)]
    sm_d = dt("sm", [DEPTH, 128, 32])
    nrm_d = dt("nrm", [DEPTH, 128, 3, 128])
    w2_d = dt("w2", [DEPTH, 16, 64])
    wmg_d = [dt("wmg%d" % l, [D, 3 * D]) for l in range(nlay)]
    wbr_d = [dt("wbr%d" % l, [3, 512, D]) for l in range(nlay)]
    wout_d = [dt("wout%d" % l, [D, D]) for l in range(nlay)]
    gb_d = dt("gb", [DEPTH, 128, 32])
    wr_d = dt("wr", [D, 16])
    rb_d = dt("rb", [128, 16])
    wg_d = [dt("wg%d" % l, [16, D, 256]) for l in range(nlay)]
    wu_d = [dt("wu%d" % l, [16, D, 256]) for l in range(nlay)]
    wd_d = [dt("wd%d" % l, [16, 256, D]) for l in range(nlay)]
    qoh_d = dt("qoh", [128, 4])
    out_d = nc.dram_tensor("hTo", [D, NTF], F32, kind="ExternalOutput").ap()
    hloc = nc.dram_tensor("hloc", [4 * D, TF], F32)
    hgrp = nc.dram_tensor("hgrp", [8 * 4 * 512, TF], F32)
    oloc = nc.dram_tensor("oloc", [12 * 128, NTF], F32)
    ogrp = nc.dram_tensor("ogrp", [12 * 4 * 128, NTF], F32)
    with ExitStack() as es:
        P = Prog(nc, es)
        k = K(P)
        banks = [P.ps("bk%d" % i, [128, 512]) for i in range(8)]
        cm = make_consts(P, k)
        c = f_common(P, k, banks, cm["ident"], cm["onesf"])
        qoh = P.sb("qoh", [128, 4])
        k.dma("sp", "qoh", qoh, qoh_d)
        msel = P.sb("msel", [128, TF])
        B_hl = [Buf("hloc%d" % i) for i in range(4)]
        B_hgrp, B_ogrp, B_out = Buf("hgrp"), Buf("ogrp"), Buf("out")
        B_oloc = [Buf("oloc%d" % i) for i in range(SEQ // TB)]
        mark0 = P.mark()
        xv = xT_d.rearrange("(kc p) t -> p kc t", p=128)
        ov = out_d.rearrange("(kc p) t -> p kc t", p=128)
        hl2, hg2, ol2, og2 = hloc.ap(), hgrp.ap(), oloc.ap(), ogrp.ap()
        ncoll = [0]

        def ag(ins, outs, reads, writes):
            ncoll[0] += 1
            P.coll("ag", lambda e: e.collective_compute("AllGather", ALU.bypass, replica_groups=RG,
                                                        ins=[ins.opt()], outs=[outs.opt()]),
                   reads=reads, writes=writes)

        def ag_h_tile(ti):
            for hf in range(2):
                cc = ti * 2 + hf
                ag(hl2[cc * 512:(cc + 1) * 512, :], hg2[cc * 2048:(cc + 1) * 2048, :], [B_hl[ti]], [B_hgrp])

        def hl_tile(ti):
            return hl2[ti * D:(ti + 1) * D, :].rearrange("(kc p) t -> p kc t", p=128)

        def h_load_M(bi, hs):
            q, ti = bi // 4, bi % 4
            for hf in range(2):
                cc = ti * 2 + hf
                r0 = cc * 2048 + q * 512
                k.dma("pool", "h%d" % (bi % 2), hs[:, hf * 4:(hf + 1) * 4, :],
                      V(hg2[r0:r0 + 512, :].rearrange("(a p) t -> p a t", p=128), [B_hgrp]))

        ol4 = ol2.rearrange("(m tq e) t -> e m tq t", m=3, tq=4)

        def o_dst_M(bi):
            tq, tl = bi // 4, (bi % 4) * TB
            return ol4[:, :, tq, tl:tl + TB]

        def after_batch_M(bi):
            if bi % 4 == 3:
                tq = bi // 4
                for m in range(3):
                    cc = m * 4 + tq
                    ag(ol2[cc * 128:(cc + 1) * 128, :], og2[cc * 512:(cc + 1) * 512, :],
                       B_oloc[4 * tq:4 * tq + 4], [B_ogrp])

        og5 = og2.rearrange("(m j hd e) t -> e m j hd t", m=3, j=4, hd=4)

        def o_load(ti, ob, stg, ceng):
            for x in range(3):
                for hc in range(4):
                    s_ = stg.nxt()
                    s3 = s_.re("p (j t) -> p j t", t=TF)
                    k.dma("sp", "stg%d" % stg.i, s3, V(og5[:, x, :, hc, ti * TF:(ti + 1) * TF], [B_ogrp]))
                    if os.environ.get("OMASK", "1") == "0":
                        k.cp("dve", ob[:, x, hc, :], s3[:, 0, :])
                        continue
                    k.ts("dve", msel, s3[:, 0, :], qoh[:, 0:1], ALU.mult)
                    for j in range(1, 3):
                        k.stt(msel, s3[:, j, :], qoh[:, j:j + 1], msel, ALU.mult, ALU.add)
                    k.stt(ob[:, x, hc, :], s3[:, 3, :], qoh[:, 3:4], msel, ALU.mult, ALU.add)

        emit_L0(P, k, c, NTF, lambda ti: xv[:, :, ti * TF:(ti + 1) * TF], gb0_d, hl_tile, lambda ti: B_hl[ti],
                after_tile=ag_h_tile)
        FSTOP = int(os.environ.get("FSTOP", "9"))

        def dbg_finish():
            P.barrier()
            P.release(mark0)
            t = P.sb("dbg", [128, KC, TF])
            for ti in range(4):
                q = ti
                for hf in range(2):
                    cc = 0 * 2 + hf
                    r0 = cc * 2048 + q * 512
                    k.dma("sp", "h0", t[:, hf * 4:(hf + 1) * 4, :],
                          V(hg2[r0:r0 + 512, :].rearrange("(a p) t -> p a t", p=128), [B_hgrp]))
                P.dma("sp", "hTo", ov[:, :, ti * TF:(ti + 1) * TF], t.ap, reads=t.bufs, writes=[B_out])

        for l in range(nlay):
            if FSTOP == 1:
                dbg_finish()
                break
            P.barrier()
            P.release(mark0)
            emit_M(P, k, cm, banks, SEQ // TB, h_load_M, o_dst_M, wM_d[l], sm_d[l], nrm_d[l], w2_d[l], obufs=B_oloc,
                   after_batch=after_batch_M)
            if FSTOP == 2:
                dbg_finish()
                break
            P.barrier()
            P.release(mark0)
            last = (l == nlay - 1)
            emit_F(P, k, c, NTF, lambda ti: V(hl_tile(ti), [B_hl[ti]]), o_load,
                   wmg_d[l], wbr_d[l], wout_d[l], gb_d[l], wr_d, rb_d, wg_d[l], wu_d[l], wd_d[l],
                   (lambda ti: ov[:, :, ti * TF:(ti + 1) * TF]) if last else hl_tile,
                   (lambda ti: B_out) if last else (lambda ti: B_hl[ti]),
                   after_tile=None if last else ag_h_tile)
        P.wait_all("sp", [B_out])
        P.run()
        print("fused program: inst", P.n_inst, "waits", P.n_wait, {e: len(v) for e, v in P.q.items()})
    return nc


_PROGS = {}
_NLAY = [DEPTH]


def _prog(name, fn):
    if name not in _PROGS:
        _PROGS[name] = fn()
    return _PROGS[name]


def kernel(x, ln0_g, ln0_b, w_in, gdn_conv, gdn_a_log, gdn_dt_bias, gdn_norm, gla_w2, gla_b2, gla_norm,
           hgrn_lb_logits, hgrn_norm, w_br_a, w_br_b, w_br_c, w_out, ln1_g, ln1_b, w_router, router_bias,
           w_gate, w_up, w_down, ln2_g, ln2_b):
    f = lambda a: np.ascontiguousarray(np.asarray(a, dtype=np.float32))
    (x, ln0_g, ln0_b, w_in, gdn_conv, gdn_a_log, gdn_dt_bias, gdn_norm, gla_w2, gla_b2, gla_norm,
     hgrn_lb_logits, hgrn_norm, w_br_a, w_br_b, w_br_c, w_out, ln1_g, ln1_b, w_router, router_bias,
     w_gate, w_up, w_down, ln2_g, ln2_b) = [f(a) for a in (
         x, ln0_g, ln0_b, w_in, gdn_conv, gdn_a_log, gdn_dt_bias, gdn_norm, gla_w2, gla_b2, gla_norm,
         hgrn_lb_logits, hgrn_norm, w_br_a, w_br_b, w_br_c, w_out, ln1_g, ln1_b, w_router, router_bias,
         w_gate, w_up, w_down, ln2_g, ln2_b)]
    cores = list(range(8))
    NLAY = _NLAY[0]
    gb0 = np.empty((128, 16), np.float32)
    gb0[:, 0:8] = ln0_g.reshape(8, 128).T
    gb0[:, 8:16] = ln0_b.reshape(8, 128).T
    gb = np.empty((DEPTH, 128, 32), np.float32)
    for l in range(DEPTH):
        gb[l, :, 0:8] = ln1_g[l].reshape(8, 128).T
        gb[l, :, 8:16] = ln1_b[l].reshape(8, 128).T
        gb[l, :, 16:24] = ln2_g[l].reshape(8, 128).T
        gb[l, :, 24:32] = ln2_b[l].reshape(8, 128).T
    wmg = np.ascontiguousarray(w_in[:, :, -3 * D:])
    wbr = np.ascontiguousarray(np.stack([w_br_a, w_br_b, w_br_c], axis=1))
    rb = np.ascontiguousarray(np.broadcast_to(router_bias[None, :], (128, 16)))
    dummy = np.zeros((D, 8), np.float32)
    ims = []
    for c in cores:
        b, i = c // 4, c % 4
        per = [m_inputs(dummy, l, i, w_in, gdn_conv, gdn_a_log, gdn_dt_bias, gdn_norm, gla_w2, gla_b2, gla_norm,
                        hgrn_lb_logits, hgrn_norm) for l in range(DEPTH)]
        qoh = np.zeros((128, 4), np.float32)
        qoh[:, i] = 1.0
        ims.append({
            "xT": np.ascontiguousarray(x[b, i * NTF:(i + 1) * NTF, :].T), "gb0": gb0,
            "sm": np.stack([p["sm"] for p in per]),
            "nrm": np.stack([p["nrm"] for p in per]), "w2": np.stack([p["w2"] for p in per]),
            "gb": gb, "wr": w_router, "rb": rb, "qoh": qoh,
        })
        for l in range(NLAY):
            ims[-1].update({"wM%d" % l: per[l]["w"], "wmg%d" % l: wmg[l], "wbr%d" % l: wbr[l], "wout%d" % l: w_out[l],
                            "wg%d" % l: w_gate[l], "wu%d" % l: w_up[l], "wd%d" % l: w_down[l]})
    ncf = _prog("fused", build_fused)
    if os.environ.get("KTRACE", "0") == "1":
        res = run_bass_kernel_spmd(ncf, ims, core_ids=cores, trace=True)
        print("EXEC_TIME_NS", res.exec_time_ns)
    else:
        res = run_bass_kernel_spmd(ncf, ims, core_ids=cores)
    hT = [np.concatenate([res.results[b * 4 + q]["hTo"] for q in range(4)], axis=1) for b in range(NBATCH)]
    out = np.stack([np.ascontiguousarray(hT[b].T) for b in range(NBATCH)]).astype(np.float32)
    return out
```

```python
import math
import os
import numpy as np
from contextlib import ExitStack
import concourse.bass as bass
import concourse.mybir as mybir
from concourse.bass_utils import run_bass_kernel_spmd

F32 = mybir.dt.float32
BF16 = mybir.dt.bfloat16
AF = mybir.ActivationFunctionType
ALU = mybir.AluOpType
AX = mybir.AxisListType

D = 1024
KC = 8
SEQ = 8192
NBATCH = 2
DEPTH = 4
ALPHA = (2.0 * DEPTH) ** 0.25
LN_EPS = 1e-5
RMS_EPS = 1e-6
TB = 512
CA_Q, CA_K, CA_V, CA_BR, CA_DR, CA_G, CA_BD = 0, 128, 256, 384, 512, 640, 768
CB0 = 770
CB_Q, CB_K, CB_LR, CB_V, CB_G = CB0, CB0 + 64, CB0 + 128, CB0 + 144, CB0 + 272
CC0 = CB0 + 400
CC_Q, CC_F, CC_I, CC_G = CC0, CC0 + 128, CC0 + 256, CC0 + 384
NCOL = CC0 + 512


class Buf:
    __slots__ = ("name", "w", "r")
    registry = []

    def __init__(self, name):
        self.name = name
        self.w = None
        self.r = []
        Buf.registry.append(self)


class V:
    __slots__ = ("ap", "bufs")

    def __init__(self, ap, bufs):
        self.ap = ap
        self.bufs = bufs

    def __getitem__(self, idx):
        return V(self.ap[idx], self.bufs)

    def re(self, s, **kw):
        return V(self.ap.rearrange(s, **kw), self.bufs)

    def bc(self, shape):
        return V(self.ap.to_broadcast(list(shape)), self.bufs)


class Prog:
    CENG = ("pe", "dve", "act", "pool")

    def __init__(self, nc, es, arena_words=49152):
        self.nc = nc
        self.es = es
        self.q = {e: [] for e in ("pe", "dve", "act", "pool", "sp")}
        self.cnt = {e: 0 for e in self.CENG}
        self.known = {e: {} for e in self.q}
        self.sems = {}
        self.dcnt = {}
        self.epoch = 0
        self.key = {}
        self.waited = {}
        Buf.registry = []
        for e in self.CENG:
            self.key[e] = (e, 0)
            self.sems[self.key[e]] = es.enter_context(nc.semaphore("s_%s0" % e))
        self.n_inst = 0
        self.n_wait = 0
        arena_words = int(nc.sbuf_bytes_remaining) // 4 - 512
        self.arena = es.enter_context(nc.sbuf_tensor("arena", [128, arena_words], F32))
        self.aoff = 0
        self.psn = 0

    def mark(self):
        return self.aoff

    def release(self, m):
        self.aoff = m

    def sb(self, name, shape, dt=F32, nb=1):
        n = 1
        for v in shape[1:]:
            n *= v
        words = n if dt == F32 else (n + 1) // 2
        words = (words + 7) // 8 * 8
        assert self.aoff + words <= self.arena.shape[1], ("SBUF arena overflow", name, self.aoff, words)
        ap = self.arena[0:shape[0], self.aoff:self.aoff + words]
        self.aoff += words
        if dt != F32:
            ap = ap.bitcast(dt)
        ap = ap[:, 0:n]
        if len(shape) == 3:
            ap = ap.rearrange("p (a b) -> p a b", b=shape[2])
        elif len(shape) == 4:
            ap = ap.rearrange("p (a b c) -> p a b c", b=shape[2], c=shape[3])
        return V(ap, [Buf(name)])

    def ps(self, name, shape, dt=F32):
        t = self.es.enter_context(self.nc.psum_tensor("ps_" + name, list(shape), dt))
        return V(t[:], [Buf(name)])

    def barrier(self):
        latest = []
        for e in self.CENG:
            if self.cnt[e] > 0:
                latest.append((self.key[e], self.cnt[e]))
        for kk, v in self.dcnt.items():
            if v > 0:
                latest.append((kk, v))
        for eng in self.q:
            kn = self.known[eng]
            for (kk, v) in latest:
                if kn.get(kk, 0) < v:
                    kn[kk] = v
                    self.q[eng].append(("w", self.sems[kk], v))
                    self.n_wait += 1
        self.epoch += 1
        for e in self.CENG:
            self.key[e] = (e, self.epoch)
            self.sems[self.key[e]] = self.es.enter_context(self.nc.semaphore("s_%s%d" % (e, self.epoch)))
            self.cnt[e] = 0
        for b in Buf.registry:
            b.w = None
            b.r = []

    def _deps(self, eng, reads, writes):
        deps = []
        for b in reads:
            if b.w is not None:
                deps.append(b.w)
        for b in writes:
            if b.w is not None:
                deps.append(b.w)
            deps.extend(b.r)
        kn = self.known[eng]
        need = {}
        for (k, v) in deps:
            if k[0] == "pe" and eng == "pe":
                continue
            if kn.get(k, 0) >= v:
                continue
            if need.get(k, 0) < v:
                need[k] = v
        for k, v in need.items():
            kn[k] = v
            self.q[eng].append(("w", self.sems[k], v))
            self.n_wait += 1
            if k[0] == "d" and self.waited.get(k, 0) < v:
                self.waited[k] = v

    def _observe(self, eng, k):
        v = self.waited.get(k, 0)
        if v > 0 and self.known[eng].get(k, 0) < v:
            self.known[eng][k] = v
            self.q[eng].append(("w", self.sems[k], v))
            self.n_wait += 1

    def _record(self, ev, reads, writes):
        for b in reads:
            if len(b.r) > 24:
                b.r = b.r[-24:]
            b.r.append(ev)
        for b in writes:
            b.w = ev
            b.r = []

    def op(self, eng, fn, reads=(), writes=()):
        self._deps(eng, reads, writes)
        self.cnt[eng] += 1
        ev = (self.key[eng], self.cnt[eng])
        self.q[eng].append(("i", fn, self.sems[self.key[eng]], 1))
        self._record(ev, reads, writes)
        self.n_inst += 1
        return ev

    def dma(self, eng, key, out, in_, reads=(), writes=()):
        k = ("d", key)
        if k not in self.sems:
            self.sems[k] = self.es.enter_context(self.nc.semaphore("d_" + str(key)))
            self.dcnt[k] = 0
        self._deps(eng, reads, writes)
        self._observe(eng, k)
        self.dcnt[k] += 16
        ev = (k, self.dcnt[k])
        self.q[eng].append(("i", lambda e: e.dma_start(out=out, in_=in_), self.sems[k], 16))
        self._record(ev, reads, writes)
        self.n_inst += 1
        return ev

    def coll(self, key, fn, reads=(), writes=()):
        k = ("d", key)
        if k not in self.sems:
            self.sems[k] = self.es.enter_context(self.nc.semaphore("c_" + str(key)))
            self.dcnt[k] = 0
        self._deps("pool", reads, writes)
        self._observe("pool", k)
        self.dcnt[k] += 1
        ev = (k, self.dcnt[k])
        self.q["pool"].append(("i", fn, self.sems[k], 1))
        self._record(ev, reads, writes)
        self.n_inst += 1
        return ev

    def wait_all(self, eng, bufs):
        self._deps(eng, bufs, ())

    def run(self):
        q = self.q

        def replay(e, lst):
            for it in lst:
                if it[0] == "w":
                    e.wait_ge(it[1], it[2])
                else:
                    it[1](e).then_inc(it[2], it[3])

        with self.nc.Block() as block:
            @block.sync
            def _(e):
                replay(e, q["sp"])

            @block.tensor
            def _(e):
                replay(e, q["pe"])

            @block.vector
            def _(e):
                replay(e, q["dve"])

            @block.scalar
            def _(e):
                replay(e, q["act"])

            @block.gpsimd
            def _(e):
                replay(e, q["pool"])


class K:
    def __init__(self, P):
        self.P = P

    @staticmethod
    def _b(*xs):
        out = []
        for x in xs:
            if isinstance(x, V):
                out.extend(x.bufs)
        return out

    @staticmethod
    def _a(x):
        return x.ap if isinstance(x, V) else x

    def mm(self, out, lhsT, rhs, start=True, stop=True):
        self.P.op("pe", lambda e: e.matmul(out.ap, lhsT=lhsT.ap, rhs=rhs.ap, start=start, stop=stop),
                  reads=lhsT.bufs + rhs.bufs, writes=out.bufs)

    def tr(self, out, in_, ident):
        self.P.op("pe", lambda e: e.transpose(out.ap, in_.ap, ident.ap),
                  reads=in_.bufs + ident.bufs, writes=out.bufs)

    def act(self, out, in_, func, bias=0.0, scale=1.0, accum=None):
        a = self._a
        if accum is None:
            fn = lambda e: e.activation(out=out.ap, in_=in_.ap, func=func, bias=a(bias), scale=a(scale))
            wr = out.bufs
        else:
            fn = lambda e: e.activation(out=out.ap, in_=in_.ap, func=func, bias=a(bias), scale=a(scale),
                                        accum_out=accum.ap)
            wr = out.bufs + accum.bufs
        self.P.op("act", fn, reads=self._b(in_, bias, scale), writes=wr)

    def cp(self, eng, out, in_):
        if eng == "act":
            self.P.op("act", lambda e: e.copy(out=out.ap, in_=in_.ap), reads=in_.bufs, writes=out.bufs)
        else:
            self.P.op(eng, lambda e: e.tensor_copy(out=out.ap, in_=in_.ap), reads=in_.bufs, writes=out.bufs)

    def tt(self, eng, out, in0, in1, op):
        self.P.op(eng, lambda e: e.tensor_tensor(out=out.ap, in0=in0.ap, in1=in1.ap, op=op),
                  reads=in0.bufs + in1.bufs, writes=out.bufs)

    def ts(self, eng, out, in0, s1, op0, s2=None, op1=None):
        a = self._a
        if op1 is None:
            fn = lambda e: e.tensor_scalar(out=out.ap, in0=in0.ap, scalar1=a(s1), scalar2=None, op0=op0)
        else:
            fn = lambda e: e.tensor_scalar(out=out.ap, in0=in0.ap, scalar1=a(s1), scalar2=a(s2), op0=op0, op1=op1)
        self.P.op(eng, fn, reads=self._b(in0, s1, s2), writes=out.bufs)

    def stt(self, out, in0, scalar, in1, op0, op1):
        a = self._a
        self.P.op("dve", lambda e: e.scalar_tensor_tensor(out=out.ap, in0=in0.ap, scalar=a(scalar), in1=in1.ap,
                                                          op0=op0, op1=op1),
                  reads=self._b(in0, scalar, in1), writes=out.bufs)

    def scan(self, out, d0, d1):
        self.P.op("dve", lambda e: e.tensor_tensor_scan(out=out.ap, data0=d0.ap, data1=d1.ap, initial=0.0,
                                                        op0=ALU.mult, op1=ALU.add),
                  reads=d0.bufs + d1.bufs, writes=out.bufs)

    def recip(self, out, in_):
        self.P.op("dve", lambda e: e.reciprocal(out=out.ap, in_=in_.ap), reads=in_.bufs, writes=out.bufs)

    def memset(self, eng, out, val):
        self.P.op(eng, lambda e: e.memset(out.ap, val), writes=out.bufs)

    def asel(self, out, pattern, cmp, cm, base=0):
        self.P.op("pool", lambda e: e.affine_select(out=out.ap, in_=out.ap, pattern=pattern, compare_op=cmp,
                                                    fill=0.0, base=base, channel_multiplier=cm),
                  reads=out.bufs, writes=out.bufs)

    def dma(self, eng, key, out, in_):
        rd = in_.bufs if isinstance(in_, V) else []
        wr = out.bufs if isinstance(out, V) else []
        self.P.dma(eng, key, self._a(out), self._a(in_), reads=rd, writes=wr)


def make_consts(P, k):
    c = {}
    c["ident"] = P.sb("ident", [128, 128])
    k.memset("pool", c["ident"], 1.0)
    k.asel(c["ident"], [[-1, 128]], ALU.is_equal, 1)
    c["identb"] = P.sb("identb", [128, 128], BF16)
    k.cp("pool", c["identb"], c["ident"])
    c["onesb"] = P.sb("onesb", [128, 128], BF16)
    k.memset("pool", c["onesb"], 1.0)
    c["onesf"] = P.sb("onesf", [128, 128])
    k.memset("pool", c["onesf"], 1.0)
    for name, cmp in (("mTi", ALU.is_ge), ("mTs", ALU.is_gt)):
        m = P.sb(name, [128, 4, 128])
        k.memset("pool", m, 0.0)
        k.memset("pool", m[0:64, :, 0:64], 1.0)
        k.memset("pool", m[64:128, :, 64:128], 1.0)
        k.asel(m, [[0, 4], [1, 128]], cmp, -1)
        c[name] = m
    m = P.sb("LTs", [128, 128])
    k.memset("pool", m, 0.0)
    k.memset("pool", m[0:64, 0:64], 1.0)
    k.memset("pool", m[64:128, 64:128], 1.0)
    k.asel(m, [[-1, 128]], ALU.is_gt, 1)
    c["LTs"] = m
    cm = P.sb("cmask", [128, TB])
    k.memset("pool", cm, 1.0)
    k.memset("pool", cm.re("p (c t) -> p c t", t=64)[:, :, 0:1], 0.0)
    c["cmask"] = cm
    ec = P.sb("epsc", [128, 4])
    k.memset("pool", ec[:, 0:1], RMS_EPS)
    k.memset("pool", ec[:, 1:2], 1.0)
    k.memset("pool", ec[:, 2:3], LN_EPS)
    k.memset("pool", ec[:, 3:4], 0.0)
    c["eps"] = ec
    bm = P.sb("bm", [128, 2])
    k.memset("pool", bm, 0.0)
    k.memset("pool", bm[0:64, 0:1], 1.0)
    k.memset("pool", bm[64:128, 1:2], 1.0)
    c["bm"] = bm
    return c


def load_weights_bf16(P, k, wdram, wbf, ncol, tag, step=256):
    wv = wdram.rearrange("(kc p) n -> p kc n", p=128)
    for c0 in range(0, ncol, step):
        c1 = min(ncol, c0 + step)
        k.dma("pool", tag, wbf[:, :, c0:c1], wv[:, :, c0:c1])


def build_M(nb=16):
    NT = nb * TB
    nc = bass.Bass("TRN2", target_bir_lowering=False)
    hT_d = nc.dram_tensor("hT", [D, NT], F32, kind="ExternalInput").ap()
    w_d = nc.dram_tensor("w", [D, NCOL], F32, kind="ExternalInput").ap()
    sm_d = nc.dram_tensor("sm", [128, 32], F32, kind="ExternalInput").ap()
    nrm_d = nc.dram_tensor("nrm", [128, 3, 128], F32, kind="ExternalInput").ap()
    w2_d = nc.dram_tensor("w2", [16, 64], F32, kind="ExternalInput").ap()
    o_d = nc.dram_tensor("oT", [3, 128, NT], F32, kind="ExternalOutput").ap()
    with ExitStack() as es:
        P = Prog(nc, es)
        k = K(P)
        banks = [P.ps("bk%d" % i, [128, 512]) for i in range(8)]
        c = make_consts(P, k)
        hv = hT_d.rearrange("(kc p) t -> p kc t", p=128)
        fin = emit_M(P, k, c, banks, nb, lambda bi, hd: k.dma("pool", "h%d" % (bi % 2), hd, hv[:, :, bi * TB:(bi + 1) * TB]),
                     lambda bi: o_d[:, :, bi * TB:(bi + 1) * TB].rearrange("m p t -> p m t"),
                     w_d, sm_d, nrm_d, w2_d)
        P.wait_all("sp", fin)
        P.run()
        print("M program: inst", P.n_inst, "waits", P.n_wait, {e: len(v) for e, v in P.q.items()})
    return nc


class _Stop(Exception):
    pass


def emit_M(P, k, c, banks, nb, h_load, o_dst, w_d, sm_d, nrm_d, w2_d, obufs=None, after_batch=None):
    MSTOP = float(os.environ.get("MSTOP", "99"))

    def stage(n):
        if n > MSTOP:
            raise _Stop()
    ident, identb, onesb, onesf = c["ident"], c["identb"], c["onesb"], c["onesf"]
    mTi, mTs, LTs, cmask, bm = c["mTi"], c["mTs"], c["LTs"], c["cmask"], c["bm"]
    UT = mTi[:, 0, :]
    EPS_R, ONE_C = c["eps"][:, 0:1], c["eps"][:, 1:2]

    pj = banks[0:2]
    pm = banks[2:5]
    pbf = V(banks[5].ap.bitcast(BF16), banks[5].bufs)
    bk6, bk7 = banks[6], banks[7]
    pcA = [bk6[:, 0:128], bk6[:, 128:256]]
    pcB = [bk7[:, 0:128]]
    pcC = [bk7[:, 128:256]]
    pjn = [0]

    pj4 = [banks[0], banks[1], banks[6], banks[7]]

    def nextpj():
        pjn[0] = (pjn[0] + 1) % 4
        return pj4[pjn[0]]

    pmn = [0]

    def nextpm():
        pmn[0] = (pmn[0] + 1) % 3
        return pm[pmn[0]]

    wbf = P.sb("wbf", [128, KC, NCOL], BF16)
    load_weights_bf16(P, k, w_d, wbf, NCOL, "wM")
    sm = P.sb("sm", [128, 32])
    k.dma("sp", "sm", sm, sm_d)
    nrm = P.sb("nrm", [128, 3, 128])
    k.dma("sp", "nrm", nrm, nrm_d)
    w2f = P.sb("w2f", [16, 64])
    k.dma("sp", "w2", w2f, w2_d)
    w2b = P.sb("w2b", [16, 64], BF16)
    k.cp("dve", w2b, w2f)
    prm = P.sb("prm", [128, 16])
    k.act(prm[:, 0:1], sm[:, 12:13], AF.Exp)
    k.ts("dve", prm[:, 0:1], prm[:, 0:1], -1.0, ALU.mult)
    negA = prm[:, 0:1]
    dtb = sm[:, 13:14]
    k.ts("dve", prm[:, 1:2], sm[:, 14:15], -1.0, ALU.mult)
    negb2 = prm[0:64, 1:2]
    lbt = P.sb("lbt", [128, 8])
    k.act(lbt[:, 0:4], sm[:, 16:20], AF.Exp)
    k.P.op("dve", lambda e: e.tensor_reduce(out=lbt.ap[:, 4:5], in_=lbt.ap[:, 0:4], axis=AX.X, op=ALU.add),
           reads=lbt.bufs, writes=lbt.bufs)
    k.recip(lbt[:, 5:6], lbt[:, 4:5])
    k.tt("dve", lbt[:, 0:4], lbt[:, 0:4], sm[:, 20:24], ALU.mult)
    k.P.op("dve", lambda e: e.tensor_reduce(out=lbt.ap[:, 6:7], in_=lbt.ap[:, 0:4], axis=AX.X, op=ALU.add),
           reads=lbt.bufs, writes=lbt.bufs)
    k.tt("dve", prm[:, 2:3], lbt[:, 6:7], lbt[:, 5:6], ALU.mult)
    k.ts("dve", prm[:, 2:3], prm[:, 2:3], 0.0, ALU.max, 1.0, ALU.min)
    lb = prm[:, 2:3]
    k.ts("dve", prm[:, 3:4], lb, -1.0, ALU.mult, 1.0, ALU.add)
    oml = prm[:, 3:4]
    k.ts("dve", prm[:, 4:5], oml, -1.0, ALU.mult)
    noml = prm[:, 4:5]

    hb = [P.sb("hb%d" % i, [128, KC, TB], BF16) for i in range(2)]
    ost = [P.sb("ost%d" % i, [128, 3, TB]) for i in range(2)]

    def W(c0, n):
        return lambda kc: wbf[:, kc, c0:c0 + n]

    def proj_fm(ps, h, wsel, m):
        for kc in range(KC):
            k.mm(ps[0:m, :], wsel(kc), h[:, kc, :], start=(kc == 0), stop=(kc == KC - 1))

    def proj_tm(ps, h, p, wsel):
        for kc in range(KC):
            k.mm(ps, h[:, kc, p * 128:(p + 1) * 128], wsel(kc), start=(kc == 0), stop=(kc == KC - 1))

    pre = [P.sb("pre%d" % i, [128, TB + 3]) for i in range(3)]
    for t in pre:
        k.memset("pool", t[:, 0:3], 0.0)
    cv = [P.sb("cv%d" % i, [128, TB]) for i in range(3)]
    sqb = P.sb("sqb", [128, TB], BF16)
    rn = P.sb("rn", [128, TB])
    qTb = P.sb("qTb", [128, TB], BF16)
    qTf = P.sb("qTf", [128, TB])
    kTf = P.sb("kTf", [128, TB])
    kTb = P.sb("kTb", [128, TB], BF16)
    kbTb = P.sb("kbTb", [128, TB], BF16)
    betaB = P.sb("betaB", [128, TB])
    gB = P.sb("gB", [128, TB])
    cumB = P.sb("cumB", [128, TB])
    ecumB = P.sb("ecumB", [128, TB])
    tok = P.sb("tok", [128, 64])
    gsel = P.sb("gsel", [128, 4, 2])
    lastA = P.sb("lastA", [128, 8])
    kb = P.sb("kb", [128, 4, 128], BF16)
    ksA = P.sb("ksA", [128, 4, 128], BF16)
    vb = P.sb("vb", [128, 4, 128], BF16)
    decT = P.sb("decT", [128, 4, 128])
    dmi = P.sb("dmi", [128, 4, 128])
    dms = P.sb("dms", [128, 4, 128])
    attA = P.sb("attA", [128, 4, 128], BF16)
    Xb = [P.sb("Xb%d" % i, [128, 4, 128], BF16) for i in range(2)]
    Yb = [P.sb("Yb%d" % i, [128, 4, 128], BF16) for i in range(2)]
    Rb = P.sb("Rb", [128, 4, 128], BF16)
    uf = P.sb("uf", [128, 4, 128])
    wTE = P.sb("wTE", [128, 4, 128])
    wTO = P.sb("wTO", [128, 4, 128])
    qEA = P.sb("qEA", [128, 4, 128])
    qOA = P.sb("qOA", [128, 4, 128])
    for t in (wTE, wTO, qEA, qOA):
        k.memset("pool", t, 0.0)
    vnew = P.sb("vnew", [128, 128], BF16)
    SA = [P.sb("SA%d" % i, [128, 128]) for i in range(3)]
    k.memset("pool", SA[0], 0.0)
    sA = [0]

    def dd_tiles(tag, dk):
        d = {}
        for nme in ("q", "kk", "c", "e", "dm", "ksf"):
            d[nme] = scr[nme][0:dk, :]
        d["qm"] = P.sb(tag + "qm", [dk, TB], BF16)
        d["kd"] = P.sb(tag + "kd", [dk, TB], BF16)
        d["qE"] = P.sb(tag + "qE", [dk, 4, 128])
        d["qO"] = P.sb(tag + "qO", [dk, 4, 128])
        k.memset("pool", d["qE"], 0.0)
        k.memset("pool", d["qO"], 0.0)
        d["ks"] = P.sb(tag + "ks", [128, 4, dk], BF16)
        d["v"] = P.sb(tag + "v", [128, 4, 128], BF16)
        d["att"] = P.sb(tag + "att", [128, 4, 128], BF16)
        d["dl"] = P.sb(tag + "dl", [dk, 8])
        d["S"] = [P.sb(tag + "S%d" % i, [dk, 128]) for i in range(3)]
        k.memset("pool", d["S"][0], 0.0)
        d["si"] = 0
        d["dk"] = dk
        return d

    scr = {nme: P.sb("scr" + nme, [128, TB]) for nme in ("q", "kk", "c", "e", "dm", "ksf")}
    tB = dd_tiles("B", 64)
    tC = dd_tiles("C", 128)
    lrb = P.sb("lrb", [16, TB], BF16)
    sgC = P.sb("sgC", [128, TB])
    esc = [{n_: P.sb("%s%d" % (n_, mi_), [128, 128] if n_ != "ssq" else [128, 4]) for n_ in ("junk", "ssq", "sgt", "gw", "onf")}
           for mi_ in range(3)]
    ebank = [pj[0], pj[1], banks[5]]
    obank = [pm[0], pm[1], pm[2]]

    gwS = P.sb("gwS", [128, 12, 128])
    sgG = P.sb("sgG", [128, 128])

    def gen_G(bi, h):
        for mi, gcol in ((0, CA_G), (1, CB_G), (2, CC_G)):
            for p in range(4):
                gp = nextpj()
                proj_tm(gp[:, 0:128], h, p, W(gcol, 128))
                k.act(sgG, gp[:, 0:128], AF.Silu)
                k.tt("pool", gwS[:, mi * 4 + p, :], sgG, nrm[:, mi, :], ALU.mult)
                yield

    def epilogue(mi, o_ps, h, p, gcol, ostg):
        e_ = esc[mi]
        junk, ssq, onf = e_["junk"], e_["ssq"], e_["onf"]
        k.memset("pool", ssq[:, 0:1], 0.0)
        k.act(junk, o_ps, AF.Square, accum=ssq[:, 0:1])
        yield
        k.act(ssq[:, 1:2], ssq[:, 0:1], AF.Sqrt, bias=EPS_R, scale=1.0 / 128.0)
        k.recip(ssq[:, 2:3], ssq[:, 1:2])
        yield
        k.stt(onf, o_ps, ssq[:, 2:3], gwS[:, mi * 4 + p, :], ALU.mult, ALU.mult)
        yield
        tp = ebank[mi]
        k.tr(tp[:, 0:128], onf, ident)
        k.cp("act", ostg[:, mi, p * 128:(p + 1) * 128], tp[:, 0:128])
        yield

    def dd_prep(t, sc):
        dk = t["dk"]
        c3 = t["c"].re("p (c t) -> p c t", t=64)
        k.act(t["e"], t["c"], AF.Exp, scale=sc)
        e4 = t["e"].re("p (a b t) -> p a b t", b=2, t=64)
        q4 = t["q"].re("p (a b t) -> p a b t", b=2, t=64)
        k.tt("dve", t["qE"][:, :, 0:64], q4[:, :, 0, :], e4[:, :, 0, :], ALU.mult)
        k.tt("dve", t["qO"][:, :, 64:128], q4[:, :, 1, :], e4[:, :, 1, :], ALU.mult)
        yield
        k.act(t["dl"], c3[:, :, 63], AF.Exp, scale=sc)
        dm3 = t["dm"].re("p (c t) -> p c t", t=64)
        k.tt("dve", dm3, c3, c3[:, :, 31:32].bc([dk, 8, 64]), ALU.subtract)
        k.act(t["e"], t["dm"], AF.Exp, scale=sc)
        k.tt("dve", t["qm"], t["q"], t["e"], ALU.mult)
        yield
        k.act(t["e"], t["dm"], AF.Exp, scale=-sc)
        k.tt("dve", t["kd"], t["kk"], t["e"], ALU.mult)
        yield
        k.tt("dve", dm3, c3[:, :, 63:64].bc([dk, 8, 64]), c3, ALU.subtract)
        k.act(t["e"], t["dm"], AF.Exp, scale=sc)
        k.tt("dve", t["ksf"], t["kk"], t["e"], ALU.mult)
        yield
        tp = nextpm()
        for p in range(4):
            k.tr(tp[:, p * 128:p * 128 + dk], t["ksf"][:, p * 128:(p + 1) * 128], ident[0:dk, 0:dk])
        k.cp("dve", t["ks"], tp.re("p (a b) -> p a b", b=128)[:, :, 0:dk])
        yield
        ap_ = nextpm()
        for p in range(4):
            k.mm(ap_[:, p * 128:(p + 1) * 128], t["kd"][:, p * 128:(p + 1) * 128], t["qm"][:, p * 128:(p + 1) * 128])
        k.tt("dve", t["att"], ap_.re("p (a b) -> p a b", b=128), mTi, ALU.mult)
        yield

    def dd_chain(t, pc, mi, h, gcol, ostg):
        dk = t["dk"]
        S = t["S"]
        for p in range(4):
            s0, s1, s2 = t["si"], (t["si"] + 1) % 3, (t["si"] + 2) % 3
            ds = pc[0]
            k.mm(ds[0:dk, :], t["ks"][0:64, p, :], t["v"][0:64, p, :])
            k.stt(S[s1], S[s0], t["dl"][:, 2 * p:2 * p + 1], ds[0:dk, :], ALU.mult, ALU.add)
            yield
            o = obank[mi][:, 0:128]
            k.mm(o, t["att"][:, p, :], t["v"][:, p, :], start=True, stop=False)
            k.mm(o, t["qE"][:, p, :], S[s0], start=False, stop=False)
            k.mm(o, t["qO"][:, p, :], S[s1], start=False, stop=True)
            k.mm(ds[0:dk, :], t["ks"][64:128, p, :], t["v"][64:128, p, :])
            k.stt(S[s2], S[s1], t["dl"][:, 2 * p + 1:2 * p + 2], ds[0:dk, :], ALU.mult, ALU.add)
            t["si"] = s2
            yield
            yield from epilogue(mi, o, h, p, gcol, ostg)

    def gen_A(bi, h):
        for i, c0 in enumerate((CA_Q, CA_K, CA_V)):
            ps = nextpj()
            proj_fm(ps, h, W(c0, 128), 128)
            if bi > 0:
                k.cp("pool", pre[i][:, 0:3], pre[i][:, TB:TB + 3])
            k.cp("act", pre[i][:, 3:TB + 3], ps)
            k.ts("dve", cv[i], pre[i][:, 0:TB], sm[:, 4 * i:4 * i + 1], ALU.mult)
            for j in range(1, 4):
                k.stt(cv[i], pre[i][:, j:TB + j], sm[:, 4 * i + j:4 * i + j + 1], cv[i], ALU.mult, ALU.add)
            k.act(cv[i], cv[i], AF.Silu)
            yield
        for i in range(2):
            k.act(sqb, cv[i], AF.Square)
            ps = nextpj()
            k.mm(ps, onesb, sqb)
            k.act(rn, ps, AF.Sqrt, bias=EPS_R)
            k.recip(rn, rn)
            if i == 0:
                k.stt(qTf, cv[0], 128.0 ** -0.5, rn, ALU.mult, ALU.mult)
                k.cp("dve", qTb, qTf)
            else:
                k.tt("dve", kTf, cv[1], rn, ALU.mult)
                k.cp("dve", kTb, kTf)
            yield
        yield
        ps = nextpj()
        proj_fm(ps, h, W(CA_BR, 128), 128)
        k.act(betaB, ps, AF.Sigmoid)
        k.tt("dve", kbTb, kTf, betaB, ALU.mult)
        yield
        ps = nextpj()
        proj_fm(ps, h, W(CA_DR, 128), 128)
        k.act(gB, ps, AF.Exp, bias=dtb)
        k.act(gB, gB, AF.Ln, bias=ONE_C)
        k.ts("dve", gB, gB, negA, ALU.mult)
        k.scan(cumB, cmask, gB)
        k.act(ecumB, cumB, AF.Exp)
        yield
        e4 = ecumB.re("p (a b t) -> p a b t", b=2, t=64)
        q4 = qTf.re("p (a b t) -> p a b t", b=2, t=64)
        k.tt("dve", qEA[:, :, 0:64], q4[:, :, 0, :], e4[:, :, 0, :], ALU.mult)
        k.tt("dve", qOA[:, :, 64:128], q4[:, :, 1, :], e4[:, :, 1, :], ALU.mult)
        yield
        ptok = nextpj()[:, 0:128]
        for p in range(4):
            proj_tm(ptok[:, 16 * p:16 * p + 16], h, p, W(CA_BD - 14, 16))
        pt2 = ptok[:, 0:64].re("p (a b) -> p a b", b=16)
        k.act(tok[:, 0:4], pt2[:, :, 14], AF.Sigmoid)
        k.act(tok[:, 4:8], pt2[:, :, 15], AF.Exp, bias=dtb)
        k.act(tok[:, 4:8], tok[:, 4:8], AF.Ln, bias=ONE_C)
        k.ts("dve", tok[:, 4:8], tok[:, 4:8], negA, ALU.mult)
        k.mm(ptok[:, 64:68], UT, tok[:, 4:8])
        k.mm(ptok[:, 68:72], LTs, tok[:, 4:8])
        k.ts("dve", gsel[:, :, 0], tok[:, 4:8], bm[:, 0:1], ALU.mult)
        k.ts("dve", gsel[:, :, 1], tok[:, 4:8], bm[:, 1:2], ALU.mult)
        k.mm(ptok[:, 72:80], onesf, gsel.re("p a b -> p (a b)"))
        k.act(lastA, ptok[:, 72:80], AF.Exp)
        k.cp("dve", tok[:, 8:12], ptok[:, 64:68])
        k.act(tok[:, 16:20], ptok[:, 64:68], AF.Exp)
        k.tt("dve", tok[:, 16:20], tok[:, 16:20], tok[:, 0:4], ALU.mult)
        k.act(tok[:, 20:24], ptok[:, 68:72], AF.Exp)
        yield
        tpk = nextpm()
        for p in range(4):
            k.tr(tpk[:, p * 128:(p + 1) * 128], kTf[:, p * 128:(p + 1) * 128], ident)
        tpv = nextpm()
        for p in range(4):
            k.tr(tpv[:, p * 128:(p + 1) * 128], cv[2][:, p * 128:(p + 1) * 128], ident)
        for p in range(4):
            k.ts("dve", kb[:, p, :], tpk[:, p * 128:(p + 1) * 128], tok[:, 16 + p:17 + p], ALU.mult)
            k.ts("dve", ksA[:, p, :], tpk[:, p * 128:(p + 1) * 128], tok[:, 20 + p:21 + p], ALU.mult)
            k.ts("dve", vb[:, p, :], tpv[:, p * 128:(p + 1) * 128], tok[:, p:p + 1], ALU.mult)
        yield
        for p in range(4):
            k.ts("dve", decT[:, p, :], cumB[:, p * 128:(p + 1) * 128], tok[:, 8 + p:9 + p], ALU.subtract,
                 0.0, ALU.min)
        k.act(decT, decT, AF.Exp)
        yield
        k.tt("pool", dmi, decT, mTi, ALU.mult)
        k.tt("pool", dms, decT, mTs, ALU.mult)
        yield
        pq = nextpm()
        for p in range(4):
            sl = slice(p * 128, (p + 1) * 128)
            k.mm(pq[:, sl], kTb[:, sl], qTb[:, sl])
        k.tt("dve", attA, pq.re("p (a b) -> p a b", b=128), dmi, ALU.mult)
        yield
        pk = nextpm()
        for p in range(4):
            sl = slice(p * 128, (p + 1) * 128)
            k.mm(pk[:, sl], kTb[:, sl], kbTb[:, sl])
        X, Y = Xb[0], Yb[0]
        k.tt("dve", X, pk.re("p (a b) -> p a b", b=128), dms, ALU.mult)
        yield
        pb3 = pbf[:, 0:512].re("p (a b) -> p a b", b=128)
        for p in range(4):
            k.tr(pb3[:, p, :], X[:, p, :], identb)
        k.cp("act", Y, pb3)
        yield
        k.tt("pool", Rb, identb.re("p (a b) -> p a b", a=1).bc([128, 4, 128]), X, ALU.subtract)
        yield
        xi = 0
        for lvl in range(int(os.environ.get('NLVL', '5'))):
            Xn, Yn = Xb[xi ^ 1], Yb[xi ^ 1]
            yield
            py = nextpm()
            for p in range(4):
                k.mm(py[:, p * 128:(p + 1) * 128], X[:, p, :], Y[:, p, :])
            k.cp("act", Yn, py.re("p (a b) -> p a b", b=128))
            yield
            if lvl < 4:
                px = nextpm()
                for p in range(4):
                    k.mm(px[:, p * 128:(p + 1) * 128], Y[:, p, :], X[:, p, :])
                k.cp("dve", Xn, px.re("p (a b) -> p a b", b=128))
            yield
            pr = nextpm()
            for p in range(4):
                k.mm(pr[:, p * 128:(p + 1) * 128], Yn[:, p, :], Rb[:, p, :])
            k.tt("dve", Rb, pr.re("p (a b) -> p a b", b=128), Rb, ALU.add)
            X, Y = Xn, Yn
            xi ^= 1
        yield
        pu = nextpm()
        for p in range(4):
            k.mm(pu[:, p * 128:(p + 1) * 128], Rb[:, p, :], vb[:, p, :])
        k.cp("act", uf, pu.re("p (a b) -> p a b", b=128))
        yield
        pw = nextpm()
        for p in range(4):
            k.mm(pw[:, p * 128:(p + 1) * 128], kb[:, p, :], Rb[:, p, :])
        pw3 = pw.re("p (a b) -> p a b", b=128)
        k.cp("dve", wTE[:, :, 0:64], pw3[:, :, 0:64])
        k.cp("dve", wTO[:, :, 64:128], pw3[:, :, 64:128])

        yield

    def gen_BC(bi, h):
        ps = nextpj()
        proj_fm(ps, h, W(CB_Q, 64), 64)
        k.act(tB["q"], ps[0:64, :], AF.Identity, scale=64.0 ** -0.5)
        yield
        ps = nextpj()
        proj_fm(ps, h, W(CB_K, 64), 64)
        k.cp("act", tB["kk"], ps[0:64, :])
        yield
        ps = nextpj()
        proj_fm(ps, h, W(CB_LR, 16), 16)
        k.cp("act", lrb, ps[0:16, :])
        yield
        ps = nextpj()
        k.mm(ps[0:64, :], w2b, lrb)
        k.act(tB["e"], ps[0:64, :], AF.Exp, bias=negb2, scale=-1.0)
        k.act(tB["e"], tB["e"], AF.Ln, bias=ONE_C[0:64, :])
        k.scan(tB["c"], cmask[0:64, :], tB["e"])
        yield
        for p in range(4):
            ps = nextpj()
            proj_tm(ps[:, 0:128], h, p, W(CB_V, 128))
            k.cp("act", tB["v"][:, p, :], ps[:, 0:128])
            yield
        yield from dd_prep(tB, -1.0 / 16.0)

        yield
        ps = nextpj()
        proj_fm(ps, h, W(CC_Q, 128), 128)
        k.act(tC["q"], ps, AF.Silu)
        k.ts("dve", tC["q"], tC["q"], 128.0 ** -0.5, ALU.mult)
        yield
        ps = nextpj()
        proj_fm(ps, h, W(CC_F, 128), 128)
        k.act(sgC, ps, AF.Sigmoid)
        k.ts("dve", tC["kk"], sgC, noml, ALU.mult, oml, ALU.add)
        k.ts("dve", tC["e"], sgC, oml, ALU.mult, lb, ALU.add)
        k.act(tC["e"], tC["e"], AF.Ln)
        k.scan(tC["c"], cmask, tC["e"])
        yield
        for p in range(4):
            ps = nextpj()
            proj_tm(ps[:, 0:128], h, p, W(CC_I, 128))
            k.cp("act", tC["v"][:, p, :], ps[:, 0:128])
            yield
        yield from dd_prep(tC, 1.0)

        yield

    def chain_A(h, ostg):
        for p in range(4):
            s0, s1, s2 = sA[0], (sA[0] + 1) % 3, (sA[0] + 2) % 3
            wS, dS = pcA
            o = obank[0][:, 0:128]
            k.mm(wS, wTE[:, p, :], SA[s0])
            k.tt("dve", vnew[0:64, :], uf[0:64, p, :], wS[0:64, :], ALU.subtract)
            yield
            k.mm(dS, ksA[0:64, p, :], vnew[0:64, :])
            k.stt(SA[s1], SA[s0], lastA[:, 2 * p:2 * p + 1], dS, ALU.mult, ALU.add)
            yield
            k.mm(wS, wTO[:, p, :], SA[s1])
            k.tt("dve", vnew[64:128, :], uf[64:128, p, :], wS[64:128, :], ALU.subtract)
            yield
            k.mm(dS, ksA[64:128, p, :], vnew[64:128, :])
            k.mm(o, attA[:, p, :], vnew, start=True, stop=False)
            k.mm(o, qEA[:, p, :], SA[s0], start=False, stop=False)
            k.mm(o, qOA[:, p, :], SA[s1], start=False, stop=True)
            k.stt(SA[s2], SA[s1], lastA[:, 2 * p + 1:2 * p + 2], dS, ALU.mult, ALU.add)
            sA[0] = s2
            yield
            yield from epilogue(0, o, h, p, CA_G, ostg)

    def interleave(gens):
        gens = list(gens)
        while gens:
            for g in list(gens):
                try:
                    next(g)
                except StopIteration:
                    gens.remove(g)

    def batch_body(bi, t0, hs, h, ostg):
        if bi + 1 < nb:
            h_load(bi + 1, hb[(bi + 1) % 2])
        if os.environ.get("NOILV", "0") == "1":
            for g in (gen_A(bi, h), gen_BC(bi, h), gen_G(bi, h), chain_A(h, ostg), dd_chain(tB, pcB, 1, h, CB_G, ostg),
                      dd_chain(tC, pcC, 2, h, CC_G, ostg)):
                for _ in g:
                    pass
            return
        interleave([gen_A(bi, h), gen_BC(bi, h), gen_G(bi, h)])
        interleave([chain_A(h, ostg), dd_chain(tB, pcB, 1, h, CB_G, ostg), dd_chain(tC, pcC, 2, h, CC_G, ostg)])


    fin = []
    h_load(0, hb[0])
    for bi in range(nb):
        t0 = bi * TB
        hs, h, ostg = None, hb[bi % 2], ost[bi % 2]
        if MSTOP < 99:
            k.memset("pool", ostg, 0.0)
        try:
            batch_body(bi, t0, hs, h, ostg)
        except _Stop:
            pass
        ob = obufs[bi] if obufs is not None else Buf("oT_out%d" % bi)
        P.dma("sp", "o%d" % (bi % 2), o_dst(bi), ostg.ap, reads=ostg.bufs, writes=[ob])
        fin.append(ob)
        if after_batch is not None:
            after_batch(bi)
    return fin


def m_inputs(hT_b, l, hd, w_in, gdn_conv, gdn_a_log, gdn_dt_bias, gdn_norm, gla_w2, gla_b2, gla_norm,
             hgrn_lb_logits, hgrn_norm):
    wl = w_in[l]
    o = 0
    offs = {}
    for name, sz in (("aq", 512), ("ak", 512), ("av", 512), ("ab", 4), ("adt", 4), ("ag", 512),
                     ("bq", 256), ("bk", 256), ("bv", 512), ("blr", 16), ("bg", 512),
                     ("cq", 512), ("cf", 512), ("ci", 512), ("cg", 512)):
        offs[name] = o
        o += sz

    def col(name, width):
        s = offs[name] + hd * width
        return wl[:, s:s + width]

    w = np.empty((D, NCOL), np.float32)
    w[:, CA_Q:CA_Q + 128] = col("aq", 128)
    w[:, CA_K:CA_K + 128] = col("ak", 128)
    w[:, CA_V:CA_V + 128] = col("av", 128)
    w[:, CA_BR:CA_BR + 128] = np.repeat(col("ab", 1), 128, axis=1)
    w[:, CA_DR:CA_DR + 128] = np.repeat(col("adt", 1), 128, axis=1)
    w[:, CA_G:CA_G + 128] = col("ag", 128)
    w[:, CA_BD:CA_BD + 1] = col("ab", 1)
    w[:, CA_BD + 1:CA_BD + 2] = col("adt", 1)
    w[:, CB_Q:CB_Q + 64] = col("bq", 64)
    w[:, CB_K:CB_K + 64] = col("bk", 64)
    w[:, CB_LR:CB_LR + 16] = wl[:, offs["blr"]:offs["blr"] + 16]
    w[:, CB_V:CB_V + 128] = col("bv", 128)
    w[:, CB_G:CB_G + 128] = col("bg", 128)
    w[:, CC_Q:CC_Q + 128] = col("cq", 128)
    w[:, CC_F:CC_F + 128] = col("cf", 128)
    w[:, CC_I:CC_I + 128] = col("ci", 128)
    w[:, CC_G:CC_G + 128] = col("cg", 128)
    sm = np.zeros((128, 32), np.float32)
    cw = gdn_conv[l]
    for i in range(3):
        sm[:, 4 * i:4 * i + 4] = cw[:, i * 512 + hd * 128:i * 512 + (hd + 1) * 128].T
    sm[:, 12] = gdn_a_log[l, hd]
    sm[:, 13] = gdn_dt_bias[l, hd]
    sm[0:64, 14] = gla_b2[l, hd * 64:(hd + 1) * 64]
    sm[:, 16:20] = hgrn_lb_logits[:, hd * 128:(hd + 1) * 128].T
    for j in range(4):
        sm[:, 20 + j] = 1.0 if (1 <= j <= l) else 0.0
    nrm = np.empty((128, 3, 128), np.float32)
    nrm[:, 0, :] = gdn_norm[l][None, :]
    nrm[:, 1, :] = gla_norm[l][None, :]
    nrm[:, 2, :] = hgrn_norm[l][None, :]
    w2 = np.ascontiguousarray(gla_w2[l][:, hd * 64:(hd + 1) * 64])
    return {"hT": np.ascontiguousarray(hT_b), "w": w, "sm": sm, "nrm": nrm, "w2": w2}


TF = 512
NTF = 2048


class Rot:
    def __init__(self, tiles):
        self.t = tiles
        self.i = -1

    def nxt(self):
        self.i = (self.i + 1) % len(self.t)
        return self.t[self.i]


def f_common(P, k, banks, ident, onesf):
    c = {"onesf": onesf, "ident": ident}
    ec = P.sb("epscF", [128, 4])
    k.memset("pool", ec[:, 0:1], LN_EPS)
    c["eps"] = ec[:, 0:1]
    c["banks"] = banks
    sel = P.sb("sel", [16, 16, 128])
    k.memset("pool", sel, 1.0)
    k.asel(sel, [[-1, 16], [0, 128]], ALU.is_equal, 1)
    c["sel"] = sel
    return c


def f_work(P, c):
    c["ps"] = Rot(c["banks"])
    c["ln"] = [P.sb("lnw%d" % i, [128, TF]) for i in range(4)]
    c["zsq"] = P.sb("zsq", [128, TF])


def emit_ln(P, k, c, z, g, b, out32, outb=None):
    onesf = c["onesf"]
    mean, msq, var, rstd = c["ln"]
    zsq = c["zsq"]
    p1 = c["ps"].nxt()
    for kc in range(KC):
        k.mm(p1, onesf, z[:, kc, :], start=(kc == 0), stop=(kc == KC - 1))
    k.act(mean, p1, AF.Identity, scale=1.0 / D)
    p2 = c["ps"].nxt()
    for kc in range(KC):
        k.act(zsq, z[:, kc, :], AF.Square)
        k.mm(p2, onesf, zsq, start=(kc == 0), stop=(kc == KC - 1))
    k.tt("dve", msq, mean, mean, ALU.mult)
    k.stt(var, p2, 1.0 / D, msq, ALU.mult, ALU.subtract)
    k.act(var, var, AF.Sqrt, bias=c["eps"])
    k.recip(rstd, var)
    for kc in range(KC):
        k.tt("dve", zsq, z[:, kc, :], mean, ALU.subtract)
        k.tt("dve", zsq, zsq, rstd, ALU.mult)
        k.ts("dve", out32[:, kc, :], zsq, g[:, kc:kc + 1], ALU.mult, b[:, kc:kc + 1], ALU.add)
        if outb is not None:
            k.cp("act", outb[:, kc, :], out32[:, kc, :])


def emit_L0(P, k, c, ntf, x_src, gb_d, out_dst, obuf, after_tile=None):
    f_work(P, c)
    gb = P.sb("gb0", [128, 16])
    k.dma("sp", "gb", gb, gb_d)
    zt = [P.sb("z%d" % i, [128, KC, TF]) for i in range(2)]
    for ti in range(ntf // TF):
        z = zt[ti % 2]
        k.dma("sp", "z%d" % (ti % 2), z, x_src(ti))
        emit_ln(P, k, c, z, gb[:, 0:8], gb[:, 8:16], z)
        P.dma("sp", "hTo", out_dst(ti), z.ap, reads=z.bufs, writes=[obuf(ti)])
        if after_tile is not None:
            after_tile(ti)


def emit_F(P, k, c, ntf, h_src, o_load, wmg_d, wbr_d, wout_d, gb_d, wr_d, rb_d, wg_d, wu_d, wd_d, out_dst, obuf,
           after_tile=None):
    f_work(P, c)
    PS = c["ps"]
    sel = c["sel"]
    gb = P.sb("gb", [128, 32])
    k.dma("sp", "gb", gb, gb_d)
    wr = P.sb("wr", [128, KC, 16])
    k.dma("sp", "wr", wr, wr_d.rearrange("(kc p) n -> p kc n", p=128))
    rb = P.sb("rb", [128, 16])
    k.dma("sp", "rb", rb, rb_d)
    stg = Rot([P.sb("stg%d" % i, [128, 2048]) for i in range(2)])
    NWB, DEPTH_PF = 12, int(os.environ.get("DPF", "8"))
    wbt = [P.sb("wbt%d" % i, [128, 2048], BF16) for i in range(NWB)]
    ceng = None
    reqs = []
    for ti in range(ntf // TF):
        for n in range(KC):
            for x in range(3):
                reqs.append((wmg_d.rearrange("(kc p) n -> p kc n", p=128)[:, :, x * D + n * 128:x * D + (n + 1) * 128], KC, 128))
                reqs.append((wbr_d[x].rearrange("(hc p) n -> p hc n", p=128)[:, :, n * 128:(n + 1) * 128], 4, 128))
        for n in range(KC):
            reqs.append((wout_d.rearrange("(kc p) n -> p kc n", p=128)[:, :, n * 128:(n + 1) * 128], KC, 128))
        for e in range(16):
            reqs.append((wg_d[e].rearrange("(kc p) f -> p kc f", p=128), KC, 256))
            reqs.append((wu_d[e].rearrange("(kc p) f -> p kc f", p=128), KC, 256))
            reqs.append((wd_d[e].rearrange("(fc p) d -> p fc d", p=128), 2, D))
    st = {"issued": 0, "taken": 0}

    def issue_one():
        i = st["issued"]
        dram3, a, b = reqs[i]
        w = wbt[i % NWB]
        k.dma("pool", "wb%d" % (i % NWB), w[:, 0:a * b].re("p (a b) -> p a b", b=b), dram3)
        st["issued"] += 1

    def stream(dram3, a, b):
        i = st["taken"]
        assert reqs[i][1] == a and reqs[i][2] == b
        while st["issued"] < len(reqs) and st["issued"] < i + 1 + DEPTH_PF:
            issue_one()
        st["taken"] += 1
        return wbt[i % NWB][:, 0:a * b].re("p (a b) -> p a b", b=b)

    hs = P.sb("hs", [128, KC, TF])
    hb = P.sb("hbF", [128, KC, TF], BF16)
    ob = P.sb("ob", [128, 3, 4, TF], BF16)
    yc = P.sb("yc", [128, TF])
    t1 = P.sb("t1", [128, TF])
    sg = P.sb("sg", [128, TF])
    yTb = P.sb("yTb", [128, KC, TF], BF16)
    z = P.sb("z", [128, KC, TF])
    h1b = P.sb("h1b", [128, KC, TF], BF16)
    yacc = P.sb("yacc", [128, KC, TF])
    aT = [P.sb("aT%d" % i, [128, TF], BF16) for i in range(2)]
    rt = P.sb("rt", [128, 4, 96])
    rtA = P.sb("rtA", [128, 256])
    rtB = P.sb("rtB", [128, 72])
    combT = P.sb("combT", [16, TF])
    for ti in range(ntf // TF):
        k.dma("sp", "hs", hs, h_src(ti))
        k.dma("pool", "hbF", hb, h_src(ti))
        o_load(ti, ob, stg, ceng)
        for n in range(KC):
            for x in range(3):
                wm = stream(wmg_d.rearrange("(kc p) n -> p kc n", p=128)[:, :, x * D + n * 128:x * D + (n + 1) * 128], KC, 128)
                pm_ = PS.nxt()
                for kc in range(KC):
                    k.mm(pm_, wm[:, kc, :], hb[:, kc, :], start=(kc == 0), stop=(kc == KC - 1))
                k.act(sg, pm_, AF.Sigmoid)
                wb = stream(wbr_d[x].rearrange("(hc p) n -> p hc n", p=128)[:, :, n * 128:(n + 1) * 128], 4, 128)
                pb_ = PS.nxt()
                for hc in range(4):
                    k.mm(pb_, wb[:, hc, :], ob[:, x, hc, :], start=(hc == 0), stop=(hc == 3))
                if x == 0:
                    k.tt("dve", yc, pb_, sg, ALU.mult)
                else:
                    k.tt("dve", t1, pb_, sg, ALU.mult)
                    k.tt("dve", yc, yc, t1, ALU.add)
            k.cp("act", yTb[:, n, :], yc)
        for n in range(KC):
            wo = stream(wout_d.rearrange("(kc p) n -> p kc n", p=128)[:, :, n * 128:(n + 1) * 128], KC, 128)
            pm_ = PS.nxt()
            for kc in range(KC):
                k.mm(pm_, wo[:, kc, :], yTb[:, kc, :], start=(kc == 0), stop=(kc == KC - 1))
            k.stt(z[:, n, :], hs[:, n, :], ALPHA, pm_, ALU.mult, ALU.add)
        emit_ln(P, k, c, z, gb[:, 0:8], gb[:, 8:16], z, h1b)
        pr_ = PS.nxt()
        for blk in range(4):
            for kc in range(KC):
                k.mm(pr_[:, blk * 16:(blk + 1) * 16], z[:, kc, blk * 128:(blk + 1) * 128], wr[:, kc, :],
                     start=(kc == 0), stop=(kc == KC - 1))
        pt_ = PS.nxt()
        sc = rtA[:, 0:64]
        se = rtA[:, 64:128]
        tm = rtA[:, 128:192]
        mk = rtA[:, 192:256]
        m1, m2, gs, gmask = rtB[:, 0:16], rtB[:, 16:32], rtB[:, 32:48], rtB[:, 48:64]
        gmax, den = rtB[:, 64:68], rtB[:, 68:72]
        g3 = lambda v: v.re("p (a j) -> p a j", j=4)
        b3 = lambda v: v.re("p (b e) -> p b e", e=16)
        u1 = lambda v: v.re("p (a o) -> p a o", o=1)
        k.act(sc, pr_[:, 0:64], AF.Sigmoid)
        k.tt("dve", b3(se), b3(sc), rb.re("p (o e) -> p o e", o=1).bc([128, 4, 16]), ALU.add)
        k.P.op("dve", lambda e, o=m1.ap, i=g3(se).ap: e.tensor_reduce(out=o, in_=i, axis=AX.X, op=ALU.max),
               reads=rtA.bufs, writes=rtB.bufs)
        k.tt("dve", g3(tm), g3(se), u1(m1).bc([128, 16, 4]), ALU.is_equal)
        k.stt(tm, tm, -1.0e9, se, ALU.mult, ALU.add)
        k.P.op("dve", lambda e, o=m2.ap, i=g3(tm).ap: e.tensor_reduce(out=o, in_=i, axis=AX.X, op=ALU.max),
               reads=rtA.bufs, writes=rtB.bufs)
        k.tt("dve", gs, m1, m2, ALU.add)
        k.P.op("dve", lambda e, o=gmax.ap, i=g3(gs).ap: e.tensor_reduce(out=o, in_=i, axis=AX.X, op=ALU.max),
               reads=rtB.bufs, writes=rtB.bufs)
        k.tt("dve", g3(gmask), g3(gs), u1(gmax).bc([128, 4, 4]), ALU.is_equal)
        k.tt("dve", g3(mk), g3(se), u1(m2).bc([128, 16, 4]), ALU.is_ge)
        k.tt("dve", g3(mk), g3(mk), u1(gmask).bc([128, 16, 4]), ALU.mult)
        k.tt("dve", mk, mk, sc, ALU.mult)
        k.P.op("dve", lambda e, o=den.ap, i=b3(mk).ap: e.tensor_reduce(out=o, in_=i, axis=AX.X, op=ALU.add),
               reads=rtA.bufs, writes=rtB.bufs)
        k.recip(den, den)
        k.tt("dve", b3(mk), b3(mk), u1(den).bc([128, 4, 16]), ALU.mult)
        for blk in range(4):
            k.tr(pt_[0:16, blk * 128:(blk + 1) * 128], mk[:, blk * 16:(blk + 1) * 16], c["ident"])
        k.cp("dve", combT, pt_[0:16, :])
        for e in range(16):
            wg = stream(wg_d[e].rearrange("(kc p) f -> p kc f", p=128), KC, 256)
            wu = stream(wu_d[e].rearrange("(kc p) f -> p kc f", p=128), KC, 256)
            wd = stream(wd_d[e].rearrange("(fc p) d -> p fc d", p=128), 2, D)
            pc_ = PS.nxt()
            k.mm(pc_, sel[:, e, :], combT)
            for f in range(2):
                pg_ = PS.nxt()
                for kc in range(KC):
                    k.mm(pg_, wg[:, kc, f * 128:(f + 1) * 128], h1b[:, kc, :], start=(kc == 0), stop=(kc == KC - 1))
                pu_ = PS.nxt()
                for kc in range(KC):
                    k.mm(pu_, wu[:, kc, f * 128:(f + 1) * 128], h1b[:, kc, :], start=(kc == 0), stop=(kc == KC - 1))
                k.act(sg, pg_, AF.Silu)
                k.tt("dve", t1, pu_, sg, ALU.mult)
                k.tt("dve", aT[f], pc_, t1, ALU.mult)
            for dc in range(KC):
                py_ = PS.nxt()
                for f in range(2):
                    k.mm(py_, wd[:, f, dc * 128:(dc + 1) * 128], aT[f], start=(f == 0), stop=(f == 1))
                if e == 0:
                    k.cp("dve", yacc[:, dc, :], py_)
                else:
                    k.tt("dve", yacc[:, dc, :], py_, yacc[:, dc, :], ALU.add)
        for n in range(KC):
            k.stt(yacc[:, n, :], z[:, n, :], ALPHA, yacc[:, n, :], ALU.mult, ALU.add)
        emit_ln(P, k, c, yacc, gb[:, 16:24], gb[:, 24:32], yacc)
        P.dma("sp", "hTo", out_dst(ti), yacc.ap, reads=yacc.bufs, writes=[obuf(ti)])
        if after_tile is not None:
            after_tile(ti)


def build_F(NTF=2048):
    nc = bass.Bass("TRN2", target_bir_lowering=False)
    hT_d = nc.dram_tensor("hT", [D, NTF], F32, kind="ExternalInput").ap()
    oT_d = nc.dram_tensor("oT", [3, 512, NTF], F32, kind="ExternalInput").ap()
    wmg_d = nc.dram_tensor("wmg", [D, 3 * D], F32, kind="ExternalInput").ap()
    wbr_d = nc.dram_tensor("wbr", [3, 512, D], F32, kind="ExternalInput").ap()
    wout_d = nc.dram_tensor("wout", [D, D], F32, kind="ExternalInput").ap()
    gb_d = nc.dram_tensor("gb", [128, 32], F32, kind="ExternalInput").ap()
    wr_d = nc.dram_tensor("wr", [D, 16], F32, kind="ExternalInput").ap()
    rb_d = nc.dram_tensor("rb", [128, 16], F32, kind="ExternalInput").ap()
    wg_d = nc.dram_tensor("wg", [16, D, 256], F32, kind="ExternalInput").ap()
    wu_d = nc.dram_tensor("wu", [16, D, 256], F32, kind="ExternalInput").ap()
    wd_d = nc.dram_tensor("wd", [16, 256, D], F32, kind="ExternalInput").ap()
    o_d = nc.dram_tensor("hTo", [D, NTF], F32, kind="ExternalOutput").ap()
    with ExitStack() as es:
        P = Prog(nc, es)
        k = K(P)
        banks = [P.ps("bk%d" % i, [128, 512]) for i in range(8)]
        cm = make_consts(P, k)
        c = f_common(P, k, banks, cm["ident"], cm["onesf"])
        hv = hT_d.rearrange("(kc p) t -> p kc t", p=128)
        ov = o_d.rearrange("(kc p) t -> p kc t", p=128)

        def o_load(ti, ob, stg, ceng):
            for x in range(3):
                s_ = stg.nxt()
                k.dma("sp", "stg%d" % stg.i, s_.re("p (a b) -> p a b", b=TF),
                      oT_d[x].rearrange("(hc p) t -> p hc t", p=128)[:, :, ti * TF:(ti + 1) * TF])
                k.cp("dve", ob[:, x, :, :].re("p a b -> p (a b)"), s_)

        obuf = Buf("hTo")
        emit_F(P, k, c, NTF, lambda ti: hv[:, :, ti * TF:(ti + 1) * TF], o_load, wmg_d, wbr_d, wout_d, gb_d, wr_d, rb_d,
               wg_d, wu_d, wd_d, lambda ti: ov[:, :, ti * TF:(ti + 1) * TF], lambda ti: obuf)
        P.wait_all("sp", [obuf])
        P.run()
        print("F program: inst", P.n_inst, "waits", P.n_wait, {e: len(v) for e, v in P.q.items()})
    return nc


def f_inputs(hT_c, oT_c, l, w_in, w_br_a, w_br_b, w_br_c, w_out, ln1_g, ln1_b, w_router, router_bias,
             w_gate, w_up, w_down, ln2_g, ln2_b):
    gb = np.empty((128, 32), np.float32)
    gb[:, 0:8] = ln1_g[l].reshape(8, 128).T
    gb[:, 8:16] = ln1_b[l].reshape(8, 128).T
    gb[:, 16:24] = ln2_g[l].reshape(8, 128).T
    gb[:, 24:32] = ln2_b[l].reshape(8, 128).T
    return {"hT": np.ascontiguousarray(hT_c), "oT": np.ascontiguousarray(oT_c),
            "wmg": np.ascontiguousarray(w_in[l][:, -3 * D:]),
            "wbr": np.stack([w_br_a[l], w_br_b[l], w_br_c[l]]), "wout": w_out[l], "gb": gb,
            "wr": w_router, "rb": np.ascontiguousarray(np.broadcast_to(router_bias[None, :], (128, 16))),
            "wg": w_gate[l], "wu": w_up[l], "wd": w_down[l]}


RG = [[0, 1, 2, 3], [4, 5, 6, 7]]


def build_fused(nlay=DEPTH):
    nc = bass.Bass("TRN2", target_bir_lowering=False)
    dt = lambda name, shape: nc.dram_tensor(name, shape, F32, kind="ExternalInput").ap()
    xT_d = dt("xT", [D, NTF])
    gb0_d = dt("gb0", [128, 16])
    wM_d = [dt("wM%d" % l, [D, NCOL]) for l in range(nlay)]
    sm_d = dt("sm", [DEPTH, 128, 32])
    nrm_d = dt("nrm", [DEPTH, 128, 3, 128])
    w2_d = dt("w2", [DEPTH, 16, 64])
    wmg_d = [dt("wmg%d" % l, [D, 3 * D]) for l in range(nlay)]
    wbr_d = [dt("wbr%d" % l, [3, 512, D]) for l in range(nlay)]
    wout_d = [dt("wout%d" % l, [D, D]) for l in range(nlay)]
    gb_d = dt("gb", [DEPTH, 128, 32])
    wr_d = dt("wr", [D, 16])
    rb_d = dt("rb", [128, 16])
    wg_d = [dt("wg%d" % l, [16, D, 256]) for l in range(nlay)]
    wu_d = [dt("wu%d" % l, [16, D, 256]) for l in range(nlay)]
    wd_d = [dt("wd%d" % l, [16, 256, D]) for l in range(nlay)]
    qoh_d = dt("qoh", [128, 4])
    out_d = nc.dram_tensor("hTo", [D, NTF], F32, kind="ExternalOutput").ap()
    hloc = nc.dram_tensor("hloc", [4 * D, TF], F32)
    hgrp = nc.dram_tensor("hgrp", [8 * 4 * 512, TF], F32)
    oloc = nc.dram_tensor("oloc", [12 * 128, NTF], F32)
    ogrp = nc.dram_tensor("ogrp", [12 * 4 * 128, NTF], F32)
    with ExitStack() as es:
        P = Prog(nc, es)
        k = K(P)
        banks = [P.ps("bk%d" % i, [128, 512]) for i in range(8)]
        cm = make_consts(P, k)
        c = f_common(P, k, banks, cm["ident"], cm["onesf"])
        qoh = P.sb("qoh", [128, 4])
        k.dma("sp", "qoh", qoh, qoh_d)
        msel = P.sb("msel", [128, TF])
        B_hl = [Buf("hloc%d" % i) for i in range(4)]
        B_hgrp, B_ogrp, B_out = Buf("hgrp"), Buf("ogrp"), Buf("out")
        B_oloc = [Buf("oloc%d" % i) for i in range(SEQ // TB)]
        mark0 = P.mark()
        xv = xT_d.rearrange("(kc p) t -> p kc t", p=128)
        ov = out_d.rearrange("(kc p) t -> p kc t", p=128)
        hl2, hg2, ol2, og2 = hloc.ap(), hgrp.ap(), oloc.ap(), ogrp.ap()
        ncoll = [0]

        def ag(ins, outs, reads, writes):
            ncoll[0] += 1
            P.coll("ag", lambda e: e.collective_compute("AllGather", ALU.bypass, replica_groups=RG,
                                                        ins=[ins.opt()], outs=[outs.opt()]),
                   reads=reads, writes=writes)

        def ag_h_tile(ti):
            for hf in range(2):
                cc = ti * 2 + hf
                ag(hl2[cc * 512:(cc + 1) * 512, :], hg2[cc * 2048:(cc + 1) * 2048, :], [B_hl[ti]], [B_hgrp])

        def hl_tile(ti):
            return hl2[ti * D:(ti + 1) * D, :].rearrange("(kc p) t -> p kc t", p=128)

        def h_load_M(bi, hs):
            q, ti = bi // 4, bi % 4
            for hf in range(2):
                cc = ti * 2 + hf
                r0 = cc * 2048 + q * 512
                k.dma("pool", "h%d" % (bi % 2), hs[:, hf * 4:(hf + 1) * 4, :],
                      V(hg2[r0:r0 + 512, :].rearrange("(a p) t -> p a t", p=128), [B_hgrp]))

        ol4 = ol2.rearrange("(m tq e) t -> e m tq t", m=3, tq=4)

        def o_dst_M(bi):
            tq, tl = bi // 4, (bi % 4) * TB
            return ol4[:, :, tq, tl:tl + TB]

        def after_batch_M(bi):
            if bi % 4 == 3:
                tq = bi // 4
                for m in range(3):
                    cc = m * 4 + tq
                    ag(ol2[cc * 128:(cc + 1) * 128, :], og2[cc * 512:(cc + 1) * 512, :],
                       B_oloc[4 * tq:4 * tq + 4], [B_ogrp])

        og5 = og2.rearrange("(m j hd e) t -> e m j hd t", m=3, j=4, hd=4)

        def o_load(ti, ob, stg, ceng):
            for x in range(3):
                for hc in range(4):
                    s_ = stg.nxt()
                    s3 = s_.re("p (j t) -> p j t", t=TF)
                    k.dma("sp", "stg%d" % stg.i, s3, V(og5[:, x, :, hc, ti * TF:(ti + 1) * TF], [B_ogrp]))
                    if os.environ.get("OMASK", "1") == "0":
                        k.cp("dve", ob[:, x, hc, :], s3[:, 0, :])
                        continue
                    k.ts("dve", msel, s3[:, 0, :], qoh[:, 0:1], ALU.mult)
                    for j in range(1, 3):
                        k.stt(msel, s3[:, j, :], qoh[:, j:j + 1], msel, ALU.mult, ALU.add)
                    k.stt(ob[:, x, hc, :], s3[:, 3, :], qoh[:, 3:4], msel, ALU.mult, ALU.add)

        emit_L0(P, k, c, NTF, lambda ti: xv[:, :, ti * TF:(ti + 1) * TF], gb0_d, hl_tile, lambda ti: B_hl[ti],
                after_tile=ag_h_tile)
        FSTOP = int(os.environ.get("FSTOP", "9"))

        def dbg_finish():
            P.barrier()
            P.release(mark0)
            t = P.sb("dbg", [128, KC, TF])
            for ti in range(4):
                q = ti
                for hf in range(2):
                    cc = 0 * 2 + hf
                    r0 = cc * 2048 + q * 512
                    k.dma("sp", "h0", t[:, hf * 4:(hf + 1) * 4, :],
                          V(hg2[r0:r0 + 512, :].rearrange("(a p) t -> p a t", p=128), [B_hgrp]))
                P.dma("sp", "hTo", ov[:, :, ti * TF:(ti + 1) * TF], t.ap, reads=t.bufs, writes=[B_out])

        for l in range(nlay):
            if FSTOP == 1:
                dbg_finish()
                break
            P.barrier()
            P.release(mark0)
            emit_M(P, k, cm, banks, SEQ // TB, h_load_M, o_dst_M, wM_d[l], sm_d[l], nrm_d[l], w2_d[l], obufs=B_oloc,
                   after_batch=after_batch_M)
            if FSTOP == 2:
                dbg_finish()
                break
            P.barrier()
            P.release(mark0)
            last = (l == nlay - 1)
            emit_F(P, k, c, NTF, lambda ti: V(hl_tile(ti), [B_hl[ti]]), o_load,
                   wmg_d[l], wbr_d[l], wout_d[l], gb_d[l], wr_d, rb_d, wg_d[l], wu_d[l], wd_d[l],
                   (lambda ti: ov[:, :, ti * TF:(ti + 1) * TF]) if last else hl_tile,
                   (lambda ti: B_out) if last else (lambda ti: B_hl[ti]),
                   after_tile=None if last else ag_h_tile)
        P.wait_all("sp", [B_out])
        P.run()
        print("fused program: inst", P.n_inst, "waits", P.n_wait, {e: len(v) for e, v in P.q.items()})
    return nc


_PROGS = {}
_NLAY = [DEPTH]


def _prog(name, fn):
    if name not in _PROGS:
        _PROGS[name] = fn()
    return _PROGS[name]


def kernel(x, ln0_g, ln0_b, w_in, gdn_conv, gdn_a_log, gdn_dt_bias, gdn_norm, gla_w2, gla_b2, gla_norm,
           hgrn_lb_logits, hgrn_norm, w_br_a, w_br_b, w_br_c, w_out, ln1_g, ln1_b, w_router, router_bias,
           w_gate, w_up, w_down, ln2_g, ln2_b):
    f = lambda a: np.ascontiguousarray(np.asarray(a, dtype=np.float32))
    (x, ln0_g, ln0_b, w_in, gdn_conv, gdn_a_log, gdn_dt_bias, gdn_norm, gla_w2, gla_b2, gla_norm,
     hgrn_lb_logits, hgrn_norm, w_br_a, w_br_b, w_br_c, w_out, ln1_g, ln1_b, w_router, router_bias,
     w_gate, w_up, w_down, ln2_g, ln2_b) = [f(a) for a in (
         x, ln0_g, ln0_b, w_in, gdn_conv, gdn_a_log, gdn_dt_bias, gdn_norm, gla_w2, gla_b2, gla_norm,
         hgrn_lb_logits, hgrn_norm, w_br_a, w_br_b, w_br_c, w_out, ln1_g, ln1_b, w_router, router_bias,
         w_gate, w_up, w_down, ln2_g, ln2_b)]
    cores = list(range(8))
    NLAY = _NLAY[0]
    gb0 = np.empty((128, 16), np.float32)
    gb0[:, 0:8] = ln0_g.reshape(8, 128).T
    gb0[:, 8:16] = ln0_b.reshape(8, 128).T
    gb = np.empty((DEPTH, 128, 32), np.float32)
    for l in range(DEPTH):
        gb[l, :, 0:8] = ln1_g[l].reshape(8, 128).T
        gb[l, :, 8:16] = ln1_b[l].reshape(8, 128).T
        gb[l, :, 16:24] = ln2_g[l].reshape(8, 128).T
        gb[l, :, 24:32] = ln2_b[l].reshape(8, 128).T
    wmg = np.ascontiguousarray(w_in[:, :, -3 * D:])
    wbr = np.ascontiguousarray(np.stack([w_br_a, w_br_b, w_br_c], axis=1))
    rb = np.ascontiguousarray(np.broadcast_to(router_bias[None, :], (128, 16)))
    dummy = np.zeros((D, 8), np.float32)
    ims = []
    for c in cores:
        b, i = c // 4, c % 4
        per = [m_inputs(dummy, l, i, w_in, gdn_conv, gdn_a_log, gdn_dt_bias, gdn_norm, gla_w2, gla_b2, gla_norm,
                        hgrn_lb_logits, hgrn_norm) for l in range(DEPTH)]
        qoh = np.zeros((128, 4), np.float32)
        qoh[:, i] = 1.0
        ims.append({
            "xT": np.ascontiguousarray(x[b, i * NTF:(i + 1) * NTF, :].T), "gb0": gb0,
            "sm": np.stack([p["sm"] for p in per]),
            "nrm": np.stack([p["nrm"] for p in per]), "w2": np.stack([p["w2"] for p in per]),
            "gb": gb, "wr": w_router, "rb": rb, "qoh": qoh,
        })
        for l in range(NLAY):
            ims[-1].update({"wM%d" % l: per[l]["w"], "wmg%d" % l: wmg[l], "wbr%d" % l: wbr[l], "wout%d" % l: w_out[l],
                            "wg%d" % l: w_gate[l], "wu%d" % l: w_up[l], "wd%d" % l: w_down[l]})
    ncf = _prog("fused", build_fused)
    if os.environ.get("KTRACE", "0") == "1":
        res = run_bass_kernel_spmd(ncf, ims, core_ids=cores, trace=True)
        print("EXEC_TIME_NS", res.exec_time_ns)
    else:
        res = run_bass_kernel_spmd(ncf, ims, core_ids=cores)
    hT = [np.concatenate([res.results[b * 4 + q]["hTo"] for q in range(4)], axis=1) for b in range(NBATCH)]
    out = np.stack([np.ascontiguousarray(hT[b].T) for b in range(NBATCH)]).astype(np.float32)
    return out
```

```python
import math
import os
import numpy as np
from contextlib import ExitStack
import concourse.bass as bass
import concourse.mybir as mybir
from concourse.bass_utils import run_bass_kernel_spmd

F32 = mybir.dt.float32
BF16 = mybir.dt.bfloat16
AF = mybir.ActivationFunctionType
ALU = mybir.AluOpType
AX = mybir.AxisListType

D = 1024
KC = 8
SEQ = 8192
NBATCH = 2
DEPTH = 4
ALPHA = (2.0 * DEPTH) ** 0.25
LN_EPS = 1e-5
RMS_EPS = 1e-6
TB = 512
CA_Q, CA_K, CA_V, CA_BR, CA_DR, CA_G, CA_BD = 0, 128, 256, 384, 512, 640, 768
CB0 = 770
CB_Q, CB_K, CB_LR, CB_V, CB_G = CB0, CB0 + 64, CB0 + 128, CB0 + 144, CB0 + 272
CC0 = CB0 + 400
CC_Q, CC_F, CC_I, CC_G = CC0, CC0 + 128, CC0 + 256, CC0 + 384
NCOL = CC0 + 512


class Buf:
    __slots__ = ("name", "w", "r")
    registry = []

    def __init__(self, name):
        self.name = name
        self.w = None
        self.r = []
        Buf.registry.append(self)


class V:
    __slots__ = ("ap", "bufs")

    def __init__(self, ap, bufs):
        self.ap = ap
        self.bufs = bufs

    def __getitem__(self, idx):
        return V(self.ap[idx], self.bufs)

    def re(self, s, **kw):
        return V(self.ap.rearrange(s, **kw), self.bufs)

    def bc(self, shape):
        return V(self.ap.to_broadcast(list(shape)), self.bufs)


class Prog:
    CENG = ("pe", "dve", "act", "pool")

    def __init__(self, nc, es, arena_words=49152):
        self.nc = nc
        self.es = es
        self.q = {e: [] for e in ("pe", "dve", "act", "pool", "sp")}
        self.cnt = {e: 0 for e in self.CENG}
        self.known = {e: {} for e in self.q}
        self.sems = {}
        self.dcnt = {}
        self.epoch = 0
        self.key = {}
        self.waited = {}
        Buf.registry = []
        for e in self.CENG:
            self.key[e] = (e, 0)
            self.sems[self.key[e]] = es.enter_context(nc.semaphore("s_%s0" % e))
        self.n_inst = 0
        self.n_wait = 0
        arena_words = int(nc.sbuf_bytes_remaining) // 4 - 512
        self.arena = es.enter_context(nc.sbuf_tensor("arena", [128, arena_words], F32))
        self.aoff = 0
        self.psn = 0

    def mark(self):
        return self.aoff

    def release(self, m):
        self.aoff = m

    def sb(self, name, shape, dt=F32, nb=1):
        n = 1
        for v in shape[1:]:
            n *= v
        words = n if dt == F32 else (n + 1) // 2
        words = (words + 7) // 8 * 8
        assert self.aoff + words <= self.arena.shape[1], ("SBUF arena overflow", name, self.aoff, words)
        ap = self.arena[0:shape[0], self.aoff:self.aoff + words]
        self.aoff += words
        if dt != F32:
            ap = ap.bitcast(dt)
        ap = ap[:, 0:n]
        if len(shape) == 3:
            ap = ap.rearrange("p (a b) -> p a b", b=shape[2])
        elif len(shape) == 4:
            ap = ap.rearrange("p (a b c) -> p a b c", b=shape[2], c=shape[3])
        return V(ap, [Buf(name)])

    def ps(self, name, shape, dt=F32):
        t = self.es.enter_context(self.nc.psum_tensor("ps_" + name, list(shape), dt))
        return V(t[:], [Buf(name)])

    def barrier(self, skip=(), keep=()):
        latest = []
        for e in self.CENG:
            if self.cnt[e] > 0:
                latest.append((self.key[e], self.cnt[e]))
        for kk, v in self.dcnt.items():
            if v > 0 and kk[1] not in skip:
                latest.append((kk, v))
        for eng in self.q:
            kn = self.known[eng]
            for (kk, v) in latest:
                if kn.get(kk, 0) < v:
                    kn[kk] = v
                    self.q[eng].append(("w", self.sems[kk], v))
                    self.n_wait += 1
        self.epoch += 1
        for e in self.CENG:
            self.key[e] = (e, self.epoch)
            self.sems[self.key[e]] = self.es.enter_context(self.nc.semaphore("s_%s%d" % (e, self.epoch)))
            self.cnt[e] = 0
        for b in Buf.registry:
            if b in keep:
                continue
            b.w = None
            b.r = []

    def _deps(self, eng, reads, writes):
        deps = []
        for b in reads:
            if b.w is not None:
                deps.append(b.w)
        for b in writes:
            if b.w is not None:
                deps.append(b.w)
            deps.extend(b.r)
        kn = self.known[eng]
        need = {}
        for (k, v) in deps:
            if k[0] == "pe" and eng == "pe":
                continue
            if kn.get(k, 0) >= v:
                continue
            if need.get(k, 0) < v:
                need[k] = v
        for k, v in need.items():
            kn[k] = v
            self.q[eng].append(("w", self.sems[k], v))
            self.n_wait += 1
            if k[0] == "d" and self.waited.get(k, 0) < v:
                self.waited[k] = v

    def _observe(self, eng, k):
        v = self.waited.get(k, 0)
        if v > 0 and self.known[eng].get(k, 0) < v:
            self.known[eng][k] = v
            self.q[eng].append(("w", self.sems[k], v))
            self.n_wait += 1

    def _record(self, ev, reads, writes):
        for b in reads:
            if len(b.r) > 24:
                b.r = b.r[-24:]
            b.r.append(ev)
        for b in writes:
            b.w = ev
            b.r = []

    def op(self, eng, fn, reads=(), writes=()):
        self._deps(eng, reads, writes)
        self.cnt[eng] += 1
        ev = (self.key[eng], self.cnt[eng])
        self.q[eng].append(("i", fn, self.sems[self.key[eng]], 1))
        self._record(ev, reads, writes)
        self.n_inst += 1
        return ev

    def dma(self, eng, key, out, in_, reads=(), writes=()):
        k = ("d", key)
        if k not in self.sems:
            self.sems[k] = self.es.enter_context(self.nc.semaphore("d_" + str(key)))
            self.dcnt[k] = 0
        self._deps(eng, reads, writes)
        self._observe(eng, k)
        self.dcnt[k] += 16
        ev = (k, self.dcnt[k])
        self.q[eng].append(("i", lambda e: e.dma_start(out=out, in_=in_), self.sems[k], 16))
        self._record(ev, reads, writes)
        self.n_inst += 1
        return ev

    def coll(self, key, fn, reads=(), writes=()):
        k = ("d", key)
        if k not in self.sems:
            self.sems[k] = self.es.enter_context(self.nc.semaphore("c_" + str(key)))
            self.dcnt[k] = 0
        self._deps("pool", reads, writes)
        self._observe("pool", k)
        self.dcnt[k] += 1
        ev = (k, self.dcnt[k])
        self.q["pool"].append(("i", fn, self.sems[k], 1))
        self._record(ev, reads, writes)
        self.n_inst += 1
        return ev

    def wait_all(self, eng, bufs):
        self._deps(eng, bufs, ())

    def run(self):
        q = self.q

        def replay(e, lst):
            for it in lst:
                if it[0] == "w":
                    e.wait_ge(it[1], it[2])
                else:
                    it[1](e).then_inc(it[2], it[3])

        with self.nc.Block() as block:
            @block.sync
            def _(e):
                replay(e, q["sp"])

            @block.tensor
            def _(e):
                replay(e, q["pe"])

            @block.vector
            def _(e):
                replay(e, q["dve"])

            @block.scalar
            def _(e):
                replay(e, q["act"])

            @block.gpsimd
            def _(e):
                replay(e, q["pool"])


class K:
    def __init__(self, P):
        self.P = P

    @staticmethod
    def _b(*xs):
        out = []
        for x in xs:
            if isinstance(x, V):
                out.extend(x.bufs)
        return out

    @staticmethod
    def _a(x):
        return x.ap if isinstance(x, V) else x

    def mm(self, out, lhsT, rhs, start=True, stop=True):
        self.P.op("pe", lambda e: e.matmul(out.ap, lhsT=lhsT.ap, rhs=rhs.ap, start=start, stop=stop),
                  reads=lhsT.bufs + rhs.bufs, writes=out.bufs)

    def tr(self, out, in_, ident):
        self.P.op("pe", lambda e: e.transpose(out.ap, in_.ap, ident.ap),
                  reads=in_.bufs + ident.bufs, writes=out.bufs)

    def act(self, out, in_, func, bias=0.0, scale=1.0, accum=None):
        a = self._a
        if accum is None:
            fn = lambda e: e.activation(out=out.ap, in_=in_.ap, func=func, bias=a(bias), scale=a(scale))
            wr = out.bufs
        else:
            fn = lambda e: e.activation(out=out.ap, in_=in_.ap, func=func, bias=a(bias), scale=a(scale),
                                        accum_out=accum.ap)
            wr = out.bufs + accum.bufs
        self.P.op("act", fn, reads=self._b(in_, bias, scale), writes=wr)

    def cp(self, eng, out, in_):
        if eng == "act":
            self.P.op("act", lambda e: e.copy(out=out.ap, in_=in_.ap), reads=in_.bufs, writes=out.bufs)
        else:
            self.P.op(eng, lambda e: e.tensor_copy(out=out.ap, in_=in_.ap), reads=in_.bufs, writes=out.bufs)

    def tt(self, eng, out, in0, in1, op):
        self.P.op(eng, lambda e: e.tensor_tensor(out=out.ap, in0=in0.ap, in1=in1.ap, op=op),
                  reads=in0.bufs + in1.bufs, writes=out.bufs)

    def ts(self, eng, out, in0, s1, op0, s2=None, op1=None):
        a = self._a
        if op1 is None:
            fn = lambda e: e.tensor_scalar(out=out.ap, in0=in0.ap, scalar1=a(s1), scalar2=None, op0=op0)
        else:
            fn = lambda e: e.tensor_scalar(out=out.ap, in0=in0.ap, scalar1=a(s1), scalar2=a(s2), op0=op0, op1=op1)
        self.P.op(eng, fn, reads=self._b(in0, s1, s2), writes=out.bufs)

    def stt(self, out, in0, scalar, in1, op0, op1):
        a = self._a
        self.P.op("dve", lambda e: e.scalar_tensor_tensor(out=out.ap, in0=in0.ap, scalar=a(scalar), in1=in1.ap,
                                                          op0=op0, op1=op1),
                  reads=self._b(in0, scalar, in1), writes=out.bufs)

    def scan(self, out, d0, d1):
        self.P.op("dve", lambda e: e.tensor_tensor_scan(out=out.ap, data0=d0.ap, data1=d1.ap, initial=0.0,
                                                        op0=ALU.mult, op1=ALU.add),
                  reads=d0.bufs + d1.bufs, writes=out.bufs)

    def recip(self, out, in_):
        self.P.op("dve", lambda e: e.reciprocal(out=out.ap, in_=in_.ap), reads=in_.bufs, writes=out.bufs)

    def memset(self, eng, out, val):
        self.P.op(eng, lambda e: e.memset(out.ap, val), writes=out.bufs)

    def asel(self, out, pattern, cmp, cm, base=0):
        self.P.op("pool", lambda e: e.affine_select(out=out.ap, in_=out.ap, pattern=pattern, compare_op=cmp,
                                                    fill=0.0, base=base, channel_multiplier=cm),
                  reads=out.bufs, writes=out.bufs)

    def dma(self, eng, key, out, in_):
        rd = in_.bufs if isinstance(in_, V) else []
        wr = out.bufs if isinstance(out, V) else []
        self.P.dma(eng, key, self._a(out), self._a(in_), reads=rd, writes=wr)


def make_consts(P, k):
    c = {}
    c["ident"] = P.sb("ident", [128, 128])
    k.memset("pool", c["ident"], 1.0)
    k.asel(c["ident"], [[-1, 128]], ALU.is_equal, 1)
    c["identb"] = P.sb("identb", [128, 128], BF16)
    k.cp("pool", c["identb"], c["ident"])
    c["onesb"] = P.sb("onesb", [128, 128], BF16)
    k.memset("pool", c["onesb"], 1.0)
    c["onesf"] = P.sb("onesf", [128, 128])
    k.memset("pool", c["onesf"], 1.0)
    for name, cmp in (("mTi", ALU.is_ge), ("mTs", ALU.is_gt)):
        m = P.sb(name, [128, 4, 128])
        k.memset("pool", m, 0.0)
        k.memset("pool", m[0:64, :, 0:64], 1.0)
        k.memset("pool", m[64:128, :, 64:128], 1.0)
        k.asel(m, [[0, 4], [1, 128]], cmp, -1)
        c[name] = m
    m = P.sb("LTs", [128, 128])
    k.memset("pool", m, 0.0)
    k.memset("pool", m[0:64, 0:64], 1.0)
    k.memset("pool", m[64:128, 64:128], 1.0)
    k.asel(m, [[-1, 128]], ALU.is_gt, 1)
    c["LTs"] = m
    cm = P.sb("cmask", [128, TB])
    k.memset("pool", cm, 1.0)
    k.memset("pool", cm.re("p (c t) -> p c t", t=64)[:, :, 0:1], 0.0)
    c["cmask"] = cm
    ec = P.sb("epsc", [128, 4])
    k.memset("pool", ec[:, 0:1], RMS_EPS)
    k.memset("pool", ec[:, 1:2], 1.0)
    k.memset("pool", ec[:, 2:3], LN_EPS)
    k.memset("pool", ec[:, 3:4], 0.0)
    c["eps"] = ec
    bm = P.sb("bm", [128, 2])
    k.memset("pool", bm, 0.0)
    k.memset("pool", bm[0:64, 0:1], 1.0)
    k.memset("pool", bm[64:128, 1:2], 1.0)
    c["bm"] = bm
    return c


def load_weights_bf16(P, k, wdram, wbf, ncol, tag, step=256):
    wv = wdram.rearrange("(kc p) n -> p kc n", p=128)
    for c0 in range(0, ncol, step):
        c1 = min(ncol, c0 + step)
        k.dma("pool", tag, wbf[:, :, c0:c1], wv[:, :, c0:c1])


def build_M(nb=16):
    NT = nb * TB
    nc = bass.Bass("TRN2", target_bir_lowering=False)
    hT_d = nc.dram_tensor("hT", [D, NT], F32, kind="ExternalInput").ap()
    w_d = nc.dram_tensor("w", [D, NCOL], F32, kind="ExternalInput").ap()
    sm_d = nc.dram_tensor("sm", [128, 32], F32, kind="ExternalInput").ap()
    nrm_d = nc.dram_tensor("nrm", [128, 3, 128], F32, kind="ExternalInput").ap()
    w2_d = nc.dram_tensor("w2", [16, 64], F32, kind="ExternalInput").ap()
    o_d = nc.dram_tensor("oT", [3, 128, NT], F32, kind="ExternalOutput").ap()
    with ExitStack() as es:
        P = Prog(nc, es)
        k = K(P)
        banks = [P.ps("bk%d" % i, [128, 512]) for i in range(8)]
        c = make_consts(P, k)
        hv = hT_d.rearrange("(kc p) t -> p kc t", p=128)
        fin = emit_M(P, k, c, banks, nb, lambda bi, hd: k.dma("pool", "h%d" % (bi % 2), hd, hv[:, :, bi * TB:(bi + 1) * TB]),
                     lambda bi: o_d[:, :, bi * TB:(bi + 1) * TB].rearrange("m p t -> p m t"),
                     w_d, sm_d, nrm_d, w2_d)
        P.wait_all("sp", fin)
        P.run()
        print("M program: inst", P.n_inst, "waits", P.n_wait, {e: len(v) for e, v in P.q.items()})
    return nc


class _Stop(Exception):
    pass


def emit_M(P, k, c, banks, nb, h_load, o_dst, w_d, sm_d, nrm_d, w2_d, obufs=None, after_batch=None):
    MSTOP = float(os.environ.get("MSTOP", "99"))

    def stage(n):
        if n > MSTOP:
            raise _Stop()
    ident, identb, onesb, onesf = c["ident"], c["identb"], c["onesb"], c["onesf"]
    mTi, mTs, LTs, cmask, bm = c["mTi"], c["mTs"], c["LTs"], c["cmask"], c["bm"]
    UT = mTi[:, 0, :]
    EPS_R, ONE_C = c["eps"][:, 0:1], c["eps"][:, 1:2]

    pj = banks[0:2]
    pm = banks[2:5]
    pbf = V(banks[5].ap.bitcast(BF16), banks[5].bufs)
    bk6, bk7 = banks[6], banks[7]
    pcA = [bk6[:, 0:128], bk6[:, 128:256]]
    pcB = [bk7[:, 0:128]]
    pcC = [bk7[:, 128:256]]
    pjn = [0]

    pj4 = [banks[0], banks[1], banks[6], banks[7]]

    def nextpj():
        pjn[0] = (pjn[0] + 1) % 4
        return pj4[pjn[0]]

    pmn = [0]

    def nextpm():
        pmn[0] = (pmn[0] + 1) % 3
        return pm[pmn[0]]

    wbf = P.sb("wbf", [128, KC, NCOL], BF16)
    load_weights_bf16(P, k, w_d, wbf, NCOL, "wM")
    sm = P.sb("sm", [128, 32])
    k.dma("sp", "sm", sm, sm_d)
    nrm = P.sb("nrm", [128, 3, 128])
    k.dma("sp", "nrm", nrm, nrm_d)
    w2f = P.sb("w2f", [16, 64])
    k.dma("sp", "w2", w2f, w2_d)
    w2b = P.sb("w2b", [16, 64], BF16)
    k.cp("dve", w2b, w2f)
    prm = P.sb("prm", [128, 16])
    k.act(prm[:, 0:1], sm[:, 12:13], AF.Exp)
    k.ts("dve", prm[:, 0:1], prm[:, 0:1], -1.0, ALU.mult)
    negA = prm[:, 0:1]
    dtb = sm[:, 13:14]
    k.ts("dve", prm[:, 1:2], sm[:, 14:15], -1.0, ALU.mult)
    negb2 = prm[0:64, 1:2]
    lbt = P.sb("lbt", [128, 8])
    k.act(lbt[:, 0:4], sm[:, 16:20], AF.Exp)
    k.P.op("dve", lambda e: e.tensor_reduce(out=lbt.ap[:, 4:5], in_=lbt.ap[:, 0:4], axis=AX.X, op=ALU.add),
           reads=lbt.bufs, writes=lbt.bufs)
    k.recip(lbt[:, 5:6], lbt[:, 4:5])
    k.tt("dve", lbt[:, 0:4], lbt[:, 0:4], sm[:, 20:24], ALU.mult)
    k.P.op("dve", lambda e: e.tensor_reduce(out=lbt.ap[:, 6:7], in_=lbt.ap[:, 0:4], axis=AX.X, op=ALU.add),
           reads=lbt.bufs, writes=lbt.bufs)
    k.tt("dve", prm[:, 2:3], lbt[:, 6:7], lbt[:, 5:6], ALU.mult)
    k.ts("dve", prm[:, 2:3], prm[:, 2:3], 0.0, ALU.max, 1.0, ALU.min)
    lb = prm[:, 2:3]
    k.ts("dve", prm[:, 3:4], lb, -1.0, ALU.mult, 1.0, ALU.add)
    oml = prm[:, 3:4]
    k.ts("dve", prm[:, 4:5], oml, -1.0, ALU.mult)
    noml = prm[:, 4:5]

    hb = [P.sb("hb%d" % i, [128, KC, TB], BF16) for i in range(2)]
    ost = [P.sb("ost%d" % i, [128, 3, TB]) for i in range(2)]

    def W(c0, n):
        return lambda kc: wbf[:, kc, c0:c0 + n]

    def proj_fm(ps, h, wsel, m):
        for kc in range(KC):
            k.mm(ps[0:m, :], wsel(kc), h[:, kc, :], start=(kc == 0), stop=(kc == KC - 1))

    def proj_tm(ps, h, p, wsel):
        for kc in range(KC):
            k.mm(ps, h[:, kc, p * 128:(p + 1) * 128], wsel(kc), start=(kc == 0), stop=(kc == KC - 1))

    pre = [P.sb("pre%d" % i, [128, TB + 3]) for i in range(3)]
    for t in pre:
        k.memset("pool", t[:, 0:3], 0.0)
    cv = [P.sb("cv%d" % i, [128, TB]) for i in range(3)]
    sqb = P.sb("sqb", [128, TB], BF16)
    rn = P.sb("rn", [128, TB])
    qTb = P.sb("qTb", [128, TB], BF16)
    qTf = P.sb("qTf", [128, TB])
    kTf = P.sb("kTf", [128, TB])
    kTb = P.sb("kTb", [128, TB], BF16)
    kbTb = P.sb("kbTb", [128, TB], BF16)
    betaB = P.sb("betaB", [128, TB])
    gB = P.sb("gB", [128, TB])
    cumB = P.sb("cumB", [128, TB])
    ecumB = P.sb("ecumB", [128, TB])
    tok = P.sb("tok", [128, 64])
    gsel = P.sb("gsel", [128, 4, 2])
    lastA = P.sb("lastA", [128, 8])
    kb = P.sb("kb", [128, 4, 128], BF16)
    ksA = P.sb("ksA", [128, 4, 128], BF16)
    vb = P.sb("vb", [128, 4, 128], BF16)
    decT = P.sb("decT", [128, 4, 128])
    dmi = P.sb("dmi", [128, 4, 128])
    dms = P.sb("dms", [128, 4, 128])
    attA = P.sb("attA", [128, 4, 128], BF16)
    Xb = [P.sb("Xb%d" % i, [128, 4, 128], BF16) for i in range(2)]
    Yb = [P.sb("Yb%d" % i, [128, 4, 128], BF16) for i in range(2)]
    Rb = P.sb("Rb", [128, 4, 128], BF16)
    uf = P.sb("uf", [128, 4, 128])
    wTE = P.sb("wTE", [128, 4, 128])
    wTO = P.sb("wTO", [128, 4, 128])
    qEA = P.sb("qEA", [128, 4, 128])
    qOA = P.sb("qOA", [128, 4, 128])
    for t in (wTE, wTO, qEA, qOA):
        k.memset("pool", t, 0.0)
    vnew = P.sb("vnew", [128, 128], BF16)
    SA = [P.sb("SA%d" % i, [128, 128]) for i in range(3)]
    k.memset("pool", SA[0], 0.0)
    sA = [0]

    def dd_tiles(tag, dk):
        d = {}
        for nme in ("q", "kk", "c", "e", "dm", "ksf"):
            d[nme] = scr[nme][0:dk, :]
        d["qm"] = P.sb(tag + "qm", [dk, TB], BF16)
        d["kd"] = P.sb(tag + "kd", [dk, TB], BF16)
        d["qE"] = P.sb(tag + "qE", [dk, 4, 128])
        d["qO"] = P.sb(tag + "qO", [dk, 4, 128])
        k.memset("pool", d["qE"], 0.0)
        k.memset("pool", d["qO"], 0.0)
        d["ks"] = P.sb(tag + "ks", [128, 4, dk], BF16)
        d["v"] = P.sb(tag + "v", [128, 4, 128], BF16)
        d["att"] = P.sb(tag + "att", [128, 4, 128], BF16)
        d["dl"] = P.sb(tag + "dl", [dk, 8])
        d["S"] = [P.sb(tag + "S%d" % i, [dk, 128]) for i in range(3)]
        k.memset("pool", d["S"][0], 0.0)
        d["si"] = 0
        d["dk"] = dk
        return d

    scr = {nme: P.sb("scr" + nme, [128, TB]) for nme in ("q", "kk", "c", "e", "dm", "ksf")}
    tB = dd_tiles("B", 64)
    tC = dd_tiles("C", 128)
    lrb = P.sb("lrb", [16, TB], BF16)
    sgC = P.sb("sgC", [128, TB])
    esc = [{n_: P.sb("%s%d" % (n_, mi_), [128, 128] if n_ != "ssq" else [128, 4]) for n_ in ("junk", "ssq", "sgt", "gw", "onf")}
           for mi_ in range(3)]
    ebank = [pj[0], pj[1], banks[5]]
    obank = [pm[0], pm[1], pm[2]]

    gwS = P.sb("gwS", [128, 12, 128])
    sgG = P.sb("sgG", [128, 128])

    def gen_G(bi, h):
        for mi, gcol in ((0, CA_G), (1, CB_G), (2, CC_G)):
            for p in range(4):
                gp = nextpj()
                proj_tm(gp[:, 0:128], h, p, W(gcol, 128))
                k.act(sgG, gp[:, 0:128], AF.Silu)
                k.tt("pool", gwS[:, mi * 4 + p, :], sgG, nrm[:, mi, :], ALU.mult)
                yield

    def epilogue(mi, o_ps, h, p, gcol, ostg):
        e_ = esc[mi]
        junk, ssq, onf = e_["junk"], e_["ssq"], e_["onf"]
        k.memset("pool", ssq[:, 0:1], 0.0)
        k.act(junk, o_ps, AF.Square, accum=ssq[:, 0:1])
        yield
        k.act(ssq[:, 1:2], ssq[:, 0:1], AF.Sqrt, bias=EPS_R, scale=1.0 / 128.0)
        k.recip(ssq[:, 2:3], ssq[:, 1:2])
        yield
        k.stt(onf, o_ps, ssq[:, 2:3], gwS[:, mi * 4 + p, :], ALU.mult, ALU.mult)
        yield
        tp = ebank[mi]
        k.tr(tp[:, 0:128], onf, ident)
        k.cp("act", ostg[:, mi, p * 128:(p + 1) * 128], tp[:, 0:128])
        yield

    def dd_prep(t, sc):
        dk = t["dk"]
        c3 = t["c"].re("p (c t) -> p c t", t=64)
        k.act(t["e"], t["c"], AF.Exp, scale=sc)
        e4 = t["e"].re("p (a b t) -> p a b t", b=2, t=64)
        q4 = t["q"].re("p (a b t) -> p a b t", b=2, t=64)
        k.tt("dve", t["qE"][:, :, 0:64], q4[:, :, 0, :], e4[:, :, 0, :], ALU.mult)
        k.tt("dve", t["qO"][:, :, 64:128], q4[:, :, 1, :], e4[:, :, 1, :], ALU.mult)
        yield
        k.act(t["dl"], c3[:, :, 63], AF.Exp, scale=sc)
        dm3 = t["dm"].re("p (c t) -> p c t", t=64)
        k.tt("dve", dm3, c3, c3[:, :, 31:32].bc([dk, 8, 64]), ALU.subtract)
        k.act(t["e"], t["dm"], AF.Exp, scale=sc)
        k.tt("dve", t["qm"], t["q"], t["e"], ALU.mult)
        yield
        k.act(t["e"], t["dm"], AF.Exp, scale=-sc)
        k.tt("dve", t["kd"], t["kk"], t["e"], ALU.mult)
        yield
        k.tt("dve", dm3, c3[:, :, 63:64].bc([dk, 8, 64]), c3, ALU.subtract)
        k.act(t["e"], t["dm"], AF.Exp, scale=sc)
        k.tt("dve", t["ksf"], t["kk"], t["e"], ALU.mult)
        yield
        tp = nextpm()
        for p in range(4):
            k.tr(tp[:, p * 128:p * 128 + dk], t["ksf"][:, p * 128:(p + 1) * 128], ident[0:dk, 0:dk])
        k.cp("dve", t["ks"], tp.re("p (a b) -> p a b", b=128)[:, :, 0:dk])
        yield
        ap_ = nextpm()
        for p in range(4):
            k.mm(ap_[:, p * 128:(p + 1) * 128], t["kd"][:, p * 128:(p + 1) * 128], t["qm"][:, p * 128:(p + 1) * 128])
        k.tt("dve", t["att"], ap_.re("p (a b) -> p a b", b=128), mTi, ALU.mult)
        yield

    def dd_chain(t, pc, mi, h, gcol, ostg):
        dk = t["dk"]
        S = t["S"]
        for p in range(4):
            s0, s1, s2 = t["si"], (t["si"] + 1) % 3, (t["si"] + 2) % 3
            ds = pc[0]
            k.mm(ds[0:dk, :], t["ks"][0:64, p, :], t["v"][0:64, p, :])
            k.stt(S[s1], S[s0], t["dl"][:, 2 * p:2 * p + 1], ds[0:dk, :], ALU.mult, ALU.add)
            yield
            o = obank[mi][:, 0:128]
            k.mm(o, t["att"][:, p, :], t["v"][:, p, :], start=True, stop=False)
            k.mm(o, t["qE"][:, p, :], S[s0], start=False, stop=False)
            k.mm(o, t["qO"][:, p, :], S[s1], start=False, stop=True)
            k.mm(ds[0:dk, :], t["ks"][64:128, p, :], t["v"][64:128, p, :])
            k.stt(S[s2], S[s1], t["dl"][:, 2 * p + 1:2 * p + 2], ds[0:dk, :], ALU.mult, ALU.add)
            t["si"] = s2
            yield
            yield from epilogue(mi, o, h, p, gcol, ostg)

    def gen_A(bi, h):
        for i, c0 in enumerate((CA_Q, CA_K, CA_V)):
            ps = nextpj()
            proj_fm(ps, h, W(c0, 128), 128)
            if bi > 0:
                k.cp("pool", pre[i][:, 0:3], pre[i][:, TB:TB + 3])
            k.cp("act", pre[i][:, 3:TB + 3], ps)
            k.ts("dve", cv[i], pre[i][:, 0:TB], sm[:, 4 * i:4 * i + 1], ALU.mult)
            for j in range(1, 4):
                k.stt(cv[i], pre[i][:, j:TB + j], sm[:, 4 * i + j:4 * i + j + 1], cv[i], ALU.mult, ALU.add)
            k.act(cv[i], cv[i], AF.Silu)
            yield
        for i in range(2):
            k.act(sqb, cv[i], AF.Square)
            ps = nextpj()
            k.mm(ps, onesb, sqb)
            k.act(rn, ps, AF.Sqrt, bias=EPS_R)
            k.recip(rn, rn)
            if i == 0:
                k.stt(qTf, cv[0], 128.0 ** -0.5, rn, ALU.mult, ALU.mult)
                k.cp("dve", qTb, qTf)
            else:
                k.tt("dve", kTf, cv[1], rn, ALU.mult)
                k.cp("dve", kTb, kTf)
            yield
        yield
        ps = nextpj()
        proj_fm(ps, h, W(CA_BR, 128), 128)
        k.act(betaB, ps, AF.Sigmoid)
        k.tt("dve", kbTb, kTf, betaB, ALU.mult)
        yield
        ps = nextpj()
        proj_fm(ps, h, W(CA_DR, 128), 128)
        k.act(gB, ps, AF.Exp, bias=dtb)
        k.act(gB, gB, AF.Ln, bias=ONE_C)
        k.ts("dve", gB, gB, negA, ALU.mult)
        k.scan(cumB, cmask, gB)
        k.act(ecumB, cumB, AF.Exp)
        yield
        e4 = ecumB.re("p (a b t) -> p a b t", b=2, t=64)
        q4 = qTf.re("p (a b t) -> p a b t", b=2, t=64)
        k.tt("dve", qEA[:, :, 0:64], q4[:, :, 0, :], e4[:, :, 0, :], ALU.mult)
        k.tt("dve", qOA[:, :, 64:128], q4[:, :, 1, :], e4[:, :, 1, :], ALU.mult)
        yield
        ptok = nextpj()[:, 0:128]
        for p in range(4):
            proj_tm(ptok[:, 16 * p:16 * p + 16], h, p, W(CA_BD - 14, 16))
        pt2 = ptok[:, 0:64].re("p (a b) -> p a b", b=16)
        k.act(tok[:, 0:4], pt2[:, :, 14], AF.Sigmoid)
        k.act(tok[:, 4:8], pt2[:, :, 15], AF.Exp, bias=dtb)
        k.act(tok[:, 4:8], tok[:, 4:8], AF.Ln, bias=ONE_C)
        k.ts("dve", tok[:, 4:8], tok[:, 4:8], negA, ALU.mult)
        k.mm(ptok[:, 64:68], UT, tok[:, 4:8])
        k.mm(ptok[:, 68:72], LTs, tok[:, 4:8])
        k.ts("dve", gsel[:, :, 0], tok[:, 4:8], bm[:, 0:1], ALU.mult)
        k.ts("dve", gsel[:, :, 1], tok[:, 4:8], bm[:, 1:2], ALU.mult)
        k.mm(ptok[:, 72:80], onesf, gsel.re("p a b -> p (a b)"))
        k.act(lastA, ptok[:, 72:80], AF.Exp)
        k.cp("dve", tok[:, 8:12], ptok[:, 64:68])
        k.act(tok[:, 16:20], ptok[:, 64:68], AF.Exp)
        k.tt("dve", tok[:, 16:20], tok[:, 16:20], tok[:, 0:4], ALU.mult)
        k.act(tok[:, 20:24], ptok[:, 68:72], AF.Exp)
        yield
        tpk = nextpm()
        for p in range(4):
            k.tr(tpk[:, p * 128:(p + 1) * 128], kTf[:, p * 128:(p + 1) * 128], ident)
        tpv = nextpm()
        for p in range(4):
            k.tr(tpv[:, p * 128:(p + 1) * 128], cv[2][:, p * 128:(p + 1) * 128], ident)
        for p in range(4):
            k.ts("dve", kb[:, p, :], tpk[:, p * 128:(p + 1) * 128], tok[:, 16 + p:17 + p], ALU.mult)
            k.ts("dve", ksA[:, p, :], tpk[:, p * 128:(p + 1) * 128], tok[:, 20 + p:21 + p], ALU.mult)
            k.ts("dve", vb[:, p, :], tpv[:, p * 128:(p + 1) * 128], tok[:, p:p + 1], ALU.mult)
        yield
        for p in range(4):
            k.ts("dve", decT[:, p, :], cumB[:, p * 128:(p + 1) * 128], tok[:, 8 + p:9 + p], ALU.subtract,
                 0.0, ALU.min)
        k.act(decT, decT, AF.Exp)
        yield
        k.tt("pool", dmi, decT, mTi, ALU.mult)
        k.tt("pool", dms, decT, mTs, ALU.mult)
        yield
        pq = nextpm()
        for p in range(4):
            sl = slice(p * 128, (p + 1) * 128)
            k.mm(pq[:, sl], kTb[:, sl], qTb[:, sl])
        k.tt("dve", attA, pq.re("p (a b) -> p a b", b=128), dmi, ALU.mult)
        yield
        pk = nextpm()
        for p in range(4):
            sl = slice(p * 128, (p + 1) * 128)
            k.mm(pk[:, sl], kTb[:, sl], kbTb[:, sl])
        X, Y = Xb[0], Yb[0]
        k.tt("dve", X, pk.re("p (a b) -> p a b", b=128), dms, ALU.mult)
        yield
        pb3 = pbf[:, 0:512].re("p (a b) -> p a b", b=128)
        for p in range(4):
            k.tr(pb3[:, p, :], X[:, p, :], identb)
        k.cp("act", Y, pb3)
        yield
        k.tt("pool", Rb, identb.re("p (a b) -> p a b", a=1).bc([128, 4, 128]), X, ALU.subtract)
        yield
        xi = 0
        for lvl in range(int(os.environ.get('NLVL', '5'))):
            Xn, Yn = Xb[xi ^ 1], Yb[xi ^ 1]
            yield
            py = nextpm()
            for p in range(4):
                k.mm(py[:, p * 128:(p + 1) * 128], X[:, p, :], Y[:, p, :])
            k.cp("act", Yn, py.re("p (a b) -> p a b", b=128))
            yield
            if lvl < 4:
                px = nextpm()
                for p in range(4):
                    k.mm(px[:, p * 128:(p + 1) * 128], Y[:, p, :], X[:, p, :])
                k.cp("dve", Xn, px.re("p (a b) -> p a b", b=128))
            yield
            pr = nextpm()
            for p in range(4):
                k.mm(pr[:, p * 128:(p + 1) * 128], Yn[:, p, :], Rb[:, p, :])
            k.tt("dve", Rb, pr.re("p (a b) -> p a b", b=128), Rb, ALU.add)
            X, Y = Xn, Yn
            xi ^= 1
        yield
        pu = nextpm()
        for p in range(4):
            k.mm(pu[:, p * 128:(p + 1) * 128], Rb[:, p, :], vb[:, p, :])
        k.cp("act", uf, pu.re("p (a b) -> p a b", b=128))
        yield
        pw = nextpm()
        for p in range(4):
            k.mm(pw[:, p * 128:(p + 1) * 128], kb[:, p, :], Rb[:, p, :])
        pw3 = pw.re("p (a b) -> p a b", b=128)
        k.cp("dve", wTE[:, :, 0:64], pw3[:, :, 0:64])
        k.cp("dve", wTO[:, :, 64:128], pw3[:, :, 64:128])

        yield

    def gen_BC(bi, h):
        ps = nextpj()
        proj_fm(ps, h, W(CB_Q, 64), 64)
        k.act(tB["q"], ps[0:64, :], AF.Identity, scale=64.0 ** -0.5)
        yield
        ps = nextpj()
        proj_fm(ps, h, W(CB_K, 64), 64)
        k.cp("act", tB["kk"], ps[0:64, :])
        yield
        ps = nextpj()
        proj_fm(ps, h, W(CB_LR, 16), 16)
        k.cp("act", lrb, ps[0:16, :])
        yield
        ps = nextpj()
        k.mm(ps[0:64, :], w2b, lrb)
        k.act(tB["e"], ps[0:64, :], AF.Exp, bias=negb2, scale=-1.0)
        k.act(tB["e"], tB["e"], AF.Ln, bias=ONE_C[0:64, :])
        k.scan(tB["c"], cmask[0:64, :], tB["e"])
        yield
        for p in range(4):
            ps = nextpj()
            proj_tm(ps[:, 0:128], h, p, W(CB_V, 128))
            k.cp("act", tB["v"][:, p, :], ps[:, 0:128])
            yield
        yield from dd_prep(tB, -1.0 / 16.0)

        yield
        ps = nextpj()
        proj_fm(ps, h, W(CC_Q, 128), 128)
        k.act(tC["q"], ps, AF.Silu)
        k.ts("dve", tC["q"], tC["q"], 128.0 ** -0.5, ALU.mult)
        yield
        ps = nextpj()
        proj_fm(ps, h, W(CC_F, 128), 128)
        k.act(sgC, ps, AF.Sigmoid)
        k.ts("dve", tC["kk"], sgC, noml, ALU.mult, oml, ALU.add)
        k.ts("dve", tC["e"], sgC, oml, ALU.mult, lb, ALU.add)
        k.act(tC["e"], tC["e"], AF.Ln)
        k.scan(tC["c"], cmask, tC["e"])
        yield
        for p in range(4):
            ps = nextpj()
            proj_tm(ps[:, 0:128], h, p, W(CC_I, 128))
            k.cp("act", tC["v"][:, p, :], ps[:, 0:128])
            yield
        yield from dd_prep(tC, 1.0)

        yield

    def chain_A(h, ostg):
        for p in range(4):
            s0, s1, s2 = sA[0], (sA[0] + 1) % 3, (sA[0] + 2) % 3
            wS, dS = pcA
            o = obank[0][:, 0:128]
            k.mm(wS, wTE[:, p, :], SA[s0])
            k.tt("dve", vnew[0:64, :], uf[0:64, p, :], wS[0:64, :], ALU.subtract)
            yield
            k.mm(dS, ksA[0:64, p, :], vnew[0:64, :])
            k.stt(SA[s1], SA[s0], lastA[:, 2 * p:2 * p + 1], dS, ALU.mult, ALU.add)
            yield
            k.mm(wS, wTO[:, p, :], SA[s1])
            k.tt("dve", vnew[64:128, :], uf[64:128, p, :], wS[64:128, :], ALU.subtract)
            yield
            k.mm(dS, ksA[64:128, p, :], vnew[64:128, :])
            k.mm(o, attA[:, p, :], vnew, start=True, stop=False)
            k.mm(o, qEA[:, p, :], SA[s0], start=False, stop=False)
            k.mm(o, qOA[:, p, :], SA[s1], start=False, stop=True)
            k.stt(SA[s2], SA[s1], lastA[:, 2 * p + 1:2 * p + 2], dS, ALU.mult, ALU.add)
            sA[0] = s2
            yield
            yield from epilogue(0, o, h, p, CA_G, ostg)

    def interleave(gens):
        gens = list(gens)
        while gens:
            for g in list(gens):
                try:
                    next(g)
                except StopIteration:
                    gens.remove(g)

    def batch_body(bi, t0, hs, h, ostg):
        if bi + 1 < nb:
            h_load(bi + 1, hb[(bi + 1) % 2])
        if os.environ.get("NOILV", "0") == "1":
            for g in (gen_A(bi, h), gen_BC(bi, h), gen_G(bi, h), chain_A(h, ostg), dd_chain(tB, pcB, 1, h, CB_G, ostg),
                      dd_chain(tC, pcC, 2, h, CC_G, ostg)):
                for _ in g:
                    pass
            return
        interleave([gen_A(bi, h), gen_BC(bi, h), gen_G(bi, h)])
        interleave([chain_A(h, ostg), dd_chain(tB, pcB, 1, h, CB_G, ostg), dd_chain(tC, pcC, 2, h, CC_G, ostg)])


    fin = []
    h_load(0, hb[0])
    for bi in range(nb):
        t0 = bi * TB
        hs, h, ostg = None, hb[bi % 2], ost[bi % 2]
        if MSTOP < 99:
            k.memset("pool", ostg, 0.0)
        try:
            batch_body(bi, t0, hs, h, ostg)
        except _Stop:
            pass
        ob = obufs[bi] if obufs is not None else Buf("oT_out%d" % bi)
        P.dma("sp", "o%d" % (bi % 2), o_dst(bi), ostg.ap, reads=ostg.bufs, writes=[ob])
        fin.append(ob)
        if after_batch is not None:
            after_batch(bi)
    return fin


def m_inputs(hT_b, l, hd, w_in, gdn_conv, gdn_a_log, gdn_dt_bias, gdn_norm, gla_w2, gla_b2, gla_norm,
             hgrn_lb_logits, hgrn_norm):
    wl = w_in[l]
    o = 0
    offs = {}
    for name, sz in (("aq", 512), ("ak", 512), ("av", 512), ("ab", 4), ("adt", 4), ("ag", 512),
                     ("bq", 256), ("bk", 256), ("bv", 512), ("blr", 16), ("bg", 512),
                     ("cq", 512), ("cf", 512), ("ci", 512), ("cg", 512)):
        offs[name] = o
        o += sz

    def col(name, width):
        s = offs[name] + hd * width
        return wl[:, s:s + width]

    w = np.empty((D, NCOL), np.float32)
    w[:, CA_Q:CA_Q + 128] = col("aq", 128)
    w[:, CA_K:CA_K + 128] = col("ak", 128)
    w[:, CA_V:CA_V + 128] = col("av", 128)
    w[:, CA_BR:CA_BR + 128] = np.repeat(col("ab", 1), 128, axis=1)
    w[:, CA_DR:CA_DR + 128] = np.repeat(col("adt", 1), 128, axis=1)
    w[:, CA_G:CA_G + 128] = col("ag", 128)
    w[:, CA_BD:CA_BD + 1] = col("ab", 1)
    w[:, CA_BD + 1:CA_BD + 2] = col("adt", 1)
    w[:, CB_Q:CB_Q + 64] = col("bq", 64)
    w[:, CB_K:CB_K + 64] = col("bk", 64)
    w[:, CB_LR:CB_LR + 16] = wl[:, offs["blr"]:offs["blr"] + 16]
    w[:, CB_V:CB_V + 128] = col("bv", 128)
    w[:, CB_G:CB_G + 128] = col("bg", 128)
    w[:, CC_Q:CC_Q + 128] = col("cq", 128)
    w[:, CC_F:CC_F + 128] = col("cf", 128)
    w[:, CC_I:CC_I + 128] = col("ci", 128)
    w[:, CC_G:CC_G + 128] = col("cg", 128)
    sm = np.zeros((128, 32), np.float32)
    cw = gdn_conv[l]
    for i in range(3):
        sm[:, 4 * i:4 * i + 4] = cw[:, i * 512 + hd * 128:i * 512 + (hd + 1) * 128].T
    sm[:, 12] = gdn_a_log[l, hd]
    sm[:, 13] = gdn_dt_bias[l, hd]
    sm[0:64, 14] = gla_b2[l, hd * 64:(hd + 1) * 64]
    sm[:, 16:20] = hgrn_lb_logits[:, hd * 128:(hd + 1) * 128].T
    for j in range(4):
        sm[:, 20 + j] = 1.0 if (1 <= j <= l) else 0.0
    nrm = np.empty((128, 3, 128), np.float32)
    nrm[:, 0, :] = gdn_norm[l][None, :]
    nrm[:, 1, :] = gla_norm[l][None, :]
    nrm[:, 2, :] = hgrn_norm[l][None, :]
    w2 = np.ascontiguousarray(gla_w2[l][:, hd * 64:(hd + 1) * 64])
    return {"hT": np.ascontiguousarray(hT_b), "w": w, "sm": sm, "nrm": nrm, "w2": w2}


TF = 512
NTF = 2048


class Rot:
    def __init__(self, tiles):
        self.t = tiles
        self.i = -1

    def nxt(self):
        self.i = (self.i + 1) % len(self.t)
        return self.t[self.i]


def f_common(P, k, banks, ident, onesf):
    c = {"onesf": onesf, "ident": ident}
    ec = P.sb("epscF", [128, 4])
    k.memset("pool", ec[:, 0:1], LN_EPS)
    c["eps"] = ec[:, 0:1]
    c["banks"] = banks
    sel = P.sb("sel", [16, 16, 128])
    k.memset("pool", sel, 1.0)
    k.asel(sel, [[-1, 16], [0, 128]], ALU.is_equal, 1)
    c["sel"] = sel
    return c


def f_work(P, c):
    c["ps"] = Rot(c["banks"])
    c["ln"] = [P.sb("lnw%d" % i, [128, TF]) for i in range(4)]
    c["zsq"] = P.sb("zsq", [128, TF])


def emit_ln(P, k, c, z, g, b, out32, outb=None):
    onesf = c["onesf"]
    mean, msq, var, rstd = c["ln"]
    zsq = c["zsq"]
    p1 = c["ps"].nxt()
    for kc in range(KC):
        k.mm(p1, onesf, z[:, kc, :], start=(kc == 0), stop=(kc == KC - 1))
    k.act(mean, p1, AF.Identity, scale=1.0 / D)
    p2 = c["ps"].nxt()
    for kc in range(KC):
        k.act(zsq, z[:, kc, :], AF.Square)
        k.mm(p2, onesf, zsq, start=(kc == 0), stop=(kc == KC - 1))
    k.tt("dve", msq, mean, mean, ALU.mult)
    k.stt(var, p2, 1.0 / D, msq, ALU.mult, ALU.subtract)
    k.act(var, var, AF.Sqrt, bias=c["eps"])
    k.recip(rstd, var)
    for kc in range(KC):
        k.tt("dve", zsq, z[:, kc, :], mean, ALU.subtract)
        k.tt("dve", zsq, zsq, rstd, ALU.mult)
        k.ts("dve", out32[:, kc, :], zsq, g[:, kc:kc + 1], ALU.mult, b[:, kc:kc + 1], ALU.add)
        if outb is not None:
            k.cp("act", outb[:, kc, :], out32[:, kc, :])


def emit_L0(P, k, c, ntf, x_src, gb_d, out_dst, obuf, after_tile=None):
    f_work(P, c)
    gb = P.sb("gb0", [128, 16])
    k.dma("sp", "gb", gb, gb_d)
    zt = [P.sb("z%d" % i, [128, KC, TF]) for i in range(2)]
    for ti in range(ntf // TF):
        z = zt[ti % 2]
        k.dma("sp", "z%d" % (ti % 2), z, x_src(ti))
        emit_ln(P, k, c, z, gb[:, 0:8], gb[:, 8:16], z)
        P.dma("sp", "hTo", out_dst(ti), z.ap, reads=z.bufs, writes=[obuf(ti)])
        if after_tile is not None:
            after_tile(ti)


def emit_F(P, k, c, ntf, h_src, o_load, wmg_d, wbr_d, wout_d, gb_d, wr_d, rb_d, wg_d, wu_d, wd_d, out_dst, obuf,
           after_tile=None):
    f_work(P, c)
    PS = c["ps"]
    sel = c["sel"]
    gb = P.sb("gb", [128, 32])
    k.dma("sp", "gb", gb, gb_d)
    wr = P.sb("wr", [128, KC, 16])
    k.dma("sp", "wr", wr, wr_d.rearrange("(kc p) n -> p kc n", p=128))
    rb = P.sb("rb", [128, 16])
    k.dma("sp", "rb", rb, rb_d)
    stg = Rot([P.sb("stg%d" % i, [128, 2048]) for i in range(2)])
    NWB, DEPTH_PF = 12, int(os.environ.get("DPF", "8"))
    wbt = [P.sb("wbt%d" % i, [128, 2048], BF16) for i in range(NWB)]
    ceng = None
    reqs = []
    for ti in range(ntf // TF):
        for n in range(KC):
            for x in range(3):
                reqs.append((wmg_d.rearrange("(kc p) n -> p kc n", p=128)[:, :, x * D + n * 128:x * D + (n + 1) * 128], KC, 128))
                reqs.append((wbr_d[x].rearrange("(hc p) n -> p hc n", p=128)[:, :, n * 128:(n + 1) * 128], 4, 128))
        for n in range(KC):
            reqs.append((wout_d.rearrange("(kc p) n -> p kc n", p=128)[:, :, n * 128:(n + 1) * 128], KC, 128))
        for e in range(16):
            reqs.append((wg_d[e].rearrange("(kc p) f -> p kc f", p=128), KC, 256))
            reqs.append((wu_d[e].rearrange("(kc p) f -> p kc f", p=128), KC, 256))
            reqs.append((wd_d[e].rearrange("(fc p) d -> p fc d", p=128), 2, D))
    st = {"issued": 0, "taken": 0}

    def issue_one():
        i = st["issued"]
        dram3, a, b = reqs[i]
        w = wbt[i % NWB]
        k.dma("pool", "wb%d" % (i % NWB), w[:, 0:a * b].re("p (a b) -> p a b", b=b), dram3)
        st["issued"] += 1

    def stream(dram3, a, b):
        i = st["taken"]
        assert reqs[i][1] == a and reqs[i][2] == b
        while st["issued"] < len(reqs) and st["issued"] < i + 1 + DEPTH_PF:
            issue_one()
        st["taken"] += 1
        return wbt[i % NWB][:, 0:a * b].re("p (a b) -> p a b", b=b)

    hs = P.sb("hs", [128, KC, TF])
    hb = P.sb("hbF", [128, KC, TF], BF16)
    ob = P.sb("ob", [128, 3, 4, TF], BF16)
    yc = P.sb("yc", [128, TF])
    t1 = P.sb("t1", [128, TF])
    sg = P.sb("sg", [128, TF])
    yTb = P.sb("yTb", [128, KC, TF], BF16)
    z = P.sb("z", [128, KC, TF])
    h1b = P.sb("h1b", [128, KC, TF], BF16)
    yacc = P.sb("yacc", [128, KC, TF])
    aT = [P.sb("aT%d" % i, [128, TF], BF16) for i in range(2)]
    rt = P.sb("rt", [128, 4, 96])
    rtA = P.sb("rtA", [128, 256])
    rtB = P.sb("rtB", [128, 72])
    combT = P.sb("combT", [16, TF])
    for ti in range(ntf // TF):
        k.dma("sp", "hs", hs, h_src(ti))
        k.dma("pool", "hbF", hb, h_src(ti))
        o_load(ti, ob, stg, ceng)
        for n in range(KC):
            for x in range(3):
                wm = stream(wmg_d.rearrange("(kc p) n -> p kc n", p=128)[:, :, x * D + n * 128:x * D + (n + 1) * 128], KC, 128)
                pm_ = PS.nxt()
                for kc in range(KC):
                    k.mm(pm_, wm[:, kc, :], hb[:, kc, :], start=(kc == 0), stop=(kc == KC - 1))
                k.act(sg, pm_, AF.Sigmoid)
                wb = stream(wbr_d[x].rearrange("(hc p) n -> p hc n", p=128)[:, :, n * 128:(n + 1) * 128], 4, 128)
                pb_ = PS.nxt()
                for hc in range(4):
                    k.mm(pb_, wb[:, hc, :], ob[:, x, hc, :], start=(hc == 0), stop=(hc == 3))
                if x == 0:
                    k.tt("dve", yc, pb_, sg, ALU.mult)
                else:
                    k.tt("dve", t1, pb_, sg, ALU.mult)
                    k.tt("dve", yc, yc, t1, ALU.add)
            k.cp("act", yTb[:, n, :], yc)
        for n in range(KC):
            wo = stream(wout_d.rearrange("(kc p) n -> p kc n", p=128)[:, :, n * 128:(n + 1) * 128], KC, 128)
            pm_ = PS.nxt()
            for kc in range(KC):
                k.mm(pm_, wo[:, kc, :], yTb[:, kc, :], start=(kc == 0), stop=(kc == KC - 1))
            k.stt(z[:, n, :], hs[:, n, :], ALPHA, pm_, ALU.mult, ALU.add)
        emit_ln(P, k, c, z, gb[:, 0:8], gb[:, 8:16], z, h1b)
        pr_ = PS.nxt()
        for blk in range(4):
            for kc in range(KC):
                k.mm(pr_[:, blk * 16:(blk + 1) * 16], z[:, kc, blk * 128:(blk + 1) * 128], wr[:, kc, :],
                     start=(kc == 0), stop=(kc == KC - 1))
        pt_ = PS.nxt()
        sc = rtA[:, 0:64]
        se = rtA[:, 64:128]
        tm = rtA[:, 128:192]
        mk = rtA[:, 192:256]
        m1, m2, gs, gmask = rtB[:, 0:16], rtB[:, 16:32], rtB[:, 32:48], rtB[:, 48:64]
        gmax, den = rtB[:, 64:68], rtB[:, 68:72]
        g3 = lambda v: v.re("p (a j) -> p a j", j=4)
        b3 = lambda v: v.re("p (b e) -> p b e", e=16)
        u1 = lambda v: v.re("p (a o) -> p a o", o=1)
        k.act(sc, pr_[:, 0:64], AF.Sigmoid)
        k.tt("dve", b3(se), b3(sc), rb.re("p (o e) -> p o e", o=1).bc([128, 4, 16]), ALU.add)
        k.P.op("dve", lambda e, o=m1.ap, i=g3(se).ap: e.tensor_reduce(out=o, in_=i, axis=AX.X, op=ALU.max),
               reads=rtA.bufs, writes=rtB.bufs)
        k.tt("dve", g3(tm), g3(se), u1(m1).bc([128, 16, 4]), ALU.is_equal)
        k.stt(tm, tm, -1.0e9, se, ALU.mult, ALU.add)
        k.P.op("dve", lambda e, o=m2.ap, i=g3(tm).ap: e.tensor_reduce(out=o, in_=i, axis=AX.X, op=ALU.max),
               reads=rtA.bufs, writes=rtB.bufs)
        k.tt("dve", gs, m1, m2, ALU.add)
        k.P.op("dve", lambda e, o=gmax.ap, i=g3(gs).ap: e.tensor_reduce(out=o, in_=i, axis=AX.X, op=ALU.max),
               reads=rtB.bufs, writes=rtB.bufs)
        k.tt("dve", g3(gmask), g3(gs), u1(gmax).bc([128, 4, 4]), ALU.is_equal)
        k.tt("dve", g3(mk), g3(se), u1(m2).bc([128, 16, 4]), ALU.is_ge)
        k.tt("dve", g3(mk), g3(mk), u1(gmask).bc([128, 16, 4]), ALU.mult)
        k.tt("dve", mk, mk, sc, ALU.mult)
        k.P.op("dve", lambda e, o=den.ap, i=b3(mk).ap: e.tensor_reduce(out=o, in_=i, axis=AX.X, op=ALU.add),
               reads=rtA.bufs, writes=rtB.bufs)
        k.recip(den, den)
        k.tt("dve", b3(mk), b3(mk), u1(den).bc([128, 4, 16]), ALU.mult)
        for blk in range(4):
            k.tr(pt_[0:16, blk * 128:(blk + 1) * 128], mk[:, blk * 16:(blk + 1) * 16], c["ident"])
        k.cp("dve", combT, pt_[0:16, :])
        for e in range(16):
            wg = stream(wg_d[e].rearrange("(kc p) f -> p kc f", p=128), KC, 256)
            wu = stream(wu_d[e].rearrange("(kc p) f -> p kc f", p=128), KC, 256)
            wd = stream(wd_d[e].rearrange("(fc p) d -> p fc d", p=128), 2, D)
            pc_ = PS.nxt()
            k.mm(pc_, sel[:, e, :], combT)
            for f in range(2):
                pg_ = PS.nxt()
                for kc in range(KC):
                    k.mm(pg_, wg[:, kc, f * 128:(f + 1) * 128], h1b[:, kc, :], start=(kc == 0), stop=(kc == KC - 1))
                pu_ = PS.nxt()
                for kc in range(KC):
                    k.mm(pu_, wu[:, kc, f * 128:(f + 1) * 128], h1b[:, kc, :], start=(kc == 0), stop=(kc == KC - 1))
                k.act(sg, pg_, AF.Silu)
                k.tt("dve", t1, pu_, sg, ALU.mult)
                k.tt("dve", aT[f], pc_, t1, ALU.mult)
            for dc in range(KC):
                py_ = PS.nxt()
                for f in range(2):
                    k.mm(py_, wd[:, f, dc * 128:(dc + 1) * 128], aT[f], start=(f == 0), stop=(f == 1))
                if e == 0:
                    k.cp("dve", yacc[:, dc, :], py_)
                else:
                    k.tt("dve", yacc[:, dc, :], py_, yacc[:, dc, :], ALU.add)
        for n in range(KC):
            k.stt(yacc[:, n, :], z[:, n, :], ALPHA, yacc[:, n, :], ALU.mult, ALU.add)
        emit_ln(P, k, c, yacc, gb[:, 16:24], gb[:, 24:32], yacc)
        P.dma("sp", "hTo", out_dst(ti), yacc.ap, reads=yacc.bufs, writes=[obuf(ti)])
        if after_tile is not None:
            after_tile(ti)


def build_F(NTF=2048):
    nc = bass.Bass("TRN2", target_bir_lowering=False)
    hT_d = nc.dram_tensor("hT", [D, NTF], F32, kind="ExternalInput").ap()
    oT_d = nc.dram_tensor("oT", [3, 512, NTF], F32, kind="ExternalInput").ap()
    wmg_d = nc.dram_tensor("wmg", [D, 3 * D], F32, kind="ExternalInput").ap()
    wbr_d = nc.dram_tensor("wbr", [3, 512, D], F32, kind="ExternalInput").ap()
    wout_d = nc.dram_tensor("wout", [D, D], F32, kind="ExternalInput").ap()
    gb_d = nc.dram_tensor("gb", [128, 32], F32, kind="ExternalInput").ap()
    wr_d = nc.dram_tensor("wr", [D, 16], F32, kind="ExternalInput").ap()
    rb_d = nc.dram_tensor("rb", [128, 16], F32, kind="ExternalInput").ap()
    wg_d = nc.dram_tensor("wg", [16, D, 256], F32, kind="ExternalInput").ap()
    wu_d = nc.dram_tensor("wu", [16, D, 256], F32, kind="ExternalInput").ap()
    wd_d = nc.dram_tensor("wd", [16, 256, D], F32, kind="ExternalInput").ap()
    o_d = nc.dram_tensor("hTo", [D, NTF], F32, kind="ExternalOutput").ap()
    with ExitStack() as es:
        P = Prog(nc, es)
        k = K(P)
        banks = [P.ps("bk%d" % i, [128, 512]) for i in range(8)]
        cm = make_consts(P, k)
        c = f_common(P, k, banks, cm["ident"], cm["onesf"])
        hv = hT_d.rearrange("(kc p) t -> p kc t", p=128)
        ov = o_d.rearrange("(kc p) t -> p kc t", p=128)

        def o_load(ti, ob, stg, ceng):
            for x in range(3):
                s_ = stg.nxt()
                k.dma("sp", "stg%d" % stg.i, s_.re("p (a b) -> p a b", b=TF),
                      oT_d[x].rearrange("(hc p) t -> p hc t", p=128)[:, :, ti * TF:(ti + 1) * TF])
                k.cp("dve", ob[:, x, :, :].re("p a b -> p (a b)"), s_)

        obuf = Buf("hTo")
        emit_F(P, k, c, NTF, lambda ti: hv[:, :, ti * TF:(ti + 1) * TF], o_load, wmg_d, wbr_d, wout_d, gb_d, wr_d, rb_d,
               wg_d, wu_d, wd_d, lambda ti: ov[:, :, ti * TF:(ti + 1) * TF], lambda ti: obuf)
        P.wait_all("sp", [obuf])
        P.run()
        print("F program: inst", P.n_inst, "waits", P.n_wait, {e: len(v) for e, v in P.q.items()})
    return nc


def f_inputs(hT_c, oT_c, l, w_in, w_br_a, w_br_b, w_br_c, w_out, ln1_g, ln1_b, w_router, router_bias,
             w_gate, w_up, w_down, ln2_g, ln2_b):
    gb = np.empty((128, 32), np.float32)
    gb[:, 0:8] = ln1_g[l].reshape(8, 128).T
    gb[:, 8:16] = ln1_b[l].reshape(8, 128).T
    gb[:, 16:24] = ln2_g[l].reshape(8, 128).T
    gb[:, 24:32] = ln2_b[l].reshape(8, 128).T
    return {"hT": np.ascontiguousarray(hT_c), "oT": np.ascontiguousarray(oT_c),
            "wmg": np.ascontiguousarray(w_in[l][:, -3 * D:]),
            "wbr": np.stack([w_br_a[l], w_br_b[l], w_br_c[l]]), "wout": w_out[l], "gb": gb,
            "wr": w_router, "rb": np.ascontiguousarray(np.broadcast_to(router_bias[None, :], (128, 16))),
            "wg": w_gate[l], "wu": w_up[l], "wd": w_down[l]}


RG = [[0, 1, 2, 3], [4, 5, 6, 7]]


def build_fused(nlay=DEPTH):
    nc = bass.Bass("TRN2", target_bir_lowering=False)
    dt = lambda name, shape: nc.dram_tensor(name, shape, F32, kind="ExternalInput").ap()
    xT_d = dt("xT", [D, NTF])
    gb0_d = dt("gb0", [128, 16])
    wM_d = [dt("wM%d" % l, [D, NCOL]) for l in range(nlay)]
    sm_d = dt("sm", [DEPTH, 128, 32])
    nrm_d = dt("nrm", [DEPTH, 128, 3, 128])
    w2_d = dt("w2", [DEPTH, 16, 64])
    wmg_d = [dt("wmg%d" % l, [D, 3 * D]) for l in range(nlay)]
    wbr_d = [dt("wbr%d" % l, [3, 512, D]) for l in range(nlay)]
    wout_d = [dt("wout%d" % l, [D, D]) for l in range(nlay)]
    gb_d = dt("gb", [DEPTH, 128, 32])
    wr_d = dt("wr", [D, 16])
    rb_d = dt("rb", [128, 16])
    wg_d = [dt("wg%d" % l, [16, D, 256]) for l in range(nlay)]
    wu_d = [dt("wu%d" % l, [16, D, 256]) for l in range(nlay)]
    wd_d = [dt("wd%d" % l, [16, 256, D]) for l in range(nlay)]
    qoh_d = dt("qoh", [128, 4])
    out_d = nc.dram_tensor("hTo", [D, NTF], F32, kind="ExternalOutput").ap()
    hloc = nc.dram_tensor("hloc", [4 * D, TF], F32)
    hgrp = nc.dram_tensor("hgrp", [8 * 4 * 512, TF], F32)
    oloc = nc.dram_tensor("oloc", [12 * 128, NTF], F32)
    ogrp = nc.dram_tensor("ogrp", [12 * 4 * 128, NTF], F32)
    with ExitStack() as es:
        P = Prog(nc, es)
        k = K(P)
        banks = [P.ps("bk%d" % i, [128, 512]) for i in range(8)]
        cm = make_consts(P, k)
        c = f_common(P, k, banks, cm["ident"], cm["onesf"])
        qoh = P.sb("qoh", [128, 4])
        k.dma("sp", "qoh", qoh, qoh_d)
        msel = P.sb("msel", [128, TF])
        B_hl = [Buf("hloc%d" % i) for i in range(4)]
        B_hgrp, B_ogrp, B_out = Buf("hgrp"), Buf("ogrp"), Buf("out")
        B_oloc = [Buf("oloc%d" % i) for i in range(SEQ // TB)]
        mark0 = P.mark()
        xv = xT_d.rearrange("(kc p) t -> p kc t", p=128)
        ov = out_d.rearrange("(kc p) t -> p kc t", p=128)
        hl2, hg2, ol2, og2 = hloc.ap(), hgrp.ap(), oloc.ap(), ogrp.ap()
        ncoll = [0]

        def ag(ins, outs, reads, writes):
            ncoll[0] += 1
            P.coll("ag", lambda e: e.collective_compute("AllGather", ALU.bypass, replica_groups=RG,
                                                        ins=[ins.opt()], outs=[outs.opt()]),
                   reads=reads, writes=writes)

        def ag_h_tile(ti):
            for hf in range(2):
                cc = ti * 2 + hf
                ag(hl2[cc * 512:(cc + 1) * 512, :], hg2[cc * 2048:(cc + 1) * 2048, :], [B_hl[ti]], [B_hgrp])

        def hl_tile(ti):
            return hl2[ti * D:(ti + 1) * D, :].rearrange("(kc p) t -> p kc t", p=128)

        def h_load_M(bi, hs):
            q, ti = bi // 4, bi % 4
            for hf in range(2):
                cc = ti * 2 + hf
                r0 = cc * 2048 + q * 512
                k.dma("pool", "h%d" % (bi % 2), hs[:, hf * 4:(hf + 1) * 4, :],
                      V(hg2[r0:r0 + 512, :].rearrange("(a p) t -> p a t", p=128), [B_hgrp]))

        ol4 = ol2.rearrange("(m tq e) t -> e m tq t", m=3, tq=4)

        def o_dst_M(bi):
            tq, tl = bi // 4, (bi % 4) * TB
            return ol4[:, :, tq, tl:tl + TB]

        def after_batch_M(bi):
            if bi % 4 == 3:
                tq = bi // 4
                for m in range(3):
                    cc = m * 4 + tq
                    ag(ol2[cc * 128:(cc + 1) * 128, :], og2[cc * 512:(cc + 1) * 512, :],
                       B_oloc[4 * tq:4 * tq + 4], [B_ogrp])

        og5 = og2.rearrange("(m j hd e) t -> e m j hd t", m=3, j=4, hd=4)

        def o_load(ti, ob, stg, ceng):
            for x in range(3):
                for hc in range(4):
                    s_ = stg.nxt()
                    s3 = s_.re("p (j t) -> p j t", t=TF)
                    k.dma("sp", "stg%d" % stg.i, s3, V(og5[:, x, :, hc, ti * TF:(ti + 1) * TF], [B_ogrp]))
                    if os.environ.get("OMASK", "1") == "0":
                        k.cp("dve", ob[:, x, hc, :], s3[:, 0, :])
                        continue
                    k.ts("dve", msel, s3[:, 0, :], qoh[:, 0:1], ALU.mult)
                    for j in range(1, 3):
                        k.stt(msel, s3[:, j, :], qoh[:, j:j + 1], msel, ALU.mult, ALU.add)
                    k.stt(ob[:, x, hc, :], s3[:, 3, :], qoh[:, 3:4], msel, ALU.mult, ALU.add)

        emit_L0(P, k, c, NTF, lambda ti: xv[:, :, ti * TF:(ti + 1) * TF], gb0_d, hl_tile, lambda ti: B_hl[ti],
                after_tile=ag_h_tile)
        FSTOP = int(os.environ.get("FSTOP", "9"))

        def dbg_finish():
            P.barrier()
            P.release(mark0)
            t = P.sb("dbg", [128, KC, TF])
            for ti in range(4):
                q = ti
                for hf in range(2):
                    cc = 0 * 2 + hf
                    r0 = cc * 2048 + q * 512
                    k.dma("sp", "h0", t[:, hf * 4:(hf + 1) * 4, :],
                          V(hg2[r0:r0 + 512, :].rearrange("(a p) t -> p a t", p=128), [B_hgrp]))
                P.dma("sp", "hTo", ov[:, :, ti * TF:(ti + 1) * TF], t.ap, reads=t.bufs, writes=[B_out])

        for l in range(nlay):
            if FSTOP == 1:
                dbg_finish()
                break
            P.barrier(skip=("ag",), keep=(B_hgrp,))
            P.release(mark0)
            emit_M(P, k, cm, banks, SEQ // TB, h_load_M, o_dst_M, wM_d[l], sm_d[l], nrm_d[l], w2_d[l], obufs=B_oloc,
                   after_batch=after_batch_M)
            if FSTOP == 2:
                dbg_finish()
                break
            P.barrier(skip=("ag",), keep=(B_ogrp,))
            P.release(mark0)
            last = (l == nlay - 1)
            emit_F(P, k, c, NTF, lambda ti: V(hl_tile(ti), [B_hl[ti]]), o_load,
                   wmg_d[l], wbr_d[l], wout_d[l], gb_d[l], wr_d, rb_d, wg_d[l], wu_d[l], wd_d[l],
                   (lambda ti: ov[:, :, ti * TF:(ti + 1) * TF]) if last else hl_tile,
                   (lambda ti: B_out) if last else (lambda ti: B_hl[ti]),
                   after_tile=None if last else ag_h_tile)
        P.wait_all("sp", [B_out])
        P.run()
        print("fused program: inst", P.n_inst, "waits", P.n_wait, {e: len(v) for e, v in P.q.items()})
    return nc


_PROGS = {}
_NLAY = [DEPTH]


def _prog(name, fn):
    if name not in _PROGS:
        _PROGS[name] = fn()
    return _PROGS[name]


def kernel(x, ln0_g, ln0_b, w_in, gdn_conv, gdn_a_log, gdn_dt_bias, gdn_norm, gla_w2, gla_b2, gla_norm,
           hgrn_lb_logits, hgrn_norm, w_br_a, w_br_b, w_br_c, w_out, ln1_g, ln1_b, w_router, router_bias,
           w_gate, w_up, w_down, ln2_g, ln2_b):
    f = lambda a: np.ascontiguousarray(np.asarray(a, dtype=np.float32))
    (x, ln0_g, ln0_b, w_in, gdn_conv, gdn_a_log, gdn_dt_bias, gdn_norm, gla_w2, gla_b2, gla_norm,
     hgrn_lb_logits, hgrn_norm, w_br_a, w_br_b, w_br_c, w_out, ln1_g, ln1_b, w_router, router_bias,
     w_gate, w_up, w_down, ln2_g, ln2_b) = [f(a) for a in (
         x, ln0_g, ln0_b, w_in, gdn_conv, gdn_a_log, gdn_dt_bias, gdn_norm, gla_w2, gla_b2, gla_norm,
         hgrn_lb_logits, hgrn_norm, w_br_a, w_br_b, w_br_c, w_out, ln1_g, ln1_b, w_router, router_bias,
         w_gate, w_up, w_down, ln2_g, ln2_b)]
    cores = list(range(8))
    NLAY = _NLAY[0]
    gb0 = np.empty((128, 16), np.float32)
    gb0[:, 0:8] = ln0_g.reshape(8, 128).T
    gb0[:, 8:16] = ln0_b.reshape(8, 128).T
    gb = np.empty((DEPTH, 128, 32), np.float32)
    for l in range(DEPTH):
        gb[l, :, 0:8] = ln1_g[l].reshape(8, 128).T
        gb[l, :, 8:16] = ln1_b[l].reshape(8, 128).T
        gb[l, :, 16:24] = ln2_g[l].reshape(8, 128).T
        gb[l, :, 24:32] = ln2_b[l].reshape(8, 128).T
    wmg = np.ascontiguousarray(w_in[:, :, -3 * D:])
    wbr = np.ascontiguousarray(np.stack([w_br_a, w_br_b, w_br_c], axis=1))
    rb = np.ascontiguousarray(np.broadcast_to(router_bias[None, :], (128, 16)))
    dummy = np.zeros((D, 8), np.float32)
    ims = []
    for c in cores:
        b, i = c // 4, c % 4
        per = [m_inputs(dummy, l, i, w_in, gdn_conv, gdn_a_log, gdn_dt_bias, gdn_norm, gla_w2, gla_b2, gla_norm,
                        hgrn_lb_logits, hgrn_norm) for l in range(DEPTH)]
        qoh = np.zeros((128, 4), np.float32)
        qoh[:, i] = 1.0
        ims.append({
            "xT": np.ascontiguousarray(x[b, i * NTF:(i + 1) * NTF, :].T), "gb0": gb0,
            "sm": np.stack([p["sm"] for p in per]),
            "nrm": np.stack([p["nrm"] for p in per]), "w2": np.stack([p["w2"] for p in per]),
            "gb": gb, "wr": w_router, "rb": rb, "qoh": qoh,
        })
        for l in range(NLAY):
            ims[-1].update({"wM%d" % l: per[l]["w"], "wmg%d" % l: wmg[l], "wbr%d" % l: wbr[l], "wout%d" % l: w_out[l],
                            "wg%d" % l: w_gate[l], "wu%d" % l: w_up[l], "wd%d" % l: w_down[l]})
    ncf = _prog("fused", build_fused)
    if os.environ.get("KTRACE", "0") == "1":
        res = run_bass_kernel_spmd(ncf, ims, core_ids=cores, trace=True)
        print("EXEC_TIME_NS", res.exec_time_ns)
    else:
        res = run_bass_kernel_spmd(ncf, ims, core_ids=cores)
    hT = [np.concatenate([res.results[b * 4 + q]["hTo"] for q in range(4)], axis=1) for b in range(NBATCH)]
    out = np.stack([np.ascontiguousarray(hT[b].T) for b in range(NBATCH)]).astype(np.float32)
    return out
```
